# Optimizing a Trainium2 kernel written in Bass

```python
import math
import jax, jax.numpy as jnp
from jax import lax
import numpy as np

D_MODEL = 1024
BATCH = 8
SEQ = 2048
DEPTH = 2
DEC_BATCH = 128
DEC_SEQ = 8
PAST_LEN = 16384
PAGE_SIZE = 128

BRANCH_W = D_MODEL // 2
N_BRANCH = 3
N_POOL_GROUPS = 4
POOL_WINDOWS = (2, 4, 8, 16)
POOL_GROUP_W = BRANCH_W // N_POOL_GROUPS
POOL_BUF = max(POOL_WINDOWS) - 1
CONV_WIDTH = 31
CONV_BUF = CONV_WIDTH - 1
N_MEM = 256
N_XHEADS = 4
XHEAD_DIM = BRANCH_W // N_XHEADS
N_IN_SLICES = 7
IN_COLS = N_IN_SLICES * BRANCH_W + N_BRANCH * D_MODEL
EPS = 1e-6

kernel_name = "hybrid_pool_conv_memattn_decoder_step"


def rmsnorm(x, g):
    xf = x.astype(jnp.float32)
    r = xf * lax.rsqrt(jnp.mean(xf * xf, axis=-1, keepdims=True) + EPS)
    return (r * g.astype(jnp.float32)).astype(x.dtype)


def layernorm(x, g, b):
    xf = x.astype(jnp.float32)
    mu = jnp.mean(xf, axis=-1, keepdims=True)
    var = jnp.mean(jnp.square(xf - mu), axis=-1, keepdims=True)
    r = (xf - mu) * lax.rsqrt(var + EPS)
    return (r * g.astype(jnp.float32) + b.astype(jnp.float32)).astype(x.dtype)


def pool_mix(ext, pos0, pool_w, pool_scale):
    B, L, W = ext.shape
    S = L - POOL_BUF
    ef = ext.astype(jnp.float32)
    csum = jnp.concatenate([jnp.zeros((B, 1, W), jnp.float32), jnp.cumsum(ef, axis=1)], axis=1)
    pos = (pos0 + jnp.arange(S)).astype(jnp.float32)
    parts = []
    for g, win in enumerate(POOL_WINDOWS):
        cg = csum[..., g * POOL_GROUP_W:(g + 1) * POOL_GROUP_W]
        hi = cg[:, POOL_BUF + 1:POOL_BUF + 1 + S]
        lo = cg[:, POOL_BUF + 1 - win:POOL_BUF + 1 - win + S]
        cnt = jnp.minimum(pos + 1.0, float(win))[None, :, None]
        parts.append((hi - lo) / cnt)
    pooled = jnp.concatenate(parts, axis=-1)
    mixed = (pooled - ef[:, POOL_BUF:]).reshape(B, S, N_POOL_GROUPS, POOL_GROUP_W)
    y = jnp.einsum('bsgc,gcd->bsgd', mixed, pool_w.astype(jnp.float32)).reshape(B, S, W)
    return (y * pool_scale.astype(jnp.float32)).astype(ext.dtype)


def causal_dwconv(ext, conv_w, conv_b):
    W = ext.shape[-1]
    out = lax.conv_general_dilated(ext, conv_w[:, None, :].astype(ext.dtype), window_strides=(1,),
                                   padding='VALID', dimension_numbers=('NWC', 'WIO', 'NWC'),
                                   feature_group_count=W)
    return out + conv_b


def mem_kv(mem, g, w):
    B = mem.shape[0]
    kv = rmsnorm(mem, g) @ w
    k = kv[..., :BRANCH_W].reshape(B, N_MEM, N_XHEADS, XHEAD_DIM)
    v = kv[..., BRANCH_W:].reshape(B, N_MEM, N_XHEADS, XHEAD_DIM)
    return k, v


def layer(x, pool_buf, conv_buf, mk, mv, pos0, g_pre, g_post, w_in, pool_w, pool_scale,
          conv_w, conv_b, conv_ln_g, conv_ln_b, w_branch, w_out):
    B, S, _ = x.shape
    W = BRANCH_W
    h = rmsnorm(x, g_pre)
    proj = h @ w_in
    p_in, p_gate, c_val, c_glu, c_gate, q, x_gate = [proj[..., i * W:(i + 1) * W] for i in range(N_IN_SLICES)]
    merge_logits = proj[..., N_IN_SLICES * W:].reshape(B, S, N_BRANCH, D_MODEL)
    pool_ext = jnp.concatenate([pool_buf, p_in], axis=1)
    a = pool_mix(pool_ext, pos0, pool_w, pool_scale) * jax.nn.silu(p_gate)
    u = c_val * jax.nn.sigmoid(c_glu)
    conv_ext = jnp.concatenate([conv_buf, u], axis=1)
    cv = jax.nn.silu(layernorm(causal_dwconv(conv_ext, conv_w, conv_b), conv_ln_g, conv_ln_b))
    bconv = cv * jax.nn.silu(c_gate)
    qh = q.reshape(B, S, N_XHEADS, XHEAD_DIM)
    s = jnp.einsum('bshd,bmhd->bhsm', qh, mk).astype(jnp.float32) / math.sqrt(XHEAD_DIM)
    p = jax.nn.softmax(s, axis=-1).astype(x.dtype)
    o = jnp.einsum('bhsm,bmhd->bshd', p, mv).reshape(B, S, W)
    cattn = o * jax.nn.silu(x_gate)
    br = jnp.einsum('bsnw,nwd->bsnd', jnp.stack([a, bconv, cattn], axis=2), w_branch)
    merged = jnp.sum(jax.nn.sigmoid(merge_logits) * br, axis=2)
    y = merged @ w_out
    x_new = x + rmsnorm(y, g_post)
    return x_new, pool_ext[:, -POOL_BUF:], conv_ext[:, -CONV_BUF:]


def setup_inputs(seed: int = 0) -> dict:
    key = jax.random.key(seed)
    ks = jax.random.split(key, 24)
    f = jnp.float32
    W = BRANCH_W
    nrm = lambda k, shape, s: jax.random.normal(k, shape, f) * s
    return {
        "x_prompt": nrm(ks[0], (BATCH, SEQ, D_MODEL), 1.0),
        "x_sample": nrm(ks[1], (DEC_BATCH, DEC_SEQ, D_MODEL), 1.0),
        "state_pool": nrm(ks[2], (DEPTH, DEC_BATCH, POOL_BUF, W), 1.0),
        "state_conv": nrm(ks[3], (DEPTH, DEC_BATCH, CONV_BUF, W), 0.5),
        "cache_mem_k": nrm(ks[4], (DEPTH, DEC_BATCH, N_MEM, N_XHEADS, XHEAD_DIM), 1.0),
        "cache_mem_v": nrm(ks[5], (DEPTH, DEC_BATCH, N_MEM, N_XHEADS, XHEAD_DIM), 1.0),
        "mem_prompt": nrm(ks[6], (BATCH, N_MEM, D_MODEL), 1.0),
        "norm_pre": 1.0 + nrm(ks[7], (DEPTH, D_MODEL), 0.05),
        "norm_post": 1.0 + nrm(ks[8], (DEPTH, D_MODEL), 0.05),
        "mem_norm": 1.0 + nrm(ks[9], (DEPTH, D_MODEL), 0.05),
        "w_mem_kv": nrm(ks[10], (DEPTH, D_MODEL, 2 * W), D_MODEL ** -0.5),
        "w_in": nrm(ks[11], (DEPTH, D_MODEL, IN_COLS), D_MODEL ** -0.5),
        "pool_w": nrm(ks[12], (DEPTH, N_POOL_GROUPS, POOL_GROUP_W, POOL_GROUP_W), POOL_GROUP_W ** -0.5),
        "pool_scale": 1.0 + nrm(ks[13], (DEPTH, W), 0.1),
        "conv_w": nrm(ks[14], (DEPTH, CONV_WIDTH, W), CONV_WIDTH ** -0.5),
        "conv_b": nrm(ks[15], (DEPTH, W), 0.02),
        "conv_ln_g": 1.0 + nrm(ks[16], (DEPTH, W), 0.05),
        "conv_ln_b": nrm(ks[17], (DEPTH, W), 0.02),
        "w_branch": nrm(ks[18], (DEPTH, N_BRANCH, W, D_MODEL), W ** -0.5),
        "w_out": nrm(ks[19], (DEPTH, D_MODEL, D_MODEL), D_MODEL ** -0.5),
    }


def reference(x_prompt, x_sample, state_pool, state_conv, cache_mem_k, cache_mem_v, mem_prompt,
              norm_pre, norm_post, mem_norm, w_mem_kv, w_in, pool_w, pool_scale, conv_w, conv_b,
              conv_ln_g, conv_ln_b, w_branch, w_out):
    xp = x_prompt
    xs = x_sample
    pool_p, conv_p, mk_p, mv_p, pool_s, conv_s = [], [], [], [], [], []
    for i in range(DEPTH):
        lw = (norm_pre[i], norm_post[i], w_in[i], pool_w[i], pool_scale[i], conv_w[i], conv_b[i],
              conv_ln_g[i], conv_ln_b[i], w_branch[i], w_out[i])
        mk, mv = mem_kv(mem_prompt, mem_norm[i], w_mem_kv[i])
        zp = jnp.zeros((xp.shape[0], POOL_BUF, BRANCH_W), xp.dtype)
        zc = jnp.zeros((xp.shape[0], CONV_BUF, BRANCH_W), xp.dtype)
        xp, nb_pool, nb_conv = layer(xp, zp, zc, mk, mv, 0, *lw)
        pool_p.append(nb_pool)
        conv_p.append(nb_conv)
        mk_p.append(mk)
        mv_p.append(mv)
        xs, ns_pool, ns_conv = layer(xs, state_pool[i], state_conv[i], cache_mem_k[i], cache_mem_v[i],
                                     PAST_LEN, *lw)
        pool_s.append(ns_pool)
        conv_s.append(ns_conv)
    return (xp, xs, jnp.stack(pool_p), jnp.stack(conv_p), jnp.stack(mk_p), jnp.stack(mv_p),
            jnp.stack(pool_s), jnp.stack(conv_s))
```

```python
import math
import numpy as np
from contextlib import ExitStack
import concourse.bass as bass
import concourse.mybir as mybir
from concourse.bass_utils import run_bass_kernel_spmd

F32 = mybir.dt.float32
BF16 = mybir.dt.bfloat16
AF = mybir.ActivationFunctionType
ALU = mybir.AluOpType

NCORES = 8
EPS = 1e-6
WINS = (2, 4, 8, 16)
P_GPRE, P_GMEM, P_PSC, P_CB, P_LNG, P_LNB, P_CW, P_ICNT = 0, 16, 32, 40, 48, 56, 64, 312
NPAR = 376
B_PIN, B_PGATE, B_CVAL, B_CGLU, B_CGATE, B_Q, B_XGATE, B_MERGE = 0, 2, 4, 6, 8, 10, 12, 14


class Sched:
    ENG = ('pe', 'act', 'dve', 'pool', 'sp')

    def __init__(self):
        self.ops = {e: [] for e in self.ENG}
        self.cnt = {e: 0 for e in self.ENG}
        self.dcnt = {}

    def op(self, eng, fn, deps=(), sig=True):
        tok = None
        if sig:
            self.cnt[eng] += 1
            tok = (eng, self.cnt[eng])
        self.ops[eng].append(('op', fn, [d for d in deps if d is not None], sig))
        return tok

    def dma(self, q, out, in_, key, deps=()):
        self.dcnt[key] = self.dcnt.get(key, 0) + 16
        tok = (key, self.dcnt[key])
        self.ops[q].append(('dma', (out, in_, key), [d for d in deps if d is not None], True))
        return tok

    def wait(self, eng, deps):
        self.ops[eng].append(('wait', None, [d for d in deps if d is not None], False))

    def emit(self, nc):
        keys = list(self.ENG) + sorted(self.dcnt.keys())
        with ExitStack() as es:
            sems = {}
            for k in keys:
                sems[k] = es.enter_context(nc.semaphore("s_" + k))
            blk = es.enter_context(nc.Block())
            sched = self

            def run(eng_name):
                def body(E):
                    known = {}
                    for kind, payload, deps, sig in sched.ops[eng_name]:
                        for (k, v) in deps:
                            if known.get(k, 0) < v:
                                E.wait_ge(sems[k], v)
                                known[k] = v
                        if kind == 'op':
                            ins = payload(E)
                            if sig:
                                ins.then_inc(sems[eng_name], 1)
                        elif kind == 'dma':
                            out, in_, key = payload
                            E.dma_start(out=out, in_=in_).then_inc(sems[key], 16)
                return body

            blk.tensor(run('pe'))
            blk.scalar(run('act'))
            blk.vector(run('dve'))
            blk.gpsimd(run('pool'))
            blk.sync(run('sp'))


DEBUG = None
MARKS = []


def build_program():
    nc = bass.Bass("TRN2", target_bir_lowering=False)
    if DEBUG:
        dbg_h = nc.dram_tensor("dbg_h", [128, 4096], BF16, kind="ExternalOutput").ap()
        dbg_r = nc.dram_tensor("dbg_r", [128, 16], F32, kind="ExternalOutput").ap()
        dbg_x = nc.dram_tensor("dbg_x", [128, 4096], F32, kind="ExternalOutput").ap()
        dbg_b = nc.dram_tensor("dbg_b", [128, 12 * 512], BF16, kind="ExternalOutput").ap()
        dbg_m = nc.dram_tensor("dbg_m", [128, 8 * 512], BF16, kind="ExternalOutput").ap()
        dbg_q = nc.dram_tensor("dbg_q", [128, 4 * 512], BF16, kind="ExternalOutput").ap()
        dbg_g = nc.dram_tensor("dbg_g", [128, 4 * 512], BF16, kind="ExternalOutput").ap()

    def din(name, shape):
        return nc.dram_tensor(name, list(shape), F32, kind="ExternalInput").ap()

    def dout(name, shape):
        return nc.dram_tensor(name, list(shape), F32, kind="ExternalOutput").ap()

    x_d = din("x", (17, 128, 1024))
    mem_d = din("mem", (2, 128, 1024))
    spool_d = din("spool", (2, 2, 120, 512))
    sconv_d = din("sconv", (2, 4, 120, 512))
    ckv_d = din("ckv", (2, 8, 128, 4096))
    win_d = din("w_in", (2, 26, 128, 2048))
    wbr_d = din("w_br", (2, 3, 4, 128, 1024))
    wout_d = din("w_out", (2, 4, 128, 2048))
    wkv_d = din("w_kv", (2, 4, 128, 2048))
    poolw_d = din("pool_w", (2, 128, 512))
    par_d = din("params", (128, NPAR))
    gpost_d = din("gpost", (128, 2048))
    gpre_d = din("gpre", (128, 2048))
    gmem_d = din("gmem", (128, 2048))
    ident_d = din("ident", (128, 128))

    wc_in = nc.dram_tensor("wc_in", [2, 26, 128, 2048], BF16).ap()
    wc_br = nc.dram_tensor("wc_br", [2, 3, 4, 128, 1024], BF16).ap()
    wc_out = nc.dram_tensor("wc_out", [2, 4, 128, 2048], BF16).ap()
    diag_c = nc.dram_tensor("diag_c", [2, 4, 128, 31 * 128], BF16).ap()

    y_d = dout("y", (17, 128, 1024))
    poolp_d = dout("pool_p", (2, 15, 512))
    convp_d = dout("conv_p", (2, 30, 512))
    kp_d = dout("k_p", (2, 256, 512))
    vp_d = dout("v_p", (2, 256, 512))
    pools_new_d = dout("pool_s_new", (2, 128, 512))
    convs_new_d = dout("conv_s_new", (2, 128, 512))
    pools_old_d = dout("pool_s_old", (2, 16, 7, 512))
    convs_old_d = dout("conv_s_old", (2, 16, 22, 512))

    S = Sched()
    es = ExitStack()
    NW = 52400
    A = es.enter_context(nc.sbuf_tensor("arena", [128, NW], F32))
    banks = [es.enter_context(nc.psum_tensor(f"ps{i}", [128, 512], F32)) for i in range(8)]
    off = [0]

    def carve(n, dt=F32):
        a = A[:, off[0]:off[0] + n]
        off[0] += n
        assert off[0] <= NW, off[0]
        if dt == BF16:
            a = a.bitcast(BF16)
        return a

    def v3(ap, a):
        return ap.rearrange("p (a b) -> p a b", a=a)

    par = carve(NPAR)
    identf = carve(128)
    onesf = carve(128)
    identb = carve(64, BF16)
    onesb = carve(64, BF16)
    poolwf = carve(512)
    poolwb = [carve(256, BF16) for _ in range(2)]
    KTp = [v3(carve(512, BF16), 4) for _ in range(2)]
    Vp = [v3(carve(512, BF16), 2) for _ in range(2)]
    ext0 = off[0]
    pext = [v3(carve(4 * 527), 4) for _ in range(2)]
    uext = [v3(carve(4 * 272, BF16), 4) for _ in range(2)]
    ext1 = off[0]
    off[0] = ext0
    psext = carve(4 * 16 * 23).rearrange("p (j b s) -> p j b s", j=4, b=16)
    usext = carve(4 * 16 * 19, BF16).rearrange("p (j b s) -> p j b s", j=4, b=16)
    off[0] = ext1
    Xg = v3(carve(4096), 4)
    hT = v3(carve(2048, BF16), 8)
    gateA = v3(carve(1024, BF16), 4)
    gateB = v3(carve(1024, BF16), 4)
    QT = v3(carve(1024, BF16), 4)
    brb = v3(carve(3072, BF16), 12)
    merged = v3(carve(2048, BF16), 8)
    tok_b = carve(512)
    tok_c = carve(512)
    tok_a = tok_b
    ssA = carve(8)
    rstdA = carve(8)
    ssD = carve(8)
    ssD2 = carve(8)
    rstdD = carve(8)
    NSLOT = 10
    slots = [carve(1024, BF16) for _ in range(NSLOT)]
    gpre = carve(2048)
    sttile = carve(512)
    ptmp = [carve(527) for _ in range(2)]
    mix = [carve(256, BF16) for _ in range(4)]
    tmpP = carve(16)
    T0 = off[0]
    tmax = [T0]

    def treset():
        off[0] = T0

    def tcarve(n, dt=F32):
        a = carve(n, dt)
        tmax[0] = max(tmax[0], off[0])
        return a

    def barrier():
        return [(e, S.cnt[e]) for e in ('pe', 'act', 'dve', 'pool') if S.cnt[e] > 0]

    treset()
    xnb = [tcarve(512, BF16) for _ in range(2)]
    junkA = tcarve(512, BF16)
    ytmp = [tcarve(1024) for _ in range(4)]
    junkD = tcarve(256, BF16)
    gpost = tcarve(1024)
    mhT = v3(tcarve(1024, BF16), 8)
    gmem = tcarve(1024)
    treset()
    sgj = [tcarve(512) for _ in range(2)]
    diag_flat = [tcarve(31 * 64, BF16) for _ in range(2)]
    diag = [v3(d_, 31) for d_ in diag_flat]
    cvg = v3(tcarve(2048), 4)
    sqg = v3(tcarve(2048), 4)
    mean_sb = tcarve(512)
    var_sb = tcarve(512)
    rstd_sb = tcarve(512)
    tmpA = tcarve(512)
    tmpB = tcarve(512)
    treset()
    Ebuf = [v3(tcarve(512, BF16), 2) for _ in range(2)]
    rz = tcarve(512)
    otmp = tcarve(512)
    KV2 = [tcarve(2048, BF16) for _ in range(4)]
    KTb2 = [v3(tcarve(512, BF16), 8) for _ in range(2)]
    Eb2 = [tcarve(32, BF16) for _ in range(2)]
    treset()
    sig3 = [tcarve(512) for _ in range(3)]
    macc = tcarve(512)
    ttmp = tcarve(512)
    off[0] = tmax[0]
    print("SBUF words used", off[0])

    bank_free = [None] * 8
    bank_next = [0]

    def get_bank():
        b = bank_next[0]
        bank_next[0] = (b + 1) % 6
        return b

    wlist = []
    PASSES = [(0, 512, False), (1, 512, False), (2, 512, False), (3, 512, False), (4, 128, True)]
    if DEBUG and DEBUG.startswith('P'):
        PASSES = PASSES[:int(DEBUG[1:])]

    def layer_blocks(l):
        bl = []
        g0 = P_GPRE + l * 8

        def wi(blk):
            return (win_d[l, blk], 8, g0, wc_in[l, blk])
        for jb in range(2):
            bl.append(wi(B_PGATE + jb))
        for jb in range(2):
            bl.append(wi(B_PIN + jb))
        for jb in range(2):
            bl.append(wi(B_CGLU + jb))
            bl.append(wi(B_CVAL + jb))
        for jb in range(2):
            bl.append(wi(B_CGATE + jb))
        for jb in range(2):
            bl.append(wi(B_Q + jb))
        for jb in range(2):
            bl.append(wi(B_XGATE + jb))
        for dq in range(4):
            for n in range(3):
                bl.append(wi(B_MERGE + n * 4 + dq))
            for n in range(3):
                bl.append((wbr_d[l, n, dq], 4, None, wc_br[l, n, dq]))
        for b4 in range(4):
            bl.append((wout_d[l, b4], 8, None, wc_out[l, b4]))
        return bl

    for l in range(2):
        for b4 in range(4):
            wlist.append((wkv_d[l, b4], 8, P_GMEM + l * 8, None, 'cast'))
    for _pi, _p in enumerate(PASSES):
        for l in range(2):
            for (a_, nk_, sc_, c_) in layer_blocks(l):
                wlist.append((a_, nk_, sc_, c_, 'cast+store' if (_pi == 0 and len(PASSES) > 1) else
                              (('cached2' if _p[2] else 'cached') if _pi > 0 else 'cast')))
    NB = len(wlist)
    w_issued = [0]
    w_ready = [None] * NB
    slot_last = [None] * NSLOT
    slot_owner = [None] * NSLOT
    t_par = [None]
    LA = 4

    slot_store = [None] * NSLOT
    cache_tok = {}

    def w_issue(i):
        ap, nk, sc, cap, mode = wlist[i]
        sl = i % NSLOT
        if slot_owner[sl] is not None:
            assert slot_last[sl] is not None, ("slot not released", i, slot_owner[sl])
        deps = list(slot_last[sl] or []) + [slot_store[sl]]
        dst = slots[sl][:, 0:nk * 256]
        if mode in ('cached', 'cached2'):
            q_ = 'sp' if (mode == 'cached2' and i % 2 == 0) else 'pool'
            td = S.dma(q_, dst, cap, f"wsl{sl}", deps=deps + [cache_tok[str(cap)]])
        else:
            td = S.dma('pool', dst, ap, f"wsl{sl}", deps=deps)
            if mode == 'cast+store':
                ts = S.dma('sp', cap, dst, f"wcs{sl}", deps=[td])
                slot_store[sl] = ts
                cache_tok[str(cap)] = ts
        slot_owner[sl] = i
        slot_last[sl] = None
        w_ready[i] = td

    w_ptr = [0]

    def wget():
        i = w_ptr[0]
        w_ptr[0] += 1
        while w_issued[0] < min(NB, i + 1 + LA):
            w_issue(w_issued[0])
            w_issued[0] += 1
        sl = i % NSLOT
        nk = wlist[i][1]
        return i, v3(slots[sl][:, 0:nk * 256], nk), w_ready[i]

    def wrel(i, tok):
        slot_last[i % NSLOT] = [tok]

    def mm_group(out_ap, pairs, deps, bank):
        n = len(pairs)
        t = None
        for i, (lt, rh) in enumerate(pairs):
            d = (list(deps) + [bank_free[bank]]) if i == 0 else []
            t = S.op('pe', lambda E, lt=lt, rh=rh, i=i, n=n: E.matmul(out_ap, lhsT=lt, rhs=rh, start=(i == 0), stop=(i == n - 1)),
                     d, sig=(i == n - 1))
        return t

    store_toks = []

    t_par[0] = S.dma('sp', par, par_d, "ldp0")
    t_id = S.dma('sp', identf, ident_d, "ldp2")
    t_ib = S.op('dve', lambda E: E.tensor_copy(out=identb, in_=identf), [t_id])
    t_of = S.op('dve', lambda E: E.memset(onesf, 1.0 / 512.0))
    t_ob = S.op('dve', lambda E: E.memset(onesb, 1.0))
    t_z = None
    for l in range(2):
        S.op('dve', lambda E, l=l: E.memset(pext[l][:, :, 0:15], 0.0), sig=False)
        t_z = S.op('dve', lambda E, l=l: E.memset(uext[l][:, :, 0:30], 0.0))
    t_pw = []
    for l in range(2):
        td = S.dma('sp', poolwf, poolw_d[l], "ldpw", deps=[t_pw[-1]] if t_pw else [])
        t_pw.append(S.op('dve', lambda E, l=l: E.tensor_copy(out=poolwb[l], in_=poolwf), [td]))

    junk_last = [None, None]

    def a_front(t, src_tile, src_rdy, g_ap, g_ready, bank=None):
        tsq_ = S.op('act', lambda E, t=t: E.activation(out=junkA, in_=src_tile, func=AF.Square, accum_out=ssA[:, t:t + 1]),
                    [src_rdy, junk_last[0]])
        junk_last[0] = tsq_
        t_ln = S.op('act', lambda E, t=t: E.activation(out=rstdA[:, t:t + 1], in_=ssA[:, t:t + 1], func=AF.Ln, scale=1.0 / 1024, bias=EPS), [tsq_])
        t_ex = S.op('act', lambda E, t=t: E.activation(out=rstdA[:, t:t + 1], in_=rstdA[:, t:t + 1], func=AF.Exp, scale=-0.5), [t_ln])
        xb = xnb[t % 2]
        t_xn = S.op('dve', lambda E, t=t, xb=xb: E.scalar_tensor_tensor(out=xb, in0=src_tile, scalar=rstdA[:, t:t + 1], in1=g_ap,
                                                                         op0=ALU.mult, op1=ALU.mult),
                    [t_ex, rms_transpose.xb_free[t % 2], g_ready])
        b = get_bank() if bank is None else bank
        pb = banks[b].bitcast(BF16)
        tt = None
        for k in range(8):
            tt = S.op('pe', lambda E, k=k, xb=xb, pb=pb: E.transpose(out=pb[:, k * 128:(k + 1) * 128], in_=xb[:, k * 128:(k + 1) * 128], identity=identb),
                      [t_xn, t_ib, bank_free[b]] if k == 0 else [], sig=(k == 7))
        rms_transpose.xb_free[t % 2] = tt
        return b, pb, tt

    def a_back(t, front, dst, dst_free):
        b, pb, tt = front
        te = S.op('act', lambda E, t=t, pb=pb: E.activation(out=dst[:, :, t * 128:(t + 1) * 128], in_=v3(pb, 8), func=AF.Copy),
                  [tt] + list(dst_free))
        bank_free[b] = te
        return te

    def rms_transpose(src_tiles, nt, dst, src_ready, dst_free, g_ap, g_ready):
        outs = []
        fr = {0: a_front(0, src_tiles[0], src_ready[0], g_ap, g_ready)}
        for t in range(nt):
            if t + 1 < nt:
                fr[t + 1] = a_front(t + 1, src_tiles[t + 1], src_ready[t + 1], g_ap, g_ready)
            outs.append(a_back(t, fr.pop(t), dst, dst_free))
        return outs
    rms_transpose.xb_free = [None, None]
    rms_transpose.sq_free = [None, None]

    t_m = [S.dma('sp', ytmp[t], mem_d[t], f"ldm{t}") for t in range(2)]
    x_pref = {}
    for t in range(PASSES[0][1] // 128):
        x_pref[t] = S.dma('sp', Xg[:, t, :], x_d[PASSES[0][0] * 4 + t], f"ldx{t}")
    t_gpre = S.dma('sp', gpre, gpre_d, "ldp3")
    tokbufs = [tok_b, tok_c]
    tok_st = [None, None]
    tok_i = [0]
    t_mh = []
    mh_rd = []
    for l in range(2):
        t_gm = S.dma('sp', gmem, gmem_d[:, l * 1024:(l + 1) * 1024], "ldgm", deps=t_mh)
        t_mh = rms_transpose([ytmp[t] for t in range(2)], 2, mhT, t_m, mh_rd, gmem, t_gm)
        wk = [wget() for _ in range(2)]
        for h in range(4):
            i, W, tr = wk[h // 2]
            b = get_bank()
            tm = mm_group(banks[b][:, 0:256], [(W[:, k, (h % 2) * 128:(h % 2) * 128 + 128], mhT[:, k, :]) for k in range(8)],
                          [tr] + t_mh, b)
            te = S.op('act', lambda E, l=l, h=h, b=b: E.activation(out=KTp[l][:, h, :], in_=banks[b][:, 0:256], func=AF.Copy), [tm])
            bank_free[b] = te
        for mt in range(2):
            b = get_bank()
            tm = None
            for q in range(2):
                i, W, tr = wk[q]
                tm = mm_group(banks[b][:, q * 256:(q + 1) * 256], [(mhT[:, k, mt * 128:(mt + 1) * 128], W[:, k, :]) for k in range(8)],
                              [tr] + t_mh, b)
            tkb = tokbufs[tok_i[0] % 2]
            te = S.op('dve', lambda E, b=b, tkb=tkb: E.tensor_copy(out=tkb, in_=banks[b][:, :]), [tm, tok_st[tok_i[0] % 2]])
            bank_free[b] = te
            tok_st[tok_i[0] % 2] = S.dma('sp', kp_d[l, mt * 128:(mt + 1) * 128, :], tkb, f"st_a{tok_i[0] % 2}", [te])
            store_toks.append(tok_st[tok_i[0] % 2])
            tok_i[0] += 1
        for q in range(2):
            wrel(wk[q][0], tm)
        wv = [wget() for _ in range(2)]
        for mt in range(2):
            b = get_bank()
            tm = None
            for q in range(2):
                i, W, tr = wv[q]
                tm = mm_group(banks[b][:, q * 256:(q + 1) * 256], [(mhT[:, k, mt * 128:(mt + 1) * 128], W[:, k, :]) for k in range(8)],
                              [tr] + t_mh, b)
            tkb = tokbufs[tok_i[0] % 2]
            te = S.op('dve', lambda E, b=b, tkb=tkb: E.tensor_copy(out=tkb, in_=banks[b][:, :]), [tm, tok_st[tok_i[0] % 2]])
            te2 = S.op('act', lambda E, l=l, mt=mt, b=b: E.activation(out=Vp[l][:, mt, :], in_=banks[b][:, :], func=AF.Copy), [tm, te])
            bank_free[b] = te2
            tok_st[tok_i[0] % 2] = S.dma('sp', vp_d[l, mt * 128:(mt + 1) * 128, :], tkb, f"st_a{tok_i[0] % 2}", [te])
            store_toks.append(tok_st[tok_i[0] % 2])
            tok_i[0] += 1
        for q in range(2):
            wrel(wv[q][0], tm)
        mh_rd = [tm]
    xg_free = [[] for _ in range(4)]
    set_tokbuf_free('b', [tok_st[0]])
    set_tokbuf_free('c', [tok_st[1]])
    for l in range(2):
        src = spool_d[l].rearrange("t (b r) c -> (t b) r c", r=15)
        store_toks.append(S.dma('sp', pools_old_d[l], src[:, 8:15, :], "st_o"))
        src = sconv_d[l].rearrange("t (b r) c -> (t b) r c", r=30)
        store_toks.append(S.dma('sp', convs_old_d[l], src[:, 8:30, :], "st_o"))

    diag_ready = {}
    diag_all_st = [[], []]
    first_pass = PASSES[0][0]

    for e_ in ('pe', 'act', 'dve'):
        S.wait(e_, [t_par[0], t_id, t_gpre])

    class _Stop(Exception):
        pass

    def dbg_stop(tag, samp, l):
        if DEBUG == tag and samp and l == 1:
            bt = barrier()
            ds = [S.dma('sp', dbg_b, brb.rearrange("p a b -> p (a b)"), "dbg1", bt),
                  S.dma('sp', dbg_m, merged.rearrange("p a b -> p (a b)"), "dbg2", bt),
                  S.dma('sp', dbg_x, Xg.rearrange("p a b -> p (a b)"), "dbg3", bt),
                  S.dma('sp', dbg_q, QT.rearrange("p a b -> p (a b)"), "dbg4", bt),
                  S.dma('sp', dbg_g, gateB.rearrange("p a b -> p (a b)"), "dbg5", bt),
                  S.dma('sp', dbg_h, hT.rearrange("p a b -> p (a b)"), "dbg6", bt)]
            S.wait('sp', ds)
            S.emit(nc)
            es.close()
            raise _Stop()

    cur_bar = [[]]
    prev_pool_T = [False]

    def mark(label):
        MARKS.append((label, sum(1 for o in S.ops['pe'] if o[0] == 'op')))

    def phase_barrier(pool_too=False, tokens=None):
        engs = ['pe', 'act', 'dve'] + (['pool'] if (pool_too or prev_pool_T[0]) else [])
        bt = [(e, S.cnt[e]) for e in engs if S.cnt[e] > 0]
        if tokens is not None:
            bt = list(tokens)
        bt = bt + diag_all_st[0] + diag_all_st[1]
        S.wait('act', bt)
        S.wait('dve', bt)
        if pool_too:
            S.wait('pool', bt)
        prev_pool_T[0] = pool_too
        cur_bar[0] = bt
        return bt

    carry_tok = {('p', 0): [t_z], ('p', 1): [t_z], ('u', 0): [t_z], ('u', 1): [t_z], ('ps',): [], ('us',): []}
    x_ready = [None] * 4
    fused_th = [None]
    last_tokA = [None]

    for (p, N, samp) in PASSES:
      try:
        nt = N // 128
        for t in range(nt):
            if t in x_pref:
                x_ready[t] = x_pref.pop(t)
            else:
                x_ready[t] = S.dma('sp', Xg[:, t, :], x_d[p * 4 + t], f"ldx{t}", deps=xg_free[t])
        _pi = [q[0] for q in PASSES].index(p)
        nxt_pass = PASSES[_pi + 1] if _pi + 1 < len(PASSES) else None
        if samp:
            bt0 = barrier()
            carry_tok[('ps',)] = list(bt0)
            carry_tok[('us',)] = list(bt0)
        for l in range(2):
            g0 = P_GPRE + l * 8
            mark(f"p{p}l{l}:A")
            phase_barrier()
            if fused_th[0] is not None:
                t_h = fused_th[0]
                fused_th[0] = None
            else:
                t_h = rms_transpose([Xg[:, t, :] for t in range(nt)], nt, hT, x_ready, hT_free_list(), gpre[:, l * 1024:(l + 1) * 1024], t_gpre)
            hT_rd = []
            bt_afterA = [(e, S.cnt[e]) for e in ('pe', 'act', 'dve') if S.cnt[e] > 0]
            if DEBUG == 'A' or (DEBUG == 'SA' and samp and l == 1):
                d1 = S.dma('sp', dbg_h, hT.rearrange("p a b -> p (a b)"), "dbg1", t_h)
                d2 = S.dma('sp', dbg_r[:, 0:8], ssA, "dbg2", t_h)
                d3 = S.dma('sp', dbg_r[:, 8:16], rstdA, "dbg3", t_h)
                d4 = S.dma('sp', dbg_x, Xg.rearrange("p a b -> p (a b)"), "dbg4", t_h)
                S.wait('sp', [d1, d2, d3, d4])
                S.emit(nc)
                es.close()
                return nc

            def proj(Wv, jj, deps):
                b = get_bank()
                tm = mm_group(banks[b][:, 0:N], [(Wv[:, k, jj * 128:(jj + 1) * 128], hT[:, k, 0:N]) for k in range(8)],
                              list(deps) + t_h, b)
                hT_rd.append(tm)
                return b, tm

            tokmaj = samp or p == 3
            tl = nt - 1

            if samp:
                for tb in range(2):
                    td = S.dma('sp', sttile[0:120, :], spool_d[l, tb], "ldst", deps=state_free() + cur_bar[0])
                    b = get_bank()
                    tt = None
                    for j in range(4):
                        tt = S.op('pe', lambda E, j=j, b=b: E.transpose(out=banks[b][:, j * 120:(j + 1) * 120], in_=sttile[0:120, j * 128:(j + 1) * 128],
                                                                       identity=identf[0:120, 0:120]),
                                  [td, t_id, bank_free[b]] if j == 0 else [], sig=(j == 3))
                    set_state_free([tt])
                    te = None
                    for j in range(4):
                        te = S.op('dve', lambda E, j=j, b=b, tb=tb: E.tensor_copy(
                            out=psext[:, j, tb * 8:(tb + 1) * 8, 0:15],
                            in_=banks[b][:, j * 120:(j + 1) * 120].rearrange("p (b r) -> p b r", r=15)),
                            [tt] + carry_tok[('ps',)] if j == 0 else [])
                    bank_free[b] = te
                    sample_hist_p = te
                for tb in range(4):
                    td = S.dma('sp', sttile[0:120, :], sconv_d[l, tb], "ldst", deps=state_free() + cur_bar[0])
                    b = get_bank()
                    tt = None
                    for j in range(4):
                        tt = S.op('pe', lambda E, j=j, b=b: E.transpose(out=banks[b][:, j * 120:(j + 1) * 120], in_=sttile[0:120, j * 128:(j + 1) * 128],
                                                                       identity=identf[0:120, 0:120]),
                                  [td, t_id, bank_free[b]] if j == 0 else [], sig=(j == 3))
                    set_state_free([tt])
                    te = None
                    for j in range(4):
                        te = S.op('dve', lambda E, j=j, b=b, tb=tb: E.tensor_copy(
                            out=usext[:, j, tb * 4:(tb + 1) * 4, 0:30],
                            in_=banks[b][:, j * 120:(j + 1) * 120].rearrange("p (b r) -> p b r", r=30)),
                            [tt] + carry_tok[('us',)] if j == 0 else [])
                    bank_free[b] = te
                    sample_hist_u = te

            dbg_stop('SB1', samp, l)
            mark(f"p{p}l{l}:pool")
            gate_w = []
            for jb in range(2):
                ig, Wg, trg = wget()
                tm = None
                for jj in range(2):
                    j = 2 * jb + jj
                    b, tm = proj(Wg, jj, [trg])
                    tg = S.op('act', lambda E, N=N, b=b, j=j: E.activation(out=gateB[:, j, 0:N], in_=banks[b][:, 0:N], func=AF.Silu),
                              [tm] + gateB_free[j])
                    bank_free[b] = tg
                    gate_w.append(tg)
                wrel(ig, tm)
            tokb_p = 6 if tokmaj else None
            tm_tp = None
            t_a_last = None
            pool_rd = []
            pool_pending = []
            pend = [None]
            pend2 = [None]

            def pool_mm():
                if pend[0] is None:
                    return
                j_, mx_, tmx_ = pend[0]
                pend[0] = None
                b2 = get_bank()
                tm2 = mm_group(banks[b2][:, 0:N], [(poolwb[l][:, j_ * 128:(j_ + 1) * 128], mx_[:, 0:N])], [tmx_, t_pw[l]], b2)
                mix_free4[j_] = tm2
                pend2[0] = (j_, b2, tm2)

            def pool_ep():
                if pend2[0] is None:
                    return None
                j_, b2, tm2 = pend2[0]
                pend2[0] = None
                ta = S.op('dve', lambda E, N=N, b2=b2, j=j_, l=l: E.scalar_tensor_tensor(
                    out=brb[:, j, 0:N], in0=banks[b2][:, 0:N], scalar=par[:, P_PSC + l * 4 + j:P_PSC + l * 4 + j + 1],
                    in1=gateB[:, j, 0:N], op0=ALU.mult, op1=ALU.mult), [tm2, gate_w[j_]] + br_free)
                bank_free[b2] = ta
                gateB_free[j_] = [ta]
                return ta

            for jb in range(2):
                ip, Wp, trp = wget()
                tlast = None
                for jj in range(2):
                    j = 2 * jb + jj
                    win = WINS[j]
                    b, tm = proj(Wp, jj, [trp])
                    tlast = tm
                    if samp:
                        xe = psext[:, j]
                        o_ap = xe[:, :, 15:23]
                        i0 = banks[b][:, 0:N].rearrange("p (b s) -> p b s", s=8)
                        wdeps = [sample_hist_p]
                        L = 23
                        sl_ = lambda ap, a, b_: ap[:, :, a:b_]
                        tv = [ptmp[q][:, 0:16 * 23].rearrange("p (b s) -> p b s", s=23) for q in range(2)]
                    else:
                        xe = pext[l][:, j]
                        o_ap = xe[:, 15:15 + N]
                        i0 = banks[b][:, 0:N]
                        wdeps = carry_tok[('p', l)]
                        L = 15 + N
                        sl_ = lambda ap, a, b_: ap[:, a:b_]
                        tv = [ptmp[q][:, 0:L] for q in range(2)]
                    tw = S.op('act', lambda E, o_ap=o_ap, i0=i0: E.activation(out=o_ap, in_=i0, func=AF.Copy), [tm] + list(wdeps) + pool_rd[-1:])
                    bank_free[b] = tw
                    s_ap = xe
                    v = 0
                    tprev = tw
                    d = 1
                    qi = 0
                    while d < win:
                        dst = tv[qi]
                        tprev = S.op('dve', lambda E, dst=dst, s_ap=s_ap, v=v, d=d, L=L, sl_=sl_: E.tensor_tensor(
                            out=sl_(dst, v + d, L), in0=sl_(s_ap, v + d, L), in1=sl_(s_ap, v, L - d), op=ALU.add), [tprev, t_a_last])
                        s_ap = dst
                        v += d
                        d *= 2
                        qi ^= 1
                    mx = mix[j]
                    if samp:
                        mo = mx[:, 0:N].rearrange("p (b s) -> p b s", s=8)
                    else:
                        mo = mx[:, 0:N]
                    tmx = S.op('dve', lambda E, mo=mo, s_ap=s_ap, xe=xe, L=L, sl_=sl_, win=win: E.scalar_tensor_tensor(
                        out=mo, in0=sl_(s_ap, 15, L), scalar=1.0 / win, in1=sl_(xe, 15, L), op0=ALU.mult, op1=ALU.subtract),
                        [tprev, mix_free4[j]])
                    if p == 0:
                        S.op('dve', lambda E, s_ap=s_ap, j=j: E.tensor_tensor(out=tmpP[:, 0:15], in0=s_ap[:, 15:30],
                                                                              in1=par[:, P_ICNT + j * 16:P_ICNT + j * 16 + 15], op=ALU.mult), [tmx], sig=True)
                        tmx = S.op('dve', lambda E, mx=mx, xe=xe: E.tensor_tensor(out=mx[:, 0:15], in0=tmpP[:, 0:15], in1=xe[:, 15:30], op=ALU.subtract),
                                   [('dve', S.cnt['dve'])])
                    pool_rd.append(tmx)
                    pool_pending.append((j, mx, tmx))
                if tokmaj:
                    tm_tp = mm_group(banks[tokb_p][:, jb * 256:(jb + 1) * 256],
                                     [(hT[:, k, tl * 128:(tl + 1) * 128], Wp[:, k, :]) for k in range(8)], [trp] + t_h, tokb_p)
                    hT_rd.append(tm_tp)
                    tlast = tm_tp
                wrel(ip, tlast)
            if tokmaj:
                tcp = S.op('dve', lambda E: E.tensor_copy(out=tok_c, in_=banks[tokb_p][:, :]), [tm_tp] + tokbuf_free('c'))
                bank_free[tokb_p] = tcp
                if samp:
                    st = S.dma('sp', pools_new_d[l], tok_c, "st_c", [tcp])
                else:
                    st = S.dma('sp', poolp_d[l], tok_c[113:128, :], "st_c", [tcp])
                store_toks.append(st)
                set_tokbuf_free('c', [st])
            mark(f"p{p}l{l}:conv")
            phase_barrier(tokens=bt_afterA)
            diag_ld = {}
            if p != first_pass:
                for j_ in range(2):
                    diag_ld[j_] = S.dma('sp', diag_flat[j_], diag_c[l, j_], f"ldd{j_}", deps=[diag_ready[(l, j_)]] + diag_free[j_] + diag_all_st[j_] + cur_bar[0])
            dbg_stop('SB0', samp, l)
            u_wr = []
            tokb_u = 7 if tokmaj else None
            tokb_g = 6 if tokmaj else None
            tm_tu = tm_tg = None
            for jb in range(2):
                ig, Wg, trg = wget()
                iv, Wv, trv = wget()
                tlast = None
                for jj in range(2):
                    j = 2 * jb + jj
                    b, tm = proj(Wg, jj, [trg])
                    sg = sgj[j % 2]
                    ts = S.op('act', lambda E, N=N, b=b, sg=sg: E.activation(out=sg[:, 0:N], in_=banks[b][:, 0:N], func=AF.Sigmoid),
                              [tm, sg_free[j % 2]])
                    bank_free[b] = ts
                    b2, tm2 = proj(Wv, jj, [trv])
                    if samp:
                        o_ap = usext[:, j, :, 30:38]
                        i0 = banks[b2][:, 0:N].rearrange("p (b s) -> p b s", s=8)
                        i1 = sg[:, 0:N].rearrange("p (b s) -> p b s", s=8)
                        wdeps = [sample_hist_u]
                    else:
                        o_ap = uext[l][:, j, 30:30 + N]
                        i0 = banks[b2][:, 0:N]
                        i1 = sg[:, 0:N]
                        wdeps = carry_tok[('u', l)]
                    tu = S.op('dve', lambda E, o_ap=o_ap, i0=i0, i1=i1: E.tensor_tensor(out=o_ap, in0=i0, in1=i1, op=ALU.mult),
                              [tm2, ts] + list(wdeps))
                    bank_free[b2] = tu
                    sg_free[j % 2] = tu
                    u_wr.append(tu)
                    tlast = tm2
                if tokmaj:
                    tm_tg = mm_group(banks[tokb_g][:, jb * 256:(jb + 1) * 256],
                                     [(hT[:, k, tl * 128:(tl + 1) * 128], Wg[:, k, :]) for k in range(8)], [trg] + t_h, tokb_g)
                    tm_tu = mm_group(banks[tokb_u][:, jb * 256:(jb + 1) * 256],
                                     [(hT[:, k, tl * 128:(tl + 1) * 128], Wv[:, k, :]) for k in range(8)], [trv] + t_h, tokb_u)
                    hT_rd.append(tm_tu)
                    tlast = tm_tu
                wrel(ig, tlast)
                wrel(iv, tlast)
            if tokmaj:
                ts = S.op('act', lambda E: E.activation(out=tok_b, in_=banks[tokb_g][:, :], func=AF.Sigmoid), [tm_tg, tm_tu] + tokbuf_free('b'))
                bank_free[tokb_g] = ts
                tu = S.op('dve', lambda E: E.tensor_tensor(out=tok_b, in0=banks[tokb_u][:, :], in1=tok_b, op=ALU.mult), [ts, tm_tu])
                bank_free[tokb_u] = tu
                if samp:
                    st = S.dma('sp', convs_new_d[l], tok_b, "st_b", [tu])
                else:
                    st = S.dma('sp', convp_d[l], tok_b[98:128, :], "st_b", [tu])
                store_toks.append(st)
                set_tokbuf_free('b', [st])
            dbg_stop('SC1', samp, l)
            conv_mm_last = None
            cv_wr = []
            for j in range(4):
                dg = diag[j % 2]
                st_d = None
                if p == first_pass:
                    td = None
                    for k in range(31):
                        td = S.op('dve', lambda E, dg=dg, k=k, j=j, l=l: E.tensor_scalar(
                            out=dg[:, k, :], in0=identb, scalar1=par[:, P_CW + (l * 4 + j) * 31 + k:P_CW + (l * 4 + j) * 31 + k + 1],
                            scalar2=None, op0=ALU.mult), ([t_ib, t_par[0]] + diag_free[j % 2] + diag_all_st[j % 2]) if k == 0 else [], sig=(k == 30))
                    st_d = S.dma('sp', diag_c[l, j], diag_flat[j % 2], f"std{l}{j}", [td])
                    diag_ready[(l, j)] = st_d
                    diag_all_st[j % 2].append(st_d)
                else:
                    td = diag_ld[j]
                b = get_bank()
                if samp:
                    pairs = [(dg[:, k, :], usext[:, j, :, k:k + 8]) for k in range(31)]
                    o_ap = banks[b][:, 0:N].rearrange("p (b s) -> p b s", s=8)
                else:
                    pairs = [(dg[:, k, :], uext[l][:, j, k:k + N]) for k in range(31)]
                    o_ap = banks[b][:, 0:N]
                tm = mm_group(o_ap, pairs, [td] + u_wr, b)
                diag_free[j % 2] = [tm, st_d]
                conv_mm_last = tm
                if j + 2 < 4 and p != first_pass:
                    diag_ld[j + 2] = S.dma('sp', diag_flat[j % 2], diag_c[l, j + 2], f"ldd{j % 2}", deps=[diag_ready[(l, j + 2)], tm] + diag_all_st[j % 2] + cur_bar[0])
                t1 = S.op('act', lambda E, N=N, b=b, j=j, l=l: E.activation(out=cvg[:, j, 0:N], in_=banks[b][:, 0:N], func=AF.Identity,
                                                                      bias=par[:, P_CB + l * 4 + j:P_CB + l * 4 + j + 1]), [tm] + cvg_free)
                t2 = S.op('act', lambda E, N=N, b=b, j=j, l=l: E.activation(out=sqg[:, j, 0:N], in_=banks[b][:, 0:N], func=AF.Square,
                                                                      bias=par[:, P_CB + l * 4 + j:P_CB + l * 4 + j + 1]), [tm])
                bank_free[b] = t2
                cv_wr.append(t2)
            if not samp:
                tc = S.op('dve', lambda E, N=N, l=l: E.tensor_copy(out=uext[l][:, :, 0:30], in_=uext[l][:, :, N:N + 30]), [conv_mm_last])
                carry_tok[('u', l)] = [tc]
            else:
                carry_tok[('us',)] = [conv_mm_last]
            for (j_, mx_, tmx_) in pool_pending:
                pend[0] = (j_, mx_, tmx_)
                pool_mm()
                t_a_last = pool_ep()
            if not samp:
                tc = S.op('dve', lambda E, N=N, l=l: E.tensor_copy(out=pext[l][:, :, 0:15], in_=pext[l][:, :, N:N + 15]), [t_a_last])
                carry_tok[('p', l)] = [tc]
            else:
                carry_tok[('ps',)] = [t_a_last]

            dbg_stop('SC2', samp, l)
            bm = get_bank()
            tmm = mm_group(banks[bm][:, 0:N], [(onesf, cvg[:, j, 0:N]) for j in range(4)], cv_wr + [t_of], bm)
            bq = get_bank()
            tmq = mm_group(banks[bq][:, 0:N], [(onesf, sqg[:, j, 0:N]) for j in range(4)], cv_wr, bq)
            t_mean = S.op('act', lambda E, bm=bm, N=N: E.activation(out=mean_sb[:, 0:N], in_=banks[bm][:, 0:N], func=AF.Copy), [tmm] + stat_free)
            bank_free[bm] = t_mean
            dbg_stop('SC0', samp, l)
            gate_w = []
            for jb in range(2):
                ig, Wg, trg = wget()
                tm = None
                for jj in range(2):
                    j = 2 * jb + jj
                    b, tm = proj(Wg, jj, [trg])
                    tg = S.op('act', lambda E, N=N, b=b, j=j: E.activation(out=gateA[:, j, 0:N], in_=banks[b][:, 0:N], func=AF.Silu),
                              [tm] + gateA_free[j])
                    bank_free[b] = tg
                    gate_w.append(tg)
                wrel(ig, tm)
            t_m2 = S.op('dve', lambda E, N=N: E.tensor_tensor(out=var_sb[:, 0:N], in0=mean_sb[:, 0:N], in1=mean_sb[:, 0:N], op=ALU.mult), [t_mean])
            t_var = S.op('dve', lambda E, bq=bq, N=N: E.tensor_tensor(out=var_sb[:, 0:N], in0=banks[bq][:, 0:N], in1=var_sb[:, 0:N], op=ALU.subtract), [t_m2, tmq])
            bank_free[bq] = t_var
            t_l = S.op('act', lambda E, N=N: E.activation(out=rstd_sb[:, 0:N], in_=var_sb[:, 0:N], func=AF.Ln, bias=EPS), [t_var])
            t_r = S.op('act', lambda E, N=N: E.activation(out=rstd_sb[:, 0:N], in_=rstd_sb[:, 0:N], func=AF.Exp, scale=-0.5), [t_l])
            mean_b = mean_sb[:, 0:N].unsqueeze(1).broadcast_to([128, 4, N])
            rstd_b = rstd_sb[:, 0:N].unsqueeze(1).broadcast_to([128, 4, N])
            ta = S.op('dve', lambda E, N=N, mean_b=mean_b: E.tensor_tensor(out=cvg[:, :, 0:N], in0=cvg[:, :, 0:N], in1=mean_b, op=ALU.subtract),
                      [t_mean, tmm])
            tb = S.op('dve', lambda E, N=N, rstd_b=rstd_b: E.tensor_tensor(out=cvg[:, :, 0:N], in0=cvg[:, :, 0:N], in1=rstd_b, op=ALU.mult), [ta, t_r])
            tcs_all = []
            for j in range(4):
                tcs_all.append(S.op('act', lambda E, N=N, j=j, l=l: E.activation(out=sqg[:, j, 0:N], in_=cvg[:, j, 0:N], func=AF.Silu,
                                                                                 scale=par[:, P_LNG + l * 4 + j:P_LNG + l * 4 + j + 1],
                                                                                 bias=par[:, P_LNB + l * 4 + j:P_LNB + l * 4 + j + 1]), [tb, tmq]))
            tb_last = S.op('dve', lambda E, N=N: E.tensor_tensor(out=brb[:, 4:8, 0:N], in0=sqg[:, :, 0:N], in1=gateA[:, :, 0:N], op=ALU.mult),
                           tcs_all + gate_w + br_free)
            for j in range(4):
                gateA_free[j] = [tb_last]
            cvg_free[:] = [tb_last]
            stat_free[:] = [tb_last]
            t_bconv = tb_last

            dbg_stop('SB2', samp, l)
            mark(f"p{p}l{l}:attn")
            phase_barrier()
            q_w = []
            for jb in range(2):
                iq, Wq, trq = wget()
                tm = None
                for jj in range(2):
                    j = 2 * jb + jj
                    b, tm = proj(Wq, jj, [trq])
                    tq = S.op('act', lambda E, N=N, b=b, j=j: E.activation(out=QT[:, j, 0:N], in_=banks[b][:, 0:N], func=AF.Copy,
                                                                      scale=1.0 / math.sqrt(128.0)), [tm] + qt_free)
                    bank_free[b] = tq
                    q_w.append(tq)
                wrel(iq, tm)
            gate_w = [None] * 4
            xg_blocks = [wget(), wget()]

            def xg_chunk(j):
                ig, Wg, trg = xg_blocks[j // 2]
                b, tm = proj(Wg, j % 2, [trg])
                tg = S.op('act', lambda E, N=N, b=b, j=j: E.activation(out=gateB[:, j, 0:N], in_=banks[b][:, 0:N], func=AF.Silu),
                          [tm] + gateB_free[j])
                bank_free[b] = tg
                gate_w[j] = tg
                if j % 2 == 1:
                    wrel(ig, tm)

            if samp:
                for j in range(4):
                    xg_chunk(j)
            att_last = None
            if not samp:
                def s_stage(h):
                    Eh = Ebuf[h % 2]
                    tes = []
                    for mt in range(2):
                        b = get_bank()
                        tm = mm_group(banks[b][:, 0:N], [(KTp[l][:, h, mt * 128:(mt + 1) * 128], QT[:, h, 0:N])], [q_w[h]], b)
                        te = S.op('act', lambda E, N=N, b=b, Eh=Eh, mt=mt: E.activation(out=Eh[:, mt, 0:N], in_=banks[b][:, 0:N], func=AF.Exp),
                                  [tm, e_free[h % 2]])
                        bank_free[b] = te
                        tes.append(te)
                    return tes

                tes_next = s_stage(0)
                for h in range(4):
                    xg_chunk(h)
                    Eh = Ebuf[h % 2]
                    tes = tes_next
                    if h + 1 < 4:
                        tes_next = s_stage(h + 1)
                    bo = get_bank()
                    tmo = mm_group(banks[bo][:, 0:N], [(Vp[l][:, mt, h * 128:(h + 1) * 128], Eh[:, mt, 0:N]) for mt in range(2)], tes, bo)
                    bz = get_bank()
                    tmz = mm_group(banks[bz][:, 0:N], [(onesb, Eh[:, mt, 0:N]) for mt in range(2)], tes + [t_ob], bz)
                    e_free[h % 2] = tmz
                    t1 = S.op('dve', lambda E, N=N, bz=bz: E.reciprocal(out=rz[:, 0:N], in_=banks[bz][:, 0:N]), [tmz, att_last])
                    bank_free[bz] = t1
                    t2 = S.op('dve', lambda E, N=N, bo=bo: E.tensor_tensor(out=otmp[:, 0:N], in0=banks[bo][:, 0:N], in1=rz[:, 0:N], op=ALU.mult), [t1, tmo])
                    bank_free[bo] = t2
                    t3 = S.op('dve', lambda E, N=N, h=h: E.tensor_tensor(out=brb[:, 8 + h, 0:N], in0=otmp[:, 0:N], in1=gateB[:, h, 0:N], op=ALU.mult),
                              [t2, gate_w[h]] + br_free)
                    gateB_free[h] = [t3]
                    att_last = t3
            else:
                BO, BZ = 6, 7
                kb_free = [[], []]
                vb_free = [[], []]
                kt_free = [[], []]
                eb_free = [[], []]

                kv_tok = {}
                kvbuf_free = [[], [], [], []]

                def sa_load(bp):
                    q = bp % 4
                    kv_tok[bp] = S.dma('pool', KV2[q], ckv_d[l, bp], f"ldkv{q}", deps=kvbuf_free[q] + cur_bar[0])
                    kvbuf_free[q] = []

                def sa_stage1(bi):
                    bp, b2 = bi // 2, bi % 2
                    q2 = bi % 2
                    KTb = KTb2[q2]
                    Kb = v3(KV2[bp % 4][:, b2 * 2048:b2 * 2048 + 1024], 2)
                    Vb = v3(KV2[bp % 4][:, b2 * 2048 + 1024:b2 * 2048 + 2048], 2)
                    tkc = kv_tok[bp]
                    b = get_bank()
                    pb = banks[b].bitcast(BF16)
                    tt = None
                    for mt in range(2):
                        for h in range(4):
                            c = mt * 4 + h
                            tt = S.op('pe', lambda E, mt=mt, h=h, c=c, pb=pb, Kb=Kb: E.transpose(out=pb[:, c * 128:(c + 1) * 128],
                                                                                                in_=Kb[:, mt, h * 128:(h + 1) * 128], identity=identb),
                                      [tkc, t_ib, bank_free[b]] if c == 0 else [], sig=(c == 7))
                    kvbuf_free[bp % 4].append(tt)
                    tkt = S.op('dve', lambda E, pb=pb, KTb=KTb: E.tensor_copy(out=KTb, in_=v3(pb, 8)), [tt] + kt_free[q2])
                    bank_free[b] = tkt
                    return tkt, tkc, Vb

                def sa_stage2(bi, tkt, tvc, Vb):
                    q2 = bi % 2
                    KTb, Eb = KTb2[q2], Eb2[q2]
                    bs = get_bank()
                    tm = None
                    for h in range(4):
                        for mt in range(2):
                            c = h * 2 + mt
                            tm = S.op('pe', lambda E, h=h, mt=mt, c=c, bs=bs, bi=bi, KTb=KTb: E.matmul(
                                banks[bs][:, c * 8:(c + 1) * 8], lhsT=KTb[:, mt * 4 + h, :], rhs=QT[:, h, bi * 8:(bi + 1) * 8], start=True, stop=True),
                                [tkt, bank_free[bs]] + q_w if c == 0 else [], sig=(c == 7))
                    kt_free[q2] = [tm]
                    te = S.op('act', lambda E, bs=bs, Eb=Eb: E.activation(out=Eb, in_=banks[bs][:, 0:64], func=AF.Exp), [tm] + eb_free[q2])
                    bank_free[bs] = te
                    tmo = tmz = None
                    for h in range(4):
                        for mt in range(2):
                            c = h * 2 + mt
                            col = bi * 32 + h * 8
                            tmo = S.op('pe', lambda E, h=h, mt=mt, c=c, col=col, Vb=Vb, Eb=Eb: E.matmul(
                                banks[BO][:, col:col + 8], lhsT=Vb[:, mt, h * 128:(h + 1) * 128], rhs=Eb[:, c * 8:(c + 1) * 8],
                                start=(mt == 0), stop=(mt == 1)), [te, tvc, bank_free[BO]] if c == 0 else [], sig=(c == 7))
                    for h in range(4):
                        for mt in range(2):
                            c = h * 2 + mt
                            col = bi * 32 + h * 8
                            tmz = S.op('pe', lambda E, h=h, mt=mt, c=c, col=col, Eb=Eb: E.matmul(
                                banks[BZ][:, col:col + 8], lhsT=onesb, rhs=Eb[:, c * 8:(c + 1) * 8],
                                start=(mt == 0), stop=(mt == 1)), [te, t_ob, bank_free[BZ]] if c == 0 else [], sig=(c == 7))
                    kvbuf_free[(bi // 2) % 4].append(tmo)
                    eb_free[q2] = [tmz]
                    return tmo, tmz

                for bp_ in range(4):
                    sa_load(bp_)
                nxt = sa_stage1(0)
                tmo = tmz = None
                for bi in range(16):
                    cur = nxt
                    if bi + 1 < 16:
                        nxt = sa_stage1(bi + 1)
                    tmo, tmz = sa_stage2(bi, cur[0], cur[1], cur[2])
                    if bi % 2 == 1 and bi // 2 + 4 < 8:
                        sa_load(bi // 2 + 4)
                t1 = S.op('dve', lambda E: E.reciprocal(out=rz[:, 0:512], in_=banks[BZ][:, 0:512]), [tmz])
                bank_free[BZ] = t1
                t2 = S.op('dve', lambda E: E.tensor_tensor(out=otmp[:, 0:512], in0=banks[BO][:, 0:512], in1=rz[:, 0:512], op=ALU.mult), [t1, tmo])
                bank_free[BO] = t2
                t3 = S.op('dve', lambda E: E.tensor_tensor(out=brb[:, 8:12, 0:128].rearrange("p h (b s) -> p b h s", s=8),
                                                           in0=otmp[:, 0:512].rearrange("p (b h s) -> p b h s", h=4, s=8),
                                                           in1=gateB[:, :, 0:128].rearrange("p h (b s) -> p b h s", s=8), op=ALU.mult),
                          [t2] + gate_w + br_free)
                att_last = t3
                for h in range(4):
                    gateB_free[h] = [att_last]
            qt_free[:] = [att_last]
            dbg_stop('SB3', samp, l)
            mark(f"p{p}l{l}:merge")
            btm = phase_barrier()
            t_gp = S.dma('sp', gpost, gpost_d[:, l * 1024:(l + 1) * 1024], "ldg", deps=btm)
            br_rd = []
            mg_wr = []
            for dq in range(4):
                Wm = [wget() for _ in range(3)]
                Wb = [wget() for _ in range(3)]
                tlast = None
                for jj in range(2):
                    j = 2 * dq + jj
                    tsg = []
                    for n in range(3):
                        b, tm = proj(Wm[n][1], jj, [Wm[n][2]])
                        ts = S.op('act', lambda E, N=N, b=b, n=n: E.activation(out=sig3[n][:, 0:N], in_=banks[b][:, 0:N], func=AF.Sigmoid),
                                  [tm, sig_free[n]])
                        bank_free[b] = ts
                        tsg.append(ts)
                    bb = []
                    for n in range(3):
                        b = get_bank()
                        tm = mm_group(banks[b][:, 0:N], [(Wb[n][1][:, wk, jj * 128:(jj + 1) * 128], brb[:, n * 4 + wk, 0:N]) for wk in range(4)],
                                      [Wb[n][2], t_bconv, t_a_last, att_last], b)
                        bb.append((b, tm))
                        br_rd.append(tm)
                        tlast = tm
                    t1 = S.op('dve', lambda E, N=N, b=bb[0][0]: E.tensor_tensor(out=macc[:, 0:N], in0=banks[b][:, 0:N], in1=sig3[0][:, 0:N], op=ALU.mult),
                              [bb[0][1], tsg[0], mg_wr[-1] if mg_wr else None])
                    bank_free[bb[0][0]] = t1
                    sig_free[0] = t1
                    t2 = S.op('dve', lambda E, N=N, b=bb[1][0]: E.tensor_tensor(out=ttmp[:, 0:N], in0=banks[b][:, 0:N], in1=sig3[1][:, 0:N], op=ALU.mult),
                              [bb[1][1], tsg[1]])
                    bank_free[bb[1][0]] = t2
                    sig_free[1] = t2
                    t3 = S.op('dve', lambda E, N=N: E.tensor_tensor(out=macc[:, 0:N], in0=macc[:, 0:N], in1=ttmp[:, 0:N], op=ALU.add), [t1, t2])
                    t4 = S.op('dve', lambda E, N=N, b=bb[2][0]: E.tensor_tensor(out=ttmp[:, 0:N], in0=banks[b][:, 0:N], in1=sig3[2][:, 0:N], op=ALU.mult),
                              [bb[2][1], tsg[2], t3])
                    bank_free[bb[2][0]] = t4
                    sig_free[2] = t4
                    t5 = S.op('dve', lambda E, N=N, j=j: E.tensor_tensor(out=merged[:, j, 0:N], in0=macc[:, 0:N], in1=ttmp[:, 0:N], op=ALU.add),
                              [t3, t4] + mg_free)
                    mg_wr.append(t5)
                for n in range(3):
                    wrel(Wm[n][0], tlast)
                    wrel(Wb[n][0], tlast)
            br_free[:] = [br_rd[-1]]
            set_hT_free([hT_rd[-1], br_rd[-1]])
            mark(f"p{p}l{l}:D")
            bt = phase_barrier()
            Wo = [wget() for _ in range(4)]
            tlast = None
            d_free = [None, None]
            d_state = {}

            def d_stage1(t):
                bh = []
                tl_ = None
                yt = ytmp[t]
                for half in range(2):
                    b = 2 * t + half
                    tm = None
                    for q in range(2):
                        blk = half * 2 + q
                        tm = mm_group(banks[b][:, q * 256:(q + 1) * 256],
                                      [(merged[:, k, t * 128:(t + 1) * 128], Wo[blk][1][:, k, :]) for k in range(8)],
                                      [Wo[blk][2]] + mg_wr, b)
                    bh.append((b, tm))
                    tl_ = tm
                tsq = []
                tys = []
                for half in range(2):
                    tq = S.op('act', lambda E, b=bh[half][0], half=half, t=t: E.activation(out=junkD, in_=banks[b][:, :], func=AF.Square,
                                                                                         accum_out=ssD[:, 2 * t + half:2 * t + half + 1]),
                              [bh[half][1], junk_last[1]])
                    junk_last[1] = tq
                    ty = S.op('dve', lambda E, b=bh[half][0], half=half, yt=yt: E.tensor_tensor(
                        out=yt[:, half * 512:(half + 1) * 512], in0=banks[b][:, :], in1=gpost[:, half * 512:half * 512 + 512], op=ALU.mult),
                        [bh[half][1], tq, t_gp])
                    bank_free[bh[half][0]] = ty
                    tsq.append(tq)
                    tys.append(ty)
                d_state[t] = (tsq, tys)
                return tl_

            def d_stage2(t):
                tsq, tys = d_state.pop(t)
                yt = ytmp[t]
                tsum = S.op('dve', lambda E, t=t: E.tensor_tensor(out=ssD2[:, t:t + 1], in0=ssD[:, 2 * t:2 * t + 1], in1=ssD[:, 2 * t + 1:2 * t + 2], op=ALU.add), tsq)
                tl1 = S.op('act', lambda E, t=t: E.activation(out=rstdD[:, t:t + 1], in_=ssD2[:, t:t + 1], func=AF.Ln, scale=1.0 / 1024, bias=EPS), [tsum])
                tl2 = S.op('act', lambda E, t=t: E.activation(out=rstdD[:, t:t + 1], in_=rstdD[:, t:t + 1], func=AF.Exp, scale=-0.5), [tl1])
                tx = S.op('dve', lambda E, t=t, yt=yt: E.scalar_tensor_tensor(out=Xg[:, t, :], in0=yt, scalar=rstdD[:, t:t + 1], in1=Xg[:, t, :],
                                                                             op0=ALU.mult, op1=ALU.add), [tl2] + tys)
                d_free[t % 2] = tx
                x_ready[t] = tx
                if l == 1:
                    st = S.dma('sp', y_d[p * 4 + t], Xg[:, t, :], f"st_y{t}", [tx])
                    store_toks.append(st)
                    xg_free[t] = [st]
                    if nxt_pass is not None:
                        for tp_ in ([t - 1] if t >= 1 else []) + ([t] if t == nt - 1 else []):
                            if tp_ < nxt_pass[1] // 128:
                                x_pref[tp_] = S.dma('sp', Xg[:, tp_, :], x_d[nxt_pass[0] * 4 + tp_], f"ldx{tp_}", deps=xg_free[tp_])

            tlast = None
            for t in range(nt):
                tlast = d_stage1(t)
            fronts = {}
            th_next = []
            d_stage2(0)
            for t in range(nt):
                if t + 1 < nt:
                    d_stage2(t + 1)
                if l == 0:
                    fronts[t] = a_front(t, Xg[:, t, :], x_ready[t], gpre[:, (l + 1) * 1024:(l + 2) * 1024], t_gpre, bank=2 * t)
                    if t >= 1:
                        th_next.append(a_back(t - 1, fronts.pop(t - 1), hT, hT_free_list()))
            if l == 0:
                th_next.append(a_back(nt - 1, fronts.pop(nt - 1), hT, hT_free_list()))
                fused_th[0] = th_next
            for q in range(4):
                wrel(Wo[q][0], tlast)
            mg_free[:] = [tlast]
            if (DEBUG == 'L0' and l == 0 and p == 0) or (DEBUG == 'L1' and l == 1 and p == 0) or (DEBUG == 'S0' and l == 0 and samp) or (DEBUG == 'S1' and l == 1 and samp):
                bt = barrier()
                ds = [S.dma('sp', dbg_b, brb.rearrange("p a b -> p (a b)"), "dbg1", bt),
                      S.dma('sp', dbg_m, merged.rearrange("p a b -> p (a b)"), "dbg2", bt),
                      S.dma('sp', dbg_x, Xg.rearrange("p a b -> p (a b)"), "dbg3", bt),
                      S.dma('sp', dbg_q, QT.rearrange("p a b -> p (a b)"), "dbg4", bt),
                      S.dma('sp', dbg_g, gateB.rearrange("p a b -> p (a b)"), "dbg5", bt),
                      S.dma('sp', dbg_h, hT.rearrange("p a b -> p (a b)"), "dbg6", bt)]
                S.wait('sp', ds)
                S.emit(nc)
                es.close()
                return nc

      except _Stop:
        return nc
    mark("end")
    _mx = {}
    for (k_, v_) in store_toks:
        _mx[k_] = max(_mx.get(k_, 0), v_)
    S.wait('sp', list(_mx.items()))
    S.emit(nc)
    es.close()
    return nc


sg_free = [None, None]
gateA_free = [[], [], [], []]
gateB_free = [[], [], [], []]
mix_free4 = [None, None, None, None]
diag_free = [[], []]
cvg_free = []
stat_free = []
br_free = []
mix_free = [None, None]
qt_free = []
e_free = [None, None]
kv_free = {'k': [], 'v': [], 'kb': [], 'vb': [], 'kt': [], 'eb': []}
sig_free = [None, None, None]
mg_free = []
dphase_last = [None]
_hT_free = [[]]
_state_free = [[]]
_tokbuf = {'b': [], 'c': []}


def hT_free_list():
    return _hT_free[0]


def set_hT_free(v):
    _hT_free[0] = list(v)


def state_free():
    return _state_free[0]


def set_state_free(v):
    _state_free[0] = list(v)


def tokbuf_free(k):
    return _tokbuf[k]


def set_tokbuf_free(k, v):
    _tokbuf[k] = list(v)


def _reset_state():
    global sg_free, diag_free, cvg_free, stat_free, br_free, mix_free, qt_free, e_free, kv_free, sig_free, mg_free
    sg_free[:] = [None, None]
    for i in range(4):
        gateA_free[i] = []
        gateB_free[i] = []
        mix_free4[i] = None
    diag_free[0] = []
    diag_free[1] = []
    cvg_free[:] = []
    stat_free[:] = []
    br_free[:] = []
    mix_free[:] = [None, None]
    qt_free[:] = []
    e_free[:] = [None, None]
    for k in kv_free:
        kv_free[k] = []
    sig_free[:] = [None, None, None]
    mg_free[:] = []
    dphase_last[0] = None
    _hT_free[0] = []
    _state_free[0] = []
    _tokbuf['b'] = []
    _tokbuf['c'] = []


_NC_CACHE = {}


def _prep_shared(inp):
    f = np.float32
    w_in = np.asarray(inp["w_in"], f)
    win_l = np.ascontiguousarray(w_in.reshape(2, 8, 128, 26, 256).transpose(0, 3, 2, 1, 4)).reshape(2, 26, 128, 2048)
    w_br = np.asarray(inp["w_branch"], f)
    wbr_l = np.ascontiguousarray(w_br.reshape(2, 3, 4, 128, 4, 256).transpose(0, 1, 4, 3, 2, 5)).reshape(2, 3, 4, 128, 1024)
    w_out = np.asarray(inp["w_out"], f)
    wout_l = np.ascontiguousarray(w_out.reshape(2, 8, 128, 4, 256).transpose(0, 3, 2, 1, 4)).reshape(2, 4, 128, 2048)
    w_kv = np.asarray(inp["w_mem_kv"], f)
    wkv_l = np.ascontiguousarray(w_kv.reshape(2, 8, 128, 4, 256).transpose(0, 3, 2, 1, 4)).reshape(2, 4, 128, 2048)
    pool_w = np.asarray(inp["pool_w"], f)
    poolw_l = np.ascontiguousarray(pool_w.transpose(0, 2, 1, 3)).reshape(2, 128, 512)
    par = np.zeros((128, NPAR), f)
    par[:, P_GPRE:P_GPRE + 16] = np.asarray(inp["norm_pre"], f).reshape(2, 8, 128).transpose(2, 0, 1).reshape(128, 16)
    par[:, P_GMEM:P_GMEM + 16] = np.asarray(inp["mem_norm"], f).reshape(2, 8, 128).transpose(2, 0, 1).reshape(128, 16)
    par[:, P_PSC:P_PSC + 8] = np.asarray(inp["pool_scale"], f).reshape(2, 4, 128).transpose(2, 0, 1).reshape(128, 8)
    par[:, P_CB:P_CB + 8] = np.asarray(inp["conv_b"], f).reshape(2, 4, 128).transpose(2, 0, 1).reshape(128, 8)
    par[:, P_LNG:P_LNG + 8] = np.asarray(inp["conv_ln_g"], f).reshape(2, 4, 128).transpose(2, 0, 1).reshape(128, 8)
    par[:, P_LNB:P_LNB + 8] = np.asarray(inp["conv_ln_b"], f).reshape(2, 4, 128).transpose(2, 0, 1).reshape(128, 8)
    par[:, P_CW:P_CW + 248] = np.asarray(inp["conv_w"], f).reshape(2, 31, 4, 128).transpose(3, 0, 2, 1).reshape(128, 248)
    icnt = np.zeros((4, 16), f)
    for j, w in enumerate(WINS):
        for t in range(16):
            icnt[j, t] = 1.0 / min(t + 1, w)
    par[:, P_ICNT:P_ICNT + 64] = icnt.reshape(1, 64)
    gpost = np.ascontiguousarray(np.broadcast_to(np.asarray(inp["norm_post"], f).reshape(1, 2048), (128, 2048)))
    gpre_b = np.ascontiguousarray(np.broadcast_to(np.asarray(inp["norm_pre"], f).reshape(1, 2048), (128, 2048)))
    gmem_b = np.ascontiguousarray(np.broadcast_to(np.asarray(inp["mem_norm"], f).reshape(1, 2048), (128, 2048)))
    return {"w_in": win_l, "w_br": wbr_l, "w_out": wout_l, "w_kv": wkv_l, "pool_w": poolw_l, "params": par,
            "gpost": gpost, "gpre": gpre_b, "gmem": gmem_b, "ident": np.eye(128, dtype=f)}


def kernel(**inp):
    f = np.float32
    _reset_state()
    nc = build_program()
    shared = _prep_shared(inp)
    xp = np.asarray(inp["x_prompt"], f)
    xs = np.asarray(inp["x_sample"], f)
    memp = np.asarray(inp["mem_prompt"], f)
    sp = np.asarray(inp["state_pool"], f)
    sc = np.asarray(inp["state_conv"], f)
    ck = np.asarray(inp["cache_mem_k"], f)
    cv = np.asarray(inp["cache_mem_v"], f)
    in_maps = []
    for c in range(NCORES):
        b0 = 16 * c
        x = np.concatenate([xp[c].reshape(16, 128, 1024), xs[b0:b0 + 16].reshape(1, 128, 1024)], axis=0)
        m = dict(shared)
        m["x"] = np.ascontiguousarray(x)
        m["mem"] = np.ascontiguousarray(memp[c].reshape(2, 128, 1024))
        m["spool"] = np.ascontiguousarray(sp[:, b0:b0 + 16].reshape(2, 2, 120, 512))
        m["sconv"] = np.ascontiguousarray(sc[:, b0:b0 + 16].reshape(2, 4, 120, 512))
        kk = ck[:, b0:b0 + 16].reshape(2, 8, 2, 2, 128, 512)
        vv = cv[:, b0:b0 + 16].reshape(2, 8, 2, 2, 128, 512)
        kvv = np.stack([kk, vv], axis=3)
        m["ckv"] = np.ascontiguousarray(kvv.transpose(0, 1, 5, 2, 3, 4, 6)).reshape(2, 8, 128, 4096)
        in_maps.append(m)
    res = run_bass_kernel_spmd(nc, in_maps, core_ids=list(range(NCORES)))
    R = res.results
    y_p = np.stack([R[c]["y"][:16].reshape(2048, 1024) for c in range(NCORES)], 0)
    y_s = np.concatenate([R[c]["y"][16].reshape(16, 8, 1024) for c in range(NCORES)], 0)
    pool_p = np.stack([R[c]["pool_p"] for c in range(NCORES)], 1)
    conv_p = np.stack([R[c]["conv_p"] for c in range(NCORES)], 1)
    k_p = np.stack([R[c]["k_p"].reshape(2, 256, 4, 128) for c in range(NCORES)], 1)
    v_p = np.stack([R[c]["v_p"].reshape(2, 256, 4, 128) for c in range(NCORES)], 1)
    pool_s = np.concatenate([np.concatenate([R[c]["pool_s_old"], R[c]["pool_s_new"].reshape(2, 16, 8, 512)], axis=2) for c in range(NCORES)], 1)
    conv_s = np.concatenate([np.concatenate([R[c]["conv_s_old"], R[c]["conv_s_new"].reshape(2, 16, 8, 512)], axis=2) for c in range(NCORES)], 1)
    return (y_p.astype(f), y_s.astype(f), pool_p.astype(f), conv_p.astype(f), k_p.astype(f), v_p.astype(f),
            pool_s.astype(f), conv_s.astype(f))
```

```python
import math
import numpy as np
from contextlib import ExitStack
import concourse.bass as bass
import concourse.mybir as mybir
from concourse.bass_utils import run_bass_kernel_spmd

F32 = mybir.dt.float32
BF16 = mybir.dt.bfloat16
AF = mybir.ActivationFunctionType
ALU = mybir.AluOpType

NCORES = 8
EPS = 1e-6
WINS = (2, 4, 8, 16)
P_GPRE, P_GMEM, P_PSC, P_CB, P_LNG, P_LNB, P_CW, P_ICNT = 0, 16, 32, 40, 48, 56, 64, 312
NPAR = 376
B_PIN, B_PGATE, B_CVAL, B_CGLU, B_CGATE, B_Q, B_XGATE, B_MERGE = 0, 2, 4, 6, 8, 10, 12, 14


class Sched:
    ENG = ('pe', 'act', 'dve', 'pool', 'sp')

    def __init__(self):
        self.ops = {e: [] for e in self.ENG}
        self.cnt = {e: 0 for e in self.ENG}
        self.dcnt = {}

    def op(self, eng, fn, deps=(), sig=True):
        tok = None
        if sig:
            self.cnt[eng] += 1
            tok = (eng, self.cnt[eng])
        self.ops[eng].append(('op', fn, [d for d in deps if d is not None], sig))
        return tok

    def dma(self, q, out, in_, key, deps=()):
        self.dcnt[key] = self.dcnt.get(key, 0) + 16
        tok = (key, self.dcnt[key])
        self.ops[q].append(('dma', (out, in_, key), [d for d in deps if d is not None], True))
        return tok

    def wait(self, eng, deps):
        self.ops[eng].append(('wait', None, [d for d in deps if d is not None], False))

    def emit(self, nc):
        keys = list(self.ENG) + sorted(self.dcnt.keys())
        with ExitStack() as es:
            sems = {}
            for k in keys:
                sems[k] = es.enter_context(nc.semaphore("s_" + k))
            blk = es.enter_context(nc.Block())
            sched = self

            def run(eng_name):
                def body(E):
                    known = {}
                    for kind, payload, deps, sig in sched.ops[eng_name]:
                        for (k, v) in deps:
                            if known.get(k, 0) < v:
                                E.wait_ge(sems[k], v)
                                known[k] = v
                        if kind == 'op':
                            ins = payload(E)
                            if sig:
                                ins.then_inc(sems[eng_name], 1)
                        elif kind == 'dma':
                            out, in_, key = payload
                            E.dma_start(out=out, in_=in_).then_inc(sems[key], 16)
                return body

            blk.tensor(run('pe'))
            blk.scalar(run('act'))
            blk.vector(run('dve'))
            blk.gpsimd(run('pool'))
            blk.sync(run('sp'))


DEBUG = None
MARKS = []


def build_program():
    nc = bass.Bass("TRN2", target_bir_lowering=False)
    if DEBUG:
        dbg_h = nc.dram_tensor("dbg_h", [128, 4096], BF16, kind="ExternalOutput").ap()
        dbg_r = nc.dram_tensor("dbg_r", [128, 16], F32, kind="ExternalOutput").ap()
        dbg_x = nc.dram_tensor("dbg_x", [128, 4096], F32, kind="ExternalOutput").ap()
        dbg_b = nc.dram_tensor("dbg_b", [128, 12 * 512], BF16, kind="ExternalOutput").ap()
        dbg_m = nc.dram_tensor("dbg_m", [128, 8 * 512], BF16, kind="ExternalOutput").ap()
        dbg_q = nc.dram_tensor("dbg_q", [128, 4 * 512], BF16, kind="ExternalOutput").ap()
        dbg_g = nc.dram_tensor("dbg_g", [128, 4 * 512], BF16, kind="ExternalOutput").ap()

    def din(name, shape):
        return nc.dram_tensor(name, list(shape), F32, kind="ExternalInput").ap()

    def dout(name, shape):
        return nc.dram_tensor(name, list(shape), F32, kind="ExternalOutput").ap()

    x_d = din("x", (17, 128, 1024))
    mem_d = din("mem", (2, 128, 1024))
    spool_d = din("spool", (2, 2, 120, 512))
    sconv_d = din("sconv", (2, 4, 120, 512))
    ckv_d = din("ckv", (2, 8, 128, 4096))
    win_d = din("w_in", (2, 26, 128, 2048))
    wbr_d = din("w_br", (2, 3, 4, 128, 1024))
    wout_d = din("w_out", (2, 4, 128, 2048))
    wkv_d = din("w_kv", (2, 4, 128, 2048))
    poolw_d = din("pool_w", (2, 128, 512))
    par_d = din("params", (128, NPAR))
    gpost_d = din("gpost", (128, 2048))
    gpre_d = din("gpre", (128, 2048))
    gmem_d = din("gmem", (128, 2048))
    ident_d = din("ident", (128, 128))

    wc_in = nc.dram_tensor("wc_in", [2, 26, 128, 2048], BF16).ap()
    wc_br = nc.dram_tensor("wc_br", [2, 3, 4, 128, 1024], BF16).ap()
    wc_out = nc.dram_tensor("wc_out", [2, 4, 128, 2048], BF16).ap()
    diag_c = nc.dram_tensor("diag_c", [2, 4, 128, 31 * 128], BF16).ap()

    y_d = dout("y", (17, 128, 1024))
    poolp_d = dout("pool_p", (2, 15, 512))
    convp_d = dout("conv_p", (2, 30, 512))
    kp_d = dout("k_p", (2, 256, 512))
    vp_d = dout("v_p", (2, 256, 512))
    pools_new_d = dout("pool_s_new", (2, 128, 512))
    convs_new_d = dout("conv_s_new", (2, 128, 512))
    pools_old_d = dout("pool_s_old", (2, 16, 7, 512))
    convs_old_d = dout("conv_s_old", (2, 16, 22, 512))

    S = Sched()
    es = ExitStack()
    NW = 53200
    A = es.enter_context(nc.sbuf_tensor("arena", [128, NW], F32))
    banks = [es.enter_context(nc.psum_tensor(f"ps{i}", [128, 512], F32)) for i in range(8)]
    off = [0]

    def carve(n, dt=F32):
        a = A[:, off[0]:off[0] + n]
        off[0] += n
        assert off[0] <= NW, off[0]
        if dt == BF16:
            a = a.bitcast(BF16)
        return a

    def v3(ap, a):
        return ap.rearrange("p (a b) -> p a b", a=a)

    par = carve(NPAR)
    identf = carve(128)
    onesf = carve(128)
    identb = carve(64, BF16)
    onesb = carve(64, BF16)
    poolwf = carve(512)
    poolwb = [carve(256, BF16) for _ in range(2)]
    KTp = [v3(carve(512, BF16), 4) for _ in range(2)]
    Vp = [v3(carve(512, BF16), 2) for _ in range(2)]
    ext0 = off[0]
    pext = [v3(carve(4 * 527), 4) for _ in range(2)]
    uext = [v3(carve(4 * 272, BF16), 4) for _ in range(2)]
    ext1 = off[0]
    off[0] = ext0
    psext = carve(4 * 16 * 23).rearrange("p (j b s) -> p j b s", j=4, b=16)
    usext = carve(4 * 16 * 19, BF16).rearrange("p (j b s) -> p j b s", j=4, b=16)
    off[0] = ext1
    Xg = v3(carve(4096), 4)
    hT = v3(carve(2048, BF16), 8)
    gateA = v3(carve(1024, BF16), 4)
    gateB = v3(carve(1024, BF16), 4)
    QT = v3(carve(1024, BF16), 4)
    brb = v3(carve(3072, BF16), 12)
    merged = v3(carve(2048, BF16), 8)
    tok_b = carve(512)
    tok_c = carve(512)
    tok_a = tok_b
    ssA = carve(8)
    rstdA = carve(8)
    ssD = carve(8)
    ssD2 = carve(8)
    rstdD = carve(8)
    NSLOT = 11
    slots = [carve(1024, BF16) for _ in range(NSLOT)]
    gpre = carve(2048)
    sttile = carve(512)
    ptmp = [carve(527) for _ in range(2)]
    mix = [carve(256, BF16) for _ in range(4)]
    tmpP = carve(16)
    T0 = off[0]
    tmax = [T0]

    def treset():
        off[0] = T0

    def tcarve(n, dt=F32):
        a = carve(n, dt)
        tmax[0] = max(tmax[0], off[0])
        return a

    def barrier():
        return [(e, S.cnt[e]) for e in ('pe', 'act', 'dve', 'pool') if S.cnt[e] > 0]

    treset()
    xnb = [tcarve(512, BF16) for _ in range(2)]
    junkA = tcarve(512, BF16)
    ytmp = [tcarve(1024) for _ in range(4)]
    junkD = tcarve(256, BF16)
    gpost = tcarve(1024)
    mhT = v3(tcarve(1024, BF16), 8)
    gmem = tcarve(1024)
    treset()
    sgj = [tcarve(512) for _ in range(2)]
    diag_flat = [tcarve(31 * 64, BF16) for _ in range(2)]
    diag = [v3(d_, 31) for d_ in diag_flat]
    cvg = v3(tcarve(2048), 4)
    sqg = v3(tcarve(2048), 4)
    mean_sb = tcarve(512)
    var_sb = tcarve(512)
    rstd_sb = tcarve(512)
    tmpA = tcarve(512)
    tmpB = tcarve(512)
    treset()
    Ebuf = [v3(tcarve(512, BF16), 2) for _ in range(2)]
    rz = tcarve(512)
    otmp = tcarve(512)
    KV2 = [tcarve(2048, BF16) for _ in range(4)]
    KTb2 = [v3(tcarve(512, BF16), 8) for _ in range(2)]
    Eb2 = [tcarve(32, BF16) for _ in range(2)]
    treset()
    sig3 = [tcarve(512) for _ in range(3)]
    macc = tcarve(512)
    ttmp = tcarve(512)
    off[0] = tmax[0]
    print("SBUF words used", off[0])

    bank_free = [None] * 8
    bank_next = [0]

    def get_bank():
        b = bank_next[0]
        bank_next[0] = (b + 1) % 6
        return b

    wlist = []
    PASSES = [(0, 512, False), (1, 512, False), (2, 512, False), (3, 512, False), (4, 128, True)]
    if DEBUG and DEBUG.startswith('P'):
        PASSES = PASSES[:int(DEBUG[1:])]

    def layer_blocks(l):
        bl = []
        g0 = P_GPRE + l * 8

        def wi(blk):
            return (win_d[l, blk], 8, g0, wc_in[l, blk])
        for jb in range(2):
            bl.append(wi(B_PGATE + jb))
        for jb in range(2):
            bl.append(wi(B_PIN + jb))
        for jb in range(2):
            bl.append(wi(B_CGLU + jb))
            bl.append(wi(B_CVAL + jb))
        for jb in range(2):
            bl.append(wi(B_CGATE + jb))
        for jb in range(2):
            bl.append(wi(B_Q + jb))
        for jb in range(2):
            bl.append(wi(B_XGATE + jb))
        for dq in range(4):
            for n in range(3):
                bl.append(wi(B_MERGE + n * 4 + dq))
            for n in range(3):
                bl.append((wbr_d[l, n, dq], 4, None, wc_br[l, n, dq]))
        for b4 in range(4):
            bl.append((wout_d[l, b4], 8, None, wc_out[l, b4]))
        return bl

    for l in range(2):
        for b4 in range(4):
            wlist.append((wkv_d[l, b4], 8, P_GMEM + l * 8, None, 'cast'))
    for _pi, _p in enumerate(PASSES):
        for l in range(2):
            for (a_, nk_, sc_, c_) in layer_blocks(l):
                wlist.append((a_, nk_, sc_, c_, 'cast+store' if (_pi == 0 and len(PASSES) > 1) else ('cached' if _pi > 0 else 'cast')))
    NB = len(wlist)
    w_issued = [0]
    w_ready = [None] * NB
    slot_last = [None] * NSLOT
    slot_owner = [None] * NSLOT
    t_par = [None]
    LA = 5

    slot_store = [None] * NSLOT
    cache_tok = {}

    def w_issue(i):
        ap, nk, sc, cap, mode = wlist[i]
        sl = i % NSLOT
        if slot_owner[sl] is not None:
            assert slot_last[sl] is not None, ("slot not released", i, slot_owner[sl])
        deps = list(slot_last[sl] or []) + [slot_store[sl]]
        dst = slots[sl][:, 0:nk * 256]
        if mode == 'cached':
            td = S.dma('pool', dst, cap, f"wsl{sl}", deps=deps + [cache_tok[str(cap)]])
        else:
            td = S.dma('pool', dst, ap, f"wsl{sl}", deps=deps)
            if mode == 'cast+store':
                ts = S.dma('sp', cap, dst, f"wcs{sl}", deps=[td])
                slot_store[sl] = ts
                cache_tok[str(cap)] = ts
        slot_owner[sl] = i
        slot_last[sl] = None
        w_ready[i] = td

    w_ptr = [0]

    def wget():
        i = w_ptr[0]
        w_ptr[0] += 1
        while w_issued[0] < min(NB, i + 1 + LA):
            w_issue(w_issued[0])
            w_issued[0] += 1
        sl = i % NSLOT
        nk = wlist[i][1]
        return i, v3(slots[sl][:, 0:nk * 256], nk), w_ready[i]

    def wrel(i, tok):
        slot_last[i % NSLOT] = [tok]

    def mm_group(out_ap, pairs, deps, bank):
        n = len(pairs)
        t = None
        for i, (lt, rh) in enumerate(pairs):
            d = (list(deps) + [bank_free[bank]]) if i == 0 else []
            t = S.op('pe', lambda E, lt=lt, rh=rh, i=i, n=n: E.matmul(out_ap, lhsT=lt, rhs=rh, start=(i == 0), stop=(i == n - 1)),
                     d, sig=(i == n - 1))
        return t

    store_toks = []

    t_par[0] = S.dma('sp', par, par_d, "ldp0")
    t_id = S.dma('sp', identf, ident_d, "ldp2")
    t_ib = S.op('dve', lambda E: E.tensor_copy(out=identb, in_=identf), [t_id])
    t_of = S.op('dve', lambda E: E.memset(onesf, 1.0 / 512.0))
    t_ob = S.op('dve', lambda E: E.memset(onesb, 1.0))
    t_z = None
    for l in range(2):
        S.op('dve', lambda E, l=l: E.memset(pext[l][:, :, 0:15], 0.0), sig=False)
        t_z = S.op('dve', lambda E, l=l: E.memset(uext[l][:, :, 0:30], 0.0))
    t_pw = []
    for l in range(2):
        td = S.dma('sp', poolwf, poolw_d[l], "ldpw", deps=[t_pw[-1]] if t_pw else [])
        t_pw.append(S.op('dve', lambda E, l=l: E.tensor_copy(out=poolwb[l], in_=poolwf), [td]))

    junk_last = [None, None]

    def a_front(t, src_tile, src_rdy, g_ap, g_ready, bank=None):
        tsq_ = S.op('act', lambda E, t=t: E.activation(out=junkA, in_=src_tile, func=AF.Square, accum_out=ssA[:, t:t + 1]),
                    [src_rdy, junk_last[0]])
        junk_last[0] = tsq_
        t_ln = S.op('act', lambda E, t=t: E.activation(out=rstdA[:, t:t + 1], in_=ssA[:, t:t + 1], func=AF.Ln, scale=1.0 / 1024, bias=EPS), [tsq_])
        t_ex = S.op('act', lambda E, t=t: E.activation(out=rstdA[:, t:t + 1], in_=rstdA[:, t:t + 1], func=AF.Exp, scale=-0.5), [t_ln])
        xb = xnb[t % 2]
        t_xn = S.op('dve', lambda E, t=t, xb=xb: E.scalar_tensor_tensor(out=xb, in0=src_tile, scalar=rstdA[:, t:t + 1], in1=g_ap,
                                                                         op0=ALU.mult, op1=ALU.mult),
                    [t_ex, rms_transpose.xb_free[t % 2], g_ready])
        b = get_bank() if bank is None else bank
        pb = banks[b].bitcast(BF16)
        tt = None
        for k in range(8):
            tt = S.op('pe', lambda E, k=k, xb=xb, pb=pb: E.transpose(out=pb[:, k * 128:(k + 1) * 128], in_=xb[:, k * 128:(k + 1) * 128], identity=identb),
                      [t_xn, t_ib, bank_free[b]] if k == 0 else [], sig=(k == 7))
        rms_transpose.xb_free[t % 2] = tt
        return b, pb, tt

    def a_back(t, front, dst, dst_free):
        b, pb, tt = front
        te = S.op('act', lambda E, t=t, pb=pb: E.activation(out=dst[:, :, t * 128:(t + 1) * 128], in_=v3(pb, 8), func=AF.Copy),
                  [tt] + list(dst_free))
        bank_free[b] = te
        return te

    def rms_transpose(src_tiles, nt, dst, src_ready, dst_free, g_ap, g_ready):
        outs = []
        fr = {0: a_front(0, src_tiles[0], src_ready[0], g_ap, g_ready)}
        for t in range(nt):
            if t + 1 < nt:
                fr[t + 1] = a_front(t + 1, src_tiles[t + 1], src_ready[t + 1], g_ap, g_ready)
            outs.append(a_back(t, fr.pop(t), dst, dst_free))
        return outs
    rms_transpose.xb_free = [None, None]
    rms_transpose.sq_free = [None, None]

    t_m = [S.dma('sp', ytmp[t], mem_d[t], f"ldm{t}") for t in range(2)]
    x_pref = {}
    for t in range(PASSES[0][1] // 128):
        x_pref[t] = S.dma('sp', Xg[:, t, :], x_d[PASSES[0][0] * 4 + t], f"ldx{t}")
    t_gpre = S.dma('sp', gpre, gpre_d, "ldp3")
    tokbufs = [tok_b, tok_c]
    tok_st = [None, None]
    tok_i = [0]
    t_mh = []
    mh_rd = []
    for l in range(2):
        t_gm = S.dma('sp', gmem, gmem_d[:, l * 1024:(l + 1) * 1024], "ldgm", deps=t_mh)
        t_mh = rms_transpose([ytmp[t] for t in range(2)], 2, mhT, t_m, mh_rd, gmem, t_gm)
        wk = [wget() for _ in range(2)]
        for h in range(4):
            i, W, tr = wk[h // 2]
            b = get_bank()
            tm = mm_group(banks[b][:, 0:256], [(W[:, k, (h % 2) * 128:(h % 2) * 128 + 128], mhT[:, k, :]) for k in range(8)],
                          [tr] + t_mh, b)
            te = S.op('act', lambda E, l=l, h=h, b=b: E.activation(out=KTp[l][:, h, :], in_=banks[b][:, 0:256], func=AF.Copy), [tm])
            bank_free[b] = te
        for mt in range(2):
            b = get_bank()
            tm = None
            for q in range(2):
                i, W, tr = wk[q]
                tm = mm_group(banks[b][:, q * 256:(q + 1) * 256], [(mhT[:, k, mt * 128:(mt + 1) * 128], W[:, k, :]) for k in range(8)],
                              [tr] + t_mh, b)
            tkb = tokbufs[tok_i[0] % 2]
            te = S.op('dve', lambda E, b=b, tkb=tkb: E.tensor_copy(out=tkb, in_=banks[b][:, :]), [tm, tok_st[tok_i[0] % 2]])
            bank_free[b] = te
            tok_st[tok_i[0] % 2] = S.dma('sp', kp_d[l, mt * 128:(mt + 1) * 128, :], tkb, f"st_a{tok_i[0] % 2}", [te])
            store_toks.append(tok_st[tok_i[0] % 2])
            tok_i[0] += 1
        for q in range(2):
            wrel(wk[q][0], tm)
        wv = [wget() for _ in range(2)]
        for mt in range(2):
            b = get_bank()
            tm = None
            for q in range(2):
                i, W, tr = wv[q]
                tm = mm_group(banks[b][:, q * 256:(q + 1) * 256], [(mhT[:, k, mt * 128:(mt + 1) * 128], W[:, k, :]) for k in range(8)],
                              [tr] + t_mh, b)
            tkb = tokbufs[tok_i[0] % 2]
            te = S.op('dve', lambda E, b=b, tkb=tkb: E.tensor_copy(out=tkb, in_=banks[b][:, :]), [tm, tok_st[tok_i[0] % 2]])
            te2 = S.op('act', lambda E, l=l, mt=mt, b=b: E.activation(out=Vp[l][:, mt, :], in_=banks[b][:, :], func=AF.Copy), [tm, te])
            bank_free[b] = te2
            tok_st[tok_i[0] % 2] = S.dma('sp', vp_d[l, mt * 128:(mt + 1) * 128, :], tkb, f"st_a{tok_i[0] % 2}", [te])
            store_toks.append(tok_st[tok_i[0] % 2])
            tok_i[0] += 1
        for q in range(2):
            wrel(wv[q][0], tm)
        mh_rd = [tm]
    xg_free = [[] for _ in range(4)]
    set_tokbuf_free('b', [tok_st[0]])
    set_tokbuf_free('c', [tok_st[1]])
    for l in range(2):
        src = spool_d[l].rearrange("t (b r) c -> (t b) r c", r=15)
        store_toks.append(S.dma('sp', pools_old_d[l], src[:, 8:15, :], "st_o"))
        src = sconv_d[l].rearrange("t (b r) c -> (t b) r c", r=30)
        store_toks.append(S.dma('sp', convs_old_d[l], src[:, 8:30, :], "st_o"))

    diag_ready = {}
    diag_all_st = [[], []]
    first_pass = PASSES[0][0]

    for e_ in ('pe', 'act', 'dve'):
        S.wait(e_, [t_par[0], t_id, t_gpre])

    class _Stop(Exception):
        pass

    def dbg_stop(tag, samp, l):
        if DEBUG == tag and samp and l == 1:
            bt = barrier()
            ds = [S.dma('sp', dbg_b, brb.rearrange("p a b -> p (a b)"), "dbg1", bt),
                  S.dma('sp', dbg_m, merged.rearrange("p a b -> p (a b)"), "dbg2", bt),
                  S.dma('sp', dbg_x, Xg.rearrange("p a b -> p (a b)"), "dbg3", bt),
                  S.dma('sp', dbg_q, QT.rearrange("p a b -> p (a b)"), "dbg4", bt),
                  S.dma('sp', dbg_g, gateB.rearrange("p a b -> p (a b)"), "dbg5", bt),
                  S.dma('sp', dbg_h, hT.rearrange("p a b -> p (a b)"), "dbg6", bt)]
            S.wait('sp', ds)
            S.emit(nc)
            es.close()
            raise _Stop()

    cur_bar = [[]]
    prev_pool_T = [False]

    def mark(label):
        MARKS.append((label, sum(1 for o in S.ops['pe'] if o[0] == 'op')))

    def phase_barrier(pool_too=False, tokens=None):
        engs = ['pe', 'act', 'dve'] + (['pool'] if (pool_too or prev_pool_T[0]) else [])
        bt = [(e, S.cnt[e]) for e in engs if S.cnt[e] > 0]
        if tokens is not None:
            bt = list(tokens)
        bt = bt + diag_all_st[0] + diag_all_st[1]
        S.wait('act', bt)
        S.wait('dve', bt)
        if pool_too:
            S.wait('pool', bt)
        prev_pool_T[0] = pool_too
        cur_bar[0] = bt
        return bt

    carry_tok = {('p', 0): [t_z], ('p', 1): [t_z], ('u', 0): [t_z], ('u', 1): [t_z], ('ps',): [], ('us',): []}
    x_ready = [None] * 4
    fused_th = [None]
    last_tokA = [None]

    for (p, N, samp) in PASSES:
      try:
        nt = N // 128
        for t in range(nt):
            if t in x_pref:
                x_ready[t] = x_pref.pop(t)
            else:
                x_ready[t] = S.dma('sp', Xg[:, t, :], x_d[p * 4 + t], f"ldx{t}", deps=xg_free[t])
        _pi = [q[0] for q in PASSES].index(p)
        nxt_pass = PASSES[_pi + 1] if _pi + 1 < len(PASSES) else None
        if samp:
            bt0 = barrier()
            carry_tok[('ps',)] = list(bt0)
            carry_tok[('us',)] = list(bt0)
        for l in range(2):
            g0 = P_GPRE + l * 8
            mark(f"p{p}l{l}:A")
            phase_barrier()
            if fused_th[0] is not None:
                t_h = fused_th[0]
                fused_th[0] = None
            else:
                t_h = rms_transpose([Xg[:, t, :] for t in range(nt)], nt, hT, x_ready, hT_free_list(), gpre[:, l * 1024:(l + 1) * 1024], t_gpre)
            hT_rd = []
            bt_afterA = [(e, S.cnt[e]) for e in ('pe', 'act', 'dve') if S.cnt[e] > 0]
            if DEBUG == 'A' or (DEBUG == 'SA' and samp and l == 1):
                d1 = S.dma('sp', dbg_h, hT.rearrange("p a b -> p (a b)"), "dbg1", t_h)
                d2 = S.dma('sp', dbg_r[:, 0:8], ssA, "dbg2", t_h)
                d3 = S.dma('sp', dbg_r[:, 8:16], rstdA, "dbg3", t_h)
                d4 = S.dma('sp', dbg_x, Xg.rearrange("p a b -> p (a b)"), "dbg4", t_h)
                S.wait('sp', [d1, d2, d3, d4])
                S.emit(nc)
                es.close()
                return nc

            def proj(Wv, jj, deps):
                b = get_bank()
                tm = mm_group(banks[b][:, 0:N], [(Wv[:, k, jj * 128:(jj + 1) * 128], hT[:, k, 0:N]) for k in range(8)],
                              list(deps) + t_h, b)
                hT_rd.append(tm)
                return b, tm

            tokmaj = samp or p == 3
            tl = nt - 1

            if samp:
                for tb in range(2):
                    td = S.dma('sp', sttile[0:120, :], spool_d[l, tb], "ldst", deps=state_free() + cur_bar[0])
                    b = get_bank()
                    tt = None
                    for j in range(4):
                        tt = S.op('pe', lambda E, j=j, b=b: E.transpose(out=banks[b][:, j * 120:(j + 1) * 120], in_=sttile[0:120, j * 128:(j + 1) * 128],
                                                                       identity=identf[0:120, 0:120]),
                                  [td, t_id, bank_free[b]] if j == 0 else [], sig=(j == 3))
                    set_state_free([tt])
                    te = None
                    for j in range(4):
                        te = S.op('dve', lambda E, j=j, b=b, tb=tb: E.tensor_copy(
                            out=psext[:, j, tb * 8:(tb + 1) * 8, 0:15],
                            in_=banks[b][:, j * 120:(j + 1) * 120].rearrange("p (b r) -> p b r", r=15)),
                            [tt] + carry_tok[('ps',)] if j == 0 else [])
                    bank_free[b] = te
                    sample_hist_p = te
                for tb in range(4):
                    td = S.dma('sp', sttile[0:120, :], sconv_d[l, tb], "ldst", deps=state_free() + cur_bar[0])
                    b = get_bank()
                    tt = None
                    for j in range(4):
                        tt = S.op('pe', lambda E, j=j, b=b: E.transpose(out=banks[b][:, j * 120:(j + 1) * 120], in_=sttile[0:120, j * 128:(j + 1) * 128],
                                                                       identity=identf[0:120, 0:120]),
                                  [td, t_id, bank_free[b]] if j == 0 else [], sig=(j == 3))
                    set_state_free([tt])
                    te = None
                    for j in range(4):
                        te = S.op('dve', lambda E, j=j, b=b, tb=tb: E.tensor_copy(
                            out=usext[:, j, tb * 4:(tb + 1) * 4, 0:30],
                            in_=banks[b][:, j * 120:(j + 1) * 120].rearrange("p (b r) -> p b r", r=30)),
                            [tt] + carry_tok[('us',)] if j == 0 else [])
                    bank_free[b] = te
                    sample_hist_u = te

            dbg_stop('SB1', samp, l)
            mark(f"p{p}l{l}:pool")
            gate_w = []
            for jb in range(2):
                ig, Wg, trg = wget()
                tm = None
                for jj in range(2):
                    j = 2 * jb + jj
                    b, tm = proj(Wg, jj, [trg])
                    tg = S.op('act', lambda E, N=N, b=b, j=j: E.activation(out=gateB[:, j, 0:N], in_=banks[b][:, 0:N], func=AF.Silu),
                              [tm] + gateB_free[j])
                    bank_free[b] = tg
                    gate_w.append(tg)
                wrel(ig, tm)
            tokb_p = 6 if tokmaj else None
            tm_tp = None
            t_a_last = None
            pool_rd = []
            pool_pending = []
            pend = [None]
            pend2 = [None]

            def pool_mm():
                if pend[0] is None:
                    return
                j_, mx_, tmx_ = pend[0]
                pend[0] = None
                b2 = get_bank()
                tm2 = mm_group(banks[b2][:, 0:N], [(poolwb[l][:, j_ * 128:(j_ + 1) * 128], mx_[:, 0:N])], [tmx_, t_pw[l]], b2)
                mix_free4[j_] = tm2
                pend2[0] = (j_, b2, tm2)

            def pool_ep():
                if pend2[0] is None:
                    return None
                j_, b2, tm2 = pend2[0]
                pend2[0] = None
                ta = S.op('dve', lambda E, N=N, b2=b2, j=j_, l=l: E.scalar_tensor_tensor(
                    out=brb[:, j, 0:N], in0=banks[b2][:, 0:N], scalar=par[:, P_PSC + l * 4 + j:P_PSC + l * 4 + j + 1],
                    in1=gateB[:, j, 0:N], op0=ALU.mult, op1=ALU.mult), [tm2, gate_w[j_]] + br_free)
                bank_free[b2] = ta
                gateB_free[j_] = [ta]
                return ta

            for jb in range(2):
                ip, Wp, trp = wget()
                tlast = None
                for jj in range(2):
                    j = 2 * jb + jj
                    win = WINS[j]
                    b, tm = proj(Wp, jj, [trp])
                    tlast = tm
                    if samp:
                        xe = psext[:, j]
                        o_ap = xe[:, :, 15:23]
                        i0 = banks[b][:, 0:N].rearrange("p (b s) -> p b s", s=8)
                        wdeps = [sample_hist_p]
                        L = 23
                        sl_ = lambda ap, a, b_: ap[:, :, a:b_]
                        tv = [ptmp[q][:, 0:16 * 23].rearrange("p (b s) -> p b s", s=23) for q in range(2)]
                    else:
                        xe = pext[l][:, j]
                        o_ap = xe[:, 15:15 + N]
                        i0 = banks[b][:, 0:N]
                        wdeps = carry_tok[('p', l)]
                        L = 15 + N
                        sl_ = lambda ap, a, b_: ap[:, a:b_]
                        tv = [ptmp[q][:, 0:L] for q in range(2)]
                    tw = S.op('act', lambda E, o_ap=o_ap, i0=i0: E.activation(out=o_ap, in_=i0, func=AF.Copy), [tm] + list(wdeps) + pool_rd[-1:])
                    bank_free[b] = tw
                    s_ap = xe
                    v = 0
                    tprev = tw
                    d = 1
                    qi = 0
                    while d < win:
                        dst = tv[qi]
                        tprev = S.op('dve', lambda E, dst=dst, s_ap=s_ap, v=v, d=d, L=L, sl_=sl_: E.tensor_tensor(
                            out=sl_(dst, v + d, L), in0=sl_(s_ap, v + d, L), in1=sl_(s_ap, v, L - d), op=ALU.add), [tprev, t_a_last])
                        s_ap = dst
                        v += d
                        d *= 2
                        qi ^= 1
                    mx = mix[j]
                    if samp:
                        mo = mx[:, 0:N].rearrange("p (b s) -> p b s", s=8)
                    else:
                        mo = mx[:, 0:N]
                    tmx = S.op('dve', lambda E, mo=mo, s_ap=s_ap, xe=xe, L=L, sl_=sl_, win=win: E.scalar_tensor_tensor(
                        out=mo, in0=sl_(s_ap, 15, L), scalar=1.0 / win, in1=sl_(xe, 15, L), op0=ALU.mult, op1=ALU.subtract),
                        [tprev, mix_free4[j]])
                    if p == 0:
                        S.op('dve', lambda E, s_ap=s_ap, j=j: E.tensor_tensor(out=tmpP[:, 0:15], in0=s_ap[:, 15:30],
                                                                              in1=par[:, P_ICNT + j * 16:P_ICNT + j * 16 + 15], op=ALU.mult), [tmx], sig=True)
                        tmx = S.op('dve', lambda E, mx=mx, xe=xe: E.tensor_tensor(out=mx[:, 0:15], in0=tmpP[:, 0:15], in1=xe[:, 15:30], op=ALU.subtract),
                                   [('dve', S.cnt['dve'])])
                    pool_rd.append(tmx)
                    pool_pending.append((j, mx, tmx))
                if tokmaj:
                    tm_tp = mm_group(banks[tokb_p][:, jb * 256:(jb + 1) * 256],
                                     [(hT[:, k, tl * 128:(tl + 1) * 128], Wp[:, k, :]) for k in range(8)], [trp] + t_h, tokb_p)
                    hT_rd.append(tm_tp)
                    tlast = tm_tp
                wrel(ip, tlast)
            if tokmaj:
                tcp = S.op('dve', lambda E: E.tensor_copy(out=tok_c, in_=banks[tokb_p][:, :]), [tm_tp] + tokbuf_free('c'))
                bank_free[tokb_p] = tcp
                if samp:
                    st = S.dma('sp', pools_new_d[l], tok_c, "st_c", [tcp])
                else:
                    st = S.dma('sp', poolp_d[l], tok_c[113:128, :], "st_c", [tcp])
                store_toks.append(st)
                set_tokbuf_free('c', [st])
            mark(f"p{p}l{l}:conv")
            phase_barrier(tokens=bt_afterA)
            diag_ld = {}
            if p != first_pass:
                for j_ in range(2):
                    diag_ld[j_] = S.dma('sp', diag_flat[j_], diag_c[l, j_], f"ldd{j_}", deps=[diag_ready[(l, j_)]] + diag_free[j_] + diag_all_st[j_] + cur_bar[0])
            dbg_stop('SB0', samp, l)
            u_wr = []
            tokb_u = 7 if tokmaj else None
            tokb_g = 6 if tokmaj else None
            tm_tu = tm_tg = None
            for jb in range(2):
                ig, Wg, trg = wget()
                iv, Wv, trv = wget()
                tlast = None
                for jj in range(2):
                    j = 2 * jb + jj
                    b, tm = proj(Wg, jj, [trg])
                    sg = sgj[j % 2]
                    ts = S.op('act', lambda E, N=N, b=b, sg=sg: E.activation(out=sg[:, 0:N], in_=banks[b][:, 0:N], func=AF.Sigmoid),
                              [tm, sg_free[j % 2]])
                    bank_free[b] = ts
                    b2, tm2 = proj(Wv, jj, [trv])
                    if samp:
                        o_ap = usext[:, j, :, 30:38]
                        i0 = banks[b2][:, 0:N].rearrange("p (b s) -> p b s", s=8)
                        i1 = sg[:, 0:N].rearrange("p (b s) -> p b s", s=8)
                        wdeps = [sample_hist_u]
                    else:
                        o_ap = uext[l][:, j, 30:30 + N]
                        i0 = banks[b2][:, 0:N]
                        i1 = sg[:, 0:N]
                        wdeps = carry_tok[('u', l)]
                    tu = S.op('dve', lambda E, o_ap=o_ap, i0=i0, i1=i1: E.tensor_tensor(out=o_ap, in0=i0, in1=i1, op=ALU.mult),
                              [tm2, ts] + list(wdeps))
                    bank_free[b2] = tu
                    sg_free[j % 2] = tu
                    u_wr.append(tu)
                    tlast = tm2
                if tokmaj:
                    tm_tg = mm_group(banks[tokb_g][:, jb * 256:(jb + 1) * 256],
                                     [(hT[:, k, tl * 128:(tl + 1) * 128], Wg[:, k, :]) for k in range(8)], [trg] + t_h, tokb_g)
                    tm_tu = mm_group(banks[tokb_u][:, jb * 256:(jb + 1) * 256],
                                     [(hT[:, k, tl * 128:(tl + 1) * 128], Wv[:, k, :]) for k in range(8)], [trv] + t_h, tokb_u)
                    hT_rd.append(tm_tu)
                    tlast = tm_tu
                wrel(ig, tlast)
                wrel(iv, tlast)
            if tokmaj:
                ts = S.op('act', lambda E: E.activation(out=tok_b, in_=banks[tokb_g][:, :], func=AF.Sigmoid), [tm_tg, tm_tu] + tokbuf_free('b'))
                bank_free[tokb_g] = ts
                tu = S.op('dve', lambda E: E.tensor_tensor(out=tok_b, in0=banks[tokb_u][:, :], in1=tok_b, op=ALU.mult), [ts, tm_tu])
                bank_free[tokb_u] = tu
                if samp:
                    st = S.dma('sp', convs_new_d[l], tok_b, "st_b", [tu])
                else:
                    st = S.dma('sp', convp_d[l], tok_b[98:128, :], "st_b", [tu])
                store_toks.append(st)
                set_tokbuf_free('b', [st])
            dbg_stop('SC1', samp, l)
            conv_mm_last = None
            cv_wr = []
            for j in range(4):
                dg = diag[j % 2]
                st_d = None
                if p == first_pass:
                    td = None
                    for k in range(31):
                        td = S.op('dve', lambda E, dg=dg, k=k, j=j, l=l: E.tensor_scalar(
                            out=dg[:, k, :], in0=identb, scalar1=par[:, P_CW + (l * 4 + j) * 31 + k:P_CW + (l * 4 + j) * 31 + k + 1],
                            scalar2=None, op0=ALU.mult), ([t_ib, t_par[0]] + diag_free[j % 2] + diag_all_st[j % 2]) if k == 0 else [], sig=(k == 30))
                    st_d = S.dma('sp', diag_c[l, j], diag_flat[j % 2], f"std{l}{j}", [td])
                    diag_ready[(l, j)] = st_d
                    diag_all_st[j % 2].append(st_d)
                else:
                    td = diag_ld[j]
                b = get_bank()
                if samp:
                    pairs = [(dg[:, k, :], usext[:, j, :, k:k + 8]) for k in range(31)]
                    o_ap = banks[b][:, 0:N].rearrange("p (b s) -> p b s", s=8)
                else:
                    pairs = [(dg[:, k, :], uext[l][:, j, k:k + N]) for k in range(31)]
                    o_ap = banks[b][:, 0:N]
                tm = mm_group(o_ap, pairs, [td] + u_wr, b)
                diag_free[j % 2] = [tm, st_d]
                conv_mm_last = tm
                if j + 2 < 4 and p != first_pass:
                    diag_ld[j + 2] = S.dma('sp', diag_flat[j % 2], diag_c[l, j + 2], f"ldd{j % 2}", deps=[diag_ready[(l, j + 2)], tm] + diag_all_st[j % 2] + cur_bar[0])
                t1 = S.op('act', lambda E, N=N, b=b, j=j, l=l: E.activation(out=cvg[:, j, 0:N], in_=banks[b][:, 0:N], func=AF.Identity,
                                                                      bias=par[:, P_CB + l * 4 + j:P_CB + l * 4 + j + 1]), [tm] + cvg_free)
                t2 = S.op('act', lambda E, N=N, b=b, j=j, l=l: E.activation(out=sqg[:, j, 0:N], in_=banks[b][:, 0:N], func=AF.Square,
                                                                      bias=par[:, P_CB + l * 4 + j:P_CB + l * 4 + j + 1]), [tm])
                bank_free[b] = t2
                cv_wr.append(t2)
            if not samp:
                tc = S.op('dve', lambda E, N=N, l=l: E.tensor_copy(out=uext[l][:, :, 0:30], in_=uext[l][:, :, N:N + 30]), [conv_mm_last])
                carry_tok[('u', l)] = [tc]
            else:
                carry_tok[('us',)] = [conv_mm_last]
            for (j_, mx_, tmx_) in pool_pending:
                pend[0] = (j_, mx_, tmx_)
                pool_mm()
                t_a_last = pool_ep()
            if not samp:
                tc = S.op('dve', lambda E, N=N, l=l: E.tensor_copy(out=pext[l][:, :, 0:15], in_=pext[l][:, :, N:N + 15]), [t_a_last])
                carry_tok[('p', l)] = [tc]
            else:
                carry_tok[('ps',)] = [t_a_last]

            dbg_stop('SC2', samp, l)
            dbg_stop('SC0', samp, l)
            gate_w = []
            for jb in range(2):
                ig, Wg, trg = wget()
                tm = None
                for jj in range(2):
                    j = 2 * jb + jj
                    b, tm = proj(Wg, jj, [trg])
                    tg = S.op('act', lambda E, N=N, b=b, j=j: E.activation(out=gateA[:, j, 0:N], in_=banks[b][:, 0:N], func=AF.Silu),
                              [tm] + gateA_free[j])
                    bank_free[b] = tg
                    gate_w.append(tg)
                wrel(ig, tm)
            s1 = tmpA[:, 0:N]
            s2 = tmpB[:, 0:N]
            tr1 = S.op('dve', lambda E, N=N, s1=s1: E.tensor_reduce(out=s1, in_=cvg[:, :, 0:N].rearrange("p j n -> p n j"), axis=mybir.AxisListType.X, op=ALU.add),
                       cv_wr + stat_free)
            tr2 = S.op('dve', lambda E, N=N, s2=s2: E.tensor_reduce(out=s2, in_=sqg[:, :, 0:N].rearrange("p j n -> p n j"), axis=mybir.AxisListType.X, op=ALU.add),
                       cv_wr + stat_free)
            bm = get_bank()
            tmm = mm_group(banks[bm][:, 0:N], [(onesf, s1)], [tr1, t_of], bm)
            bq = get_bank()
            tmq = mm_group(banks[bq][:, 0:N], [(onesf, s2)], [tr2], bq)
            t_mean = S.op('act', lambda E, bm=bm, N=N: E.activation(out=mean_sb[:, 0:N], in_=banks[bm][:, 0:N], func=AF.Copy), [tmm] + stat_free)
            bank_free[bm] = t_mean
            t_m2 = S.op('dve', lambda E, N=N: E.tensor_tensor(out=var_sb[:, 0:N], in0=mean_sb[:, 0:N], in1=mean_sb[:, 0:N], op=ALU.mult), [t_mean])
            t_var = S.op('dve', lambda E, bq=bq, N=N: E.tensor_tensor(out=var_sb[:, 0:N], in0=banks[bq][:, 0:N], in1=var_sb[:, 0:N], op=ALU.subtract), [t_m2, tmq])
            bank_free[bq] = t_var
            t_l = S.op('act', lambda E, N=N: E.activation(out=rstd_sb[:, 0:N], in_=var_sb[:, 0:N], func=AF.Ln, bias=EPS), [t_var])
            t_r = S.op('act', lambda E, N=N: E.activation(out=rstd_sb[:, 0:N], in_=rstd_sb[:, 0:N], func=AF.Exp, scale=-0.5), [t_l])
            mean_b = mean_sb[:, 0:N].unsqueeze(1).broadcast_to([128, 4, N])
            rstd_b = rstd_sb[:, 0:N].unsqueeze(1).broadcast_to([128, 4, N])
            ta = S.op('dve', lambda E, N=N, mean_b=mean_b: E.tensor_tensor(out=cvg[:, :, 0:N], in0=cvg[:, :, 0:N], in1=mean_b, op=ALU.subtract),
                      [t_mean, tmm, tr1])
            tb = S.op('dve', lambda E, N=N, rstd_b=rstd_b: E.tensor_tensor(out=cvg[:, :, 0:N], in0=cvg[:, :, 0:N], in1=rstd_b, op=ALU.mult), [ta, t_r])
            tcs_all = []
            for j in range(4):
                tcs_all.append(S.op('act', lambda E, N=N, j=j, l=l: E.activation(out=sqg[:, j, 0:N], in_=cvg[:, j, 0:N], func=AF.Silu,
                                                                                 scale=par[:, P_LNG + l * 4 + j:P_LNG + l * 4 + j + 1],
                                                                                 bias=par[:, P_LNB + l * 4 + j:P_LNB + l * 4 + j + 1]), [tb, tmq, tr2]))
            tb_last = S.op('dve', lambda E, N=N: E.tensor_tensor(out=brb[:, 4:8, 0:N], in0=sqg[:, :, 0:N], in1=gateA[:, :, 0:N], op=ALU.mult),
                           tcs_all + gate_w + br_free)
            for j in range(4):
                gateA_free[j] = [tb_last]
            cvg_free[:] = [tb_last]
            stat_free[:] = [tb_last]
            t_bconv = tb_last

            dbg_stop('SB2', samp, l)
            mark(f"p{p}l{l}:attn")
            phase_barrier()
            q_w = []
            for jb in range(2):
                iq, Wq, trq = wget()
                tm = None
                for jj in range(2):
                    j = 2 * jb + jj
                    b, tm = proj(Wq, jj, [trq])
                    tq = S.op('act', lambda E, N=N, b=b, j=j: E.activation(out=QT[:, j, 0:N], in_=banks[b][:, 0:N], func=AF.Copy,
                                                                      scale=1.0 / math.sqrt(128.0)), [tm] + qt_free)
                    bank_free[b] = tq
                    q_w.append(tq)
                wrel(iq, tm)
            gate_w = [None] * 4
            xg_blocks = [wget(), wget()]

            def xg_chunk(j):
                ig, Wg, trg = xg_blocks[j // 2]
                b, tm = proj(Wg, j % 2, [trg])
                tg = S.op('act', lambda E, N=N, b=b, j=j: E.activation(out=gateB[:, j, 0:N], in_=banks[b][:, 0:N], func=AF.Silu),
                          [tm] + gateB_free[j])
                bank_free[b] = tg
                gate_w[j] = tg
                if j % 2 == 1:
                    wrel(ig, tm)

            if samp:
                for j in range(4):
                    xg_chunk(j)
            att_last = None
            if not samp:
                def s_stage(h):
                    Eh = Ebuf[h % 2]
                    tes = []
                    for mt in range(2):
                        b = get_bank()
                        tm = mm_group(banks[b][:, 0:N], [(KTp[l][:, h, mt * 128:(mt + 1) * 128], QT[:, h, 0:N])], [q_w[h]], b)
                        te = S.op('act', lambda E, N=N, b=b, Eh=Eh, mt=mt: E.activation(out=Eh[:, mt, 0:N], in_=banks[b][:, 0:N], func=AF.Exp),
                                  [tm, e_free[h % 2]])
                        bank_free[b] = te
                        tes.append(te)
                    return tes

                tes_next = s_stage(0)
                for h in range(4):
                    xg_chunk(h)
                    Eh = Ebuf[h % 2]
                    tes = tes_next
                    if h + 1 < 4:
                        tes_next = s_stage(h + 1)
                    bo = get_bank()
                    tmo = mm_group(banks[bo][:, 0:N], [(Vp[l][:, mt, h * 128:(h + 1) * 128], Eh[:, mt, 0:N]) for mt in range(2)], tes, bo)
                    bz = get_bank()
                    tmz = mm_group(banks[bz][:, 0:N], [(onesb, Eh[:, mt, 0:N]) for mt in range(2)], tes + [t_ob], bz)
                    e_free[h % 2] = tmz
                    t1 = S.op('dve', lambda E, N=N, bz=bz: E.reciprocal(out=rz[:, 0:N], in_=banks[bz][:, 0:N]), [tmz, att_last])
                    bank_free[bz] = t1
                    t2 = S.op('dve', lambda E, N=N, bo=bo: E.tensor_tensor(out=otmp[:, 0:N], in0=banks[bo][:, 0:N], in1=rz[:, 0:N], op=ALU.mult), [t1, tmo])
                    bank_free[bo] = t2
                    t3 = S.op('dve', lambda E, N=N, h=h: E.tensor_tensor(out=brb[:, 8 + h, 0:N], in0=otmp[:, 0:N], in1=gateB[:, h, 0:N], op=ALU.mult),
                              [t2, gate_w[h]] + br_free)
                    gateB_free[h] = [t3]
                    att_last = t3
            else:
                BO, BZ = 6, 7
                kb_free = [[], []]
                vb_free = [[], []]
                kt_free = [[], []]
                eb_free = [[], []]

                kv_tok = {}
                kvbuf_free = [[], [], [], []]

                def sa_load(bp):
                    q = bp % 4
                    kv_tok[bp] = S.dma('pool', KV2[q], ckv_d[l, bp], f"ldkv{q}", deps=kvbuf_free[q] + cur_bar[0])
                    kvbuf_free[q] = []

                def sa_stage1(bi):
                    bp, b2 = bi // 2, bi % 2
                    q2 = bi % 2
                    KTb = KTb2[q2]
                    Kb = v3(KV2[bp % 4][:, b2 * 2048:b2 * 2048 + 1024], 2)
                    Vb = v3(KV2[bp % 4][:, b2 * 2048 + 1024:b2 * 2048 + 2048], 2)
                    tkc = kv_tok[bp]
                    b = get_bank()
                    pb = banks[b].bitcast(BF16)
                    tt = None
                    for mt in range(2):
                        for h in range(4):
                            c = mt * 4 + h
                            tt = S.op('pe', lambda E, mt=mt, h=h, c=c, pb=pb, Kb=Kb: E.transpose(out=pb[:, c * 128:(c + 1) * 128],
                                                                                                in_=Kb[:, mt, h * 128:(h + 1) * 128], identity=identb),
                                      [tkc, t_ib, bank_free[b]] if c == 0 else [], sig=(c == 7))
                    kvbuf_free[bp % 4].append(tt)
                    tkt = S.op('dve', lambda E, pb=pb, KTb=KTb: E.tensor_copy(out=KTb, in_=v3(pb, 8)), [tt] + kt_free[q2])
                    bank_free[b] = tkt
                    return tkt, tkc, Vb

                def sa_stage2(bi, tkt, tvc, Vb):
                    q2 = bi % 2
                    KTb, Eb = KTb2[q2], Eb2[q2]
                    bs = get_bank()
                    tm = None
                    for h in range(4):
                        for mt in range(2):
                            c = h * 2 + mt
                            tm = S.op('pe', lambda E, h=h, mt=mt, c=c, bs=bs, bi=bi, KTb=KTb: E.matmul(
                                banks[bs][:, c * 8:(c + 1) * 8], lhsT=KTb[:, mt * 4 + h, :], rhs=QT[:, h, bi * 8:(bi + 1) * 8], start=True, stop=True),
                                [tkt, bank_free[bs]] + q_w if c == 0 else [], sig=(c == 7))
                    kt_free[q2] = [tm]
                    te = S.op('act', lambda E, bs=bs, Eb=Eb: E.activation(out=Eb, in_=banks[bs][:, 0:64], func=AF.Exp), [tm] + eb_free[q2])
                    bank_free[bs] = te
                    tmo = tmz = None
                    for h in range(4):
                        for mt in range(2):
                            c = h * 2 + mt
                            col = bi * 32 + h * 8
                            tmo = S.op('pe', lambda E, h=h, mt=mt, c=c, col=col, Vb=Vb, Eb=Eb: E.matmul(
                                banks[BO][:, col:col + 8], lhsT=Vb[:, mt, h * 128:(h + 1) * 128], rhs=Eb[:, c * 8:(c + 1) * 8],
                                start=(mt == 0), stop=(mt == 1)), [te, tvc, bank_free[BO]] if c == 0 else [], sig=(c == 7))
                    for h in range(4):
                        for mt in range(2):
                            c = h * 2 + mt
                            col = bi * 32 + h * 8
                            tmz = S.op('pe', lambda E, h=h, mt=mt, c=c, col=col, Eb=Eb: E.matmul(
                                banks[BZ][:, col:col + 8], lhsT=onesb, rhs=Eb[:, c * 8:(c + 1) * 8],
                                start=(mt == 0), stop=(mt == 1)), [te, t_ob, bank_free[BZ]] if c == 0 else [], sig=(c == 7))
                    kvbuf_free[(bi // 2) % 4].append(tmo)
                    eb_free[q2] = [tmz]
                    return tmo, tmz

                for bp_ in range(4):
                    sa_load(bp_)
                nxt = sa_stage1(0)
                tmo = tmz = None
                for bi in range(16):
                    cur = nxt
                    if bi + 1 < 16:
                        nxt = sa_stage1(bi + 1)
                    tmo, tmz = sa_stage2(bi, cur[0], cur[1], cur[2])
                    if bi % 2 == 1 and bi // 2 + 4 < 8:
                        sa_load(bi // 2 + 4)
                t1 = S.op('dve', lambda E: E.reciprocal(out=rz[:, 0:512], in_=banks[BZ][:, 0:512]), [tmz])
                bank_free[BZ] = t1
                t2 = S.op('dve', lambda E: E.tensor_tensor(out=otmp[:, 0:512], in0=banks[BO][:, 0:512], in1=rz[:, 0:512], op=ALU.mult), [t1, tmo])
                bank_free[BO] = t2
                t3 = S.op('dve', lambda E: E.tensor_tensor(out=brb[:, 8:12, 0:128].rearrange("p h (b s) -> p b h s", s=8),
                                                           in0=otmp[:, 0:512].rearrange("p (b h s) -> p b h s", h=4, s=8),
                                                           in1=gateB[:, :, 0:128].rearrange("p h (b s) -> p b h s", s=8), op=ALU.mult),
                          [t2] + gate_w + br_free)
                att_last = t3
                for h in range(4):
                    gateB_free[h] = [att_last]
            qt_free[:] = [att_last]
            dbg_stop('SB3', samp, l)
            mark(f"p{p}l{l}:merge")
            btm = phase_barrier()
            t_gp = S.dma('sp', gpost, gpost_d[:, l * 1024:(l + 1) * 1024], "ldg", deps=btm)
            br_rd = []
            mg_wr = []
            for dq in range(4):
                Wm = [wget() for _ in range(3)]
                Wb = [wget() for _ in range(3)]
                tlast = None
                for jj in range(2):
                    j = 2 * dq + jj
                    tsg = []
                    for n in range(3):
                        b, tm = proj(Wm[n][1], jj, [Wm[n][2]])
                        ts = S.op('act', lambda E, N=N, b=b, n=n: E.activation(out=sig3[n][:, 0:N], in_=banks[b][:, 0:N], func=AF.Sigmoid),
                                  [tm, sig_free[n]])
                        bank_free[b] = ts
                        tsg.append(ts)
                    bb = []
                    for n in range(3):
                        b = get_bank()
                        tm = mm_group(banks[b][:, 0:N], [(Wb[n][1][:, wk, jj * 128:(jj + 1) * 128], brb[:, n * 4 + wk, 0:N]) for wk in range(4)],
                                      [Wb[n][2], t_bconv, t_a_last, att_last], b)
                        bb.append((b, tm))
                        br_rd.append(tm)
                        tlast = tm
                    t1 = S.op('dve', lambda E, N=N, b=bb[0][0]: E.tensor_tensor(out=macc[:, 0:N], in0=banks[b][:, 0:N], in1=sig3[0][:, 0:N], op=ALU.mult),
                              [bb[0][1], tsg[0], mg_wr[-1] if mg_wr else None])
                    bank_free[bb[0][0]] = t1
                    sig_free[0] = t1
                    t2 = S.op('dve', lambda E, N=N, b=bb[1][0]: E.tensor_tensor(out=ttmp[:, 0:N], in0=banks[b][:, 0:N], in1=sig3[1][:, 0:N], op=ALU.mult),
                              [bb[1][1], tsg[1]])
                    bank_free[bb[1][0]] = t2
                    sig_free[1] = t2
                    t3 = S.op('dve', lambda E, N=N: E.tensor_tensor(out=macc[:, 0:N], in0=macc[:, 0:N], in1=ttmp[:, 0:N], op=ALU.add), [t1, t2])
                    t4 = S.op('dve', lambda E, N=N, b=bb[2][0]: E.tensor_tensor(out=ttmp[:, 0:N], in0=banks[b][:, 0:N], in1=sig3[2][:, 0:N], op=ALU.mult),
                              [bb[2][1], tsg[2], t3])
                    bank_free[bb[2][0]] = t4
                    sig_free[2] = t4
                    t5 = S.op('dve', lambda E, N=N, j=j: E.tensor_tensor(out=merged[:, j, 0:N], in0=macc[:, 0:N], in1=ttmp[:, 0:N], op=ALU.add),
                              [t3, t4] + mg_free)
                    mg_wr.append(t5)
                for n in range(3):
                    wrel(Wm[n][0], tlast)
                    wrel(Wb[n][0], tlast)
            br_free[:] = [br_rd[-1]]
            set_hT_free([hT_rd[-1], br_rd[-1]])
            mark(f"p{p}l{l}:D")
            bt = phase_barrier()
            Wo = [wget() for _ in range(4)]
            tlast = None
            d_free = [None, None]
            d_state = {}

            def d_stage1(t):
                bh = []
                tl_ = None
                yt = ytmp[t]
                for half in range(2):
                    b = 2 * t + half
                    tm = None
                    for q in range(2):
                        blk = half * 2 + q
                        tm = mm_group(banks[b][:, q * 256:(q + 1) * 256],
                                      [(merged[:, k, t * 128:(t + 1) * 128], Wo[blk][1][:, k, :]) for k in range(8)],
                                      [Wo[blk][2]] + mg_wr, b)
                    bh.append((b, tm))
                    tl_ = tm
                tsq = []
                tys = []
                for half in range(2):
                    tq = S.op('act', lambda E, b=bh[half][0], half=half, t=t: E.activation(out=junkD, in_=banks[b][:, :], func=AF.Square,
                                                                                         accum_out=ssD[:, 2 * t + half:2 * t + half + 1]),
                              [bh[half][1], junk_last[1]])
                    junk_last[1] = tq
                    ty = S.op('dve', lambda E, b=bh[half][0], half=half, yt=yt: E.tensor_tensor(
                        out=yt[:, half * 512:(half + 1) * 512], in0=banks[b][:, :], in1=gpost[:, half * 512:half * 512 + 512], op=ALU.mult),
                        [bh[half][1], tq, t_gp])
                    bank_free[bh[half][0]] = ty
                    tsq.append(tq)
                    tys.append(ty)
                d_state[t] = (tsq, tys)
                return tl_

            def d_stage2(t):
                tsq, tys = d_state.pop(t)
                yt = ytmp[t]
                tsum = S.op('dve', lambda E, t=t: E.tensor_tensor(out=ssD2[:, t:t + 1], in0=ssD[:, 2 * t:2 * t + 1], in1=ssD[:, 2 * t + 1:2 * t + 2], op=ALU.add), tsq)
                tl1 = S.op('act', lambda E, t=t: E.activation(out=rstdD[:, t:t + 1], in_=ssD2[:, t:t + 1], func=AF.Ln, scale=1.0 / 1024, bias=EPS), [tsum])
                tl2 = S.op('act', lambda E, t=t: E.activation(out=rstdD[:, t:t + 1], in_=rstdD[:, t:t + 1], func=AF.Exp, scale=-0.5), [tl1])
                tx = S.op('dve', lambda E, t=t, yt=yt: E.scalar_tensor_tensor(out=Xg[:, t, :], in0=yt, scalar=rstdD[:, t:t + 1], in1=Xg[:, t, :],
                                                                             op0=ALU.mult, op1=ALU.add), [tl2] + tys)
                d_free[t % 2] = tx
                x_ready[t] = tx
                if l == 1:
                    st = S.dma('sp', y_d[p * 4 + t], Xg[:, t, :], f"st_y{t}", [tx])
                    store_toks.append(st)
                    xg_free[t] = [st]
                    if nxt_pass is not None:
                        for tp_ in ([t - 1] if t >= 1 else []) + ([t] if t == nt - 1 else []):
                            if tp_ < nxt_pass[1] // 128:
                                x_pref[tp_] = S.dma('sp', Xg[:, tp_, :], x_d[nxt_pass[0] * 4 + tp_], f"ldx{tp_}", deps=xg_free[tp_])

            tlast = None
            for t in range(nt):
                tlast = d_stage1(t)
            fronts = {}
            th_next = []
            d_stage2(0)
            for t in range(nt):
                if t + 1 < nt:
                    d_stage2(t + 1)
                if l == 0:
                    fronts[t] = a_front(t, Xg[:, t, :], x_ready[t], gpre[:, (l + 1) * 1024:(l + 2) * 1024], t_gpre, bank=2 * t)
                    if t >= 1:
                        th_next.append(a_back(t - 1, fronts.pop(t - 1), hT, hT_free_list()))
            if l == 0:
                th_next.append(a_back(nt - 1, fronts.pop(nt - 1), hT, hT_free_list()))
                fused_th[0] = th_next
            for q in range(4):
                wrel(Wo[q][0], tlast)
            mg_free[:] = [tlast]
            if (DEBUG == 'L0' and l == 0 and p == 0) or (DEBUG == 'L1' and l == 1 and p == 0) or (DEBUG == 'S0' and l == 0 and samp) or (DEBUG == 'S1' and l == 1 and samp):
                bt = barrier()
                ds = [S.dma('sp', dbg_b, brb.rearrange("p a b -> p (a b)"), "dbg1", bt),
                      S.dma('sp', dbg_m, merged.rearrange("p a b -> p (a b)"), "dbg2", bt),
                      S.dma('sp', dbg_x, Xg.rearrange("p a b -> p (a b)"), "dbg3", bt),
                      S.dma('sp', dbg_q, QT.rearrange("p a b -> p (a b)"), "dbg4", bt),
                      S.dma('sp', dbg_g, gateB.rearrange("p a b -> p (a b)"), "dbg5", bt),
                      S.dma('sp', dbg_h, hT.rearrange("p a b -> p (a b)"), "dbg6", bt)]
                S.wait('sp', ds)
                S.emit(nc)
                es.close()
                return nc

      except _Stop:
        return nc
    mark("end")
    _mx = {}
    for (k_, v_) in store_toks:
        _mx[k_] = max(_mx.get(k_, 0), v_)
    S.wait('sp', list(_mx.items()))
    S.emit(nc)
    es.close()
    return nc


sg_free = [None, None]
gateA_free = [[], [], [], []]
gateB_free = [[], [], [], []]
mix_free4 = [None, None, None, None]
diag_free = [[], []]
cvg_free = []
stat_free = []
br_free = []
mix_free = [None, None]
qt_free = []
e_free = [None, None]
kv_free = {'k': [], 'v': [], 'kb': [], 'vb': [], 'kt': [], 'eb': []}
sig_free = [None, None, None]
mg_free = []
dphase_last = [None]
_hT_free = [[]]
_state_free = [[]]
_tokbuf = {'b': [], 'c': []}


def hT_free_list():
    return _hT_free[0]


def set_hT_free(v):
    _hT_free[0] = list(v)


def state_free():
    return _state_free[0]


def set_state_free(v):
    _state_free[0] = list(v)


def tokbuf_free(k):
    return _tokbuf[k]


def set_tokbuf_free(k, v):
    _tokbuf[k] = list(v)


def _reset_state():
    global sg_free, diag_free, cvg_free, stat_free, br_free, mix_free, qt_free, e_free, kv_free, sig_free, mg_free
    sg_free[:] = [None, None]
    for i in range(4):
        gateA_free[i] = []
        gateB_free[i] = []
        mix_free4[i] = None
    diag_free[0] = []
    diag_free[1] = []
    cvg_free[:] = []
    stat_free[:] = []
    br_free[:] = []
    mix_free[:] = [None, None]
    qt_free[:] = []
    e_free[:] = [None, None]
    for k in kv_free:
        kv_free[k] = []
    sig_free[:] = [None, None, None]
    mg_free[:] = []
    dphase_last[0] = None
    _hT_free[0] = []
    _state_free[0] = []
    _tokbuf['b'] = []
    _tokbuf['c'] = []


_NC_CACHE = {}


def _prep_shared(inp):
    f = np.float32
    w_in = np.asarray(inp["w_in"], f)
    win_l = np.ascontiguousarray(w_in.reshape(2, 8, 128, 26, 256).transpose(0, 3, 2, 1, 4)).reshape(2, 26, 128, 2048)
    w_br = np.asarray(inp["w_branch"], f)
    wbr_l = np.ascontiguousarray(w_br.reshape(2, 3, 4, 128, 4, 256).transpose(0, 1, 4, 3, 2, 5)).reshape(2, 3, 4, 128, 1024)
    w_out = np.asarray(inp["w_out"], f)
    wout_l = np.ascontiguousarray(w_out.reshape(2, 8, 128, 4, 256).transpose(0, 3, 2, 1, 4)).reshape(2, 4, 128, 2048)
    w_kv = np.asarray(inp["w_mem_kv"], f)
    wkv_l = np.ascontiguousarray(w_kv.reshape(2, 8, 128, 4, 256).transpose(0, 3, 2, 1, 4)).reshape(2, 4, 128, 2048)
    pool_w = np.asarray(inp["pool_w"], f)
    poolw_l = np.ascontiguousarray(pool_w.transpose(0, 2, 1, 3)).reshape(2, 128, 512)
    par = np.zeros((128, NPAR), f)
    par[:, P_GPRE:P_GPRE + 16] = np.asarray(inp["norm_pre"], f).reshape(2, 8, 128).transpose(2, 0, 1).reshape(128, 16)
    par[:, P_GMEM:P_GMEM + 16] = np.asarray(inp["mem_norm"], f).reshape(2, 8, 128).transpose(2, 0, 1).reshape(128, 16)
    par[:, P_PSC:P_PSC + 8] = np.asarray(inp["pool_scale"], f).reshape(2, 4, 128).transpose(2, 0, 1).reshape(128, 8)
    par[:, P_CB:P_CB + 8] = np.asarray(inp["conv_b"], f).reshape(2, 4, 128).transpose(2, 0, 1).reshape(128, 8)
    par[:, P_LNG:P_LNG + 8] = np.asarray(inp["conv_ln_g"], f).reshape(2, 4, 128).transpose(2, 0, 1).reshape(128, 8)
    par[:, P_LNB:P_LNB + 8] = np.asarray(inp["conv_ln_b"], f).reshape(2, 4, 128).transpose(2, 0, 1).reshape(128, 8)
    par[:, P_CW:P_CW + 248] = np.asarray(inp["conv_w"], f).reshape(2, 31, 4, 128).transpose(3, 0, 2, 1).reshape(128, 248)
    icnt = np.zeros((4, 16), f)
    for j, w in enumerate(WINS):
        for t in range(16):
            icnt[j, t] = 1.0 / min(t + 1, w)
    par[:, P_ICNT:P_ICNT + 64] = icnt.reshape(1, 64)
    gpost = np.ascontiguousarray(np.broadcast_to(np.asarray(inp["norm_post"], f).reshape(1, 2048), (128, 2048)))
    gpre_b = np.ascontiguousarray(np.broadcast_to(np.asarray(inp["norm_pre"], f).reshape(1, 2048), (128, 2048)))
    gmem_b = np.ascontiguousarray(np.broadcast_to(np.asarray(inp["mem_norm"], f).reshape(1, 2048), (128, 2048)))
    return {"w_in": win_l, "w_br": wbr_l, "w_out": wout_l, "w_kv": wkv_l, "pool_w": poolw_l, "params": par,
            "gpost": gpost, "gpre": gpre_b, "gmem": gmem_b, "ident": np.eye(128, dtype=f)}


def kernel(**inp):
    f = np.float32
    _reset_state()
    nc = build_program()
    shared = _prep_shared(inp)
    xp = np.asarray(inp["x_prompt"], f)
    xs = np.asarray(inp["x_sample"], f)
    memp = np.asarray(inp["mem_prompt"], f)
    sp = np.asarray(inp["state_pool"], f)
    sc = np.asarray(inp["state_conv"], f)
    ck = np.asarray(inp["cache_mem_k"], f)
    cv = np.asarray(inp["cache_mem_v"], f)
    in_maps = []
    for c in range(NCORES):
        b0 = 16 * c
        x = np.concatenate([xp[c].reshape(16, 128, 1024), xs[b0:b0 + 16].reshape(1, 128, 1024)], axis=0)
        m = dict(shared)
        m["x"] = np.ascontiguousarray(x)
        m["mem"] = np.ascontiguousarray(memp[c].reshape(2, 128, 1024))
        m["spool"] = np.ascontiguousarray(sp[:, b0:b0 + 16].reshape(2, 2, 120, 512))
        m["sconv"] = np.ascontiguousarray(sc[:, b0:b0 + 16].reshape(2, 4, 120, 512))
        kk = ck[:, b0:b0 + 16].reshape(2, 8, 2, 2, 128, 512)
        vv = cv[:, b0:b0 + 16].reshape(2, 8, 2, 2, 128, 512)
        kvv = np.stack([kk, vv], axis=3)
        m["ckv"] = np.ascontiguousarray(kvv.transpose(0, 1, 5, 2, 3, 4, 6)).reshape(2, 8, 128, 4096)
        in_maps.append(m)
    res = run_bass_kernel_spmd(nc, in_maps, core_ids=list(range(NCORES)))
    R = res.results
    y_p = np.stack([R[c]["y"][:16].reshape(2048, 1024) for c in range(NCORES)], 0)
    y_s = np.concatenate([R[c]["y"][16].reshape(16, 8, 1024) for c in range(NCORES)], 0)
    pool_p = np.stack([R[c]["pool_p"] for c in range(NCORES)], 1)
    conv_p = np.stack([R[c]["conv_p"] for c in range(NCORES)], 1)
    k_p = np.stack([R[c]["k_p"].reshape(2, 256, 4, 128) for c in range(NCORES)], 1)
    v_p = np.stack([R[c]["v_p"].reshape(2, 256, 4, 128) for c in range(NCORES)], 1)
    pool_s = np.concatenate([np.concatenate([R[c]["pool_s_old"], R[c]["pool_s_new"].reshape(2, 16, 8, 512)], axis=2) for c in range(NCORES)], 1)
    conv_s = np.concatenate([np.concatenate([R[c]["conv_s_old"], R[c]["conv_s_new"].reshape(2, 16, 8, 512)], axis=2) for c in range(NCORES)], 1)
    return (y_p.astype(f), y_s.astype(f), pool_p.astype(f), conv_p.astype(f), k_p.astype(f), v_p.astype(f),
            pool_s.astype(f), conv_s.astype(f))
```

```python
import math
import numpy as np
from contextlib import ExitStack
import concourse.bass as bass
import concourse.mybir as mybir
from concourse.bass_utils import run_bass_kernel_spmd

F32 = mybir.dt.float32
BF16 = mybir.dt.bfloat16
AF = mybir.ActivationFunctionType
ALU = mybir.AluOpType

NCORES = 8
EPS = 1e-6
WINS = (2, 4, 8, 16)
P_GPRE, P_GMEM, P_PSC, P_CB, P_LNG, P_LNB, P_CW, P_ICNT = 0, 16, 32, 40, 48, 56, 64, 312
NPAR = 376
B_PIN, B_PGATE, B_CVAL, B_CGLU, B_CGATE, B_Q, B_XGATE, B_MERGE = 0, 2, 4, 6, 8, 10, 12, 14


class Sched:
    ENG = ('pe', 'act', 'dve', 'pool', 'sp')

    def __init__(self):
        self.ops = {e: [] for e in self.ENG}
        self.cnt = {e: 0 for e in self.ENG}
        self.dcnt = {}

    def op(self, eng, fn, deps=(), sig=True):
        tok = None
        if sig:
            self.cnt[eng] += 1
            tok = (eng, self.cnt[eng])
        self.ops[eng].append(('op', fn, [d for d in deps if d is not None], sig))
        return tok

    def dma(self, q, out, in_, key, deps=()):
        self.dcnt[key] = self.dcnt.get(key, 0) + 16
        tok = (key, self.dcnt[key])
        self.ops[q].append(('dma', (out, in_, key), [d for d in deps if d is not None], True))
        return tok

    def wait(self, eng, deps):
        self.ops[eng].append(('wait', None, [d for d in deps if d is not None], False))

    def emit(self, nc):
        keys = list(self.ENG) + sorted(self.dcnt.keys())
        with ExitStack() as es:
            sems = {}
            for k in keys:
                sems[k] = es.enter_context(nc.semaphore("s_" + k))
            blk = es.enter_context(nc.Block())
            sched = self

            def run(eng_name):
                def body(E):
                    known = {}
                    for kind, payload, deps, sig in sched.ops[eng_name]:
                        for (k, v) in deps:
                            if known.get(k, 0) < v:
                                E.wait_ge(sems[k], v)
                                known[k] = v
                        if kind == 'op':
                            ins = payload(E)
                            if sig:
                                ins.then_inc(sems[eng_name], 1)
                        elif kind == 'dma':
                            out, in_, key = payload
                            E.dma_start(out=out, in_=in_).then_inc(sems[key], 16)
                return body

            blk.tensor(run('pe'))
            blk.scalar(run('act'))
            blk.vector(run('dve'))
            blk.gpsimd(run('pool'))
            blk.sync(run('sp'))


DEBUG = None
MARKS = []


def build_program():
    nc = bass.Bass("TRN2", target_bir_lowering=False)
    if DEBUG:
        dbg_h = nc.dram_tensor("dbg_h", [128, 4096], BF16, kind="ExternalOutput").ap()
        dbg_r = nc.dram_tensor("dbg_r", [128, 16], F32, kind="ExternalOutput").ap()
        dbg_x = nc.dram_tensor("dbg_x", [128, 4096], F32, kind="ExternalOutput").ap()
        dbg_b = nc.dram_tensor("dbg_b", [128, 12 * 512], BF16, kind="ExternalOutput").ap()
        dbg_m = nc.dram_tensor("dbg_m", [128, 8 * 512], BF16, kind="ExternalOutput").ap()
        dbg_q = nc.dram_tensor("dbg_q", [128, 4 * 512], BF16, kind="ExternalOutput").ap()
        dbg_g = nc.dram_tensor("dbg_g", [128, 4 * 512], BF16, kind="ExternalOutput").ap()

    def din(name, shape):
        return nc.dram_tensor(name, list(shape), F32, kind="ExternalInput").ap()

    def dout(name, shape):
        return nc.dram_tensor(name, list(shape), F32, kind="ExternalOutput").ap()

    x_d = din("x", (17, 128, 1024))
    mem_d = din("mem", (2, 128, 1024))
    spool_d = din("spool", (2, 2, 120, 512))
    sconv_d = din("sconv", (2, 4, 120, 512))
    ckv_d = din("ckv", (2, 8, 128, 4096))
    win_d = din("w_in", (2, 26, 128, 2048))
    wbr_d = din("w_br", (2, 3, 4, 128, 1024))
    wout_d = din("w_out", (2, 4, 128, 2048))
    wkv_d = din("w_kv", (2, 4, 128, 2048))
    poolw_d = din("pool_w", (2, 128, 512))
    par_d = din("params", (128, NPAR))
    gpost_d = din("gpost", (128, 2048))
    gpre_d = din("gpre", (128, 2048))
    gmem_d = din("gmem", (128, 2048))
    ident_d = din("ident", (128, 128))

    wc_in = nc.dram_tensor("wc_in", [2, 26, 128, 2048], BF16).ap()
    wc_br = nc.dram_tensor("wc_br", [2, 3, 4, 128, 1024], BF16).ap()
    wc_out = nc.dram_tensor("wc_out", [2, 4, 128, 2048], BF16).ap()
    diag_c = nc.dram_tensor("diag_c", [2, 4, 128, 31 * 128], BF16).ap()

    y_d = dout("y", (17, 128, 1024))
    poolp_d = dout("pool_p", (2, 15, 512))
    convp_d = dout("conv_p", (2, 30, 512))
    kp_d = dout("k_p", (2, 256, 512))
    vp_d = dout("v_p", (2, 256, 512))
    pools_new_d = dout("pool_s_new", (2, 128, 512))
    convs_new_d = dout("conv_s_new", (2, 128, 512))
    pools_old_d = dout("pool_s_old", (2, 16, 7, 512))
    convs_old_d = dout("conv_s_old", (2, 16, 22, 512))

    S = Sched()
    es = ExitStack()
    NW = 52400
    A = es.enter_context(nc.sbuf_tensor("arena", [128, NW], F32))
    banks = [es.enter_context(nc.psum_tensor(f"ps{i}", [128, 512], F32)) for i in range(8)]
    off = [0]

    def carve(n, dt=F32):
        a = A[:, off[0]:off[0] + n]
        off[0] += n
        assert off[0] <= NW, off[0]
        if dt == BF16:
            a = a.bitcast(BF16)
        return a

    def v3(ap, a):
        return ap.rearrange("p (a b) -> p a b", a=a)

    par = carve(NPAR)
    identf = carve(128)
    onesf = carve(128)
    identb = carve(64, BF16)
    onesb = carve(64, BF16)
    poolwf = carve(512)
    poolwb = [carve(256, BF16) for _ in range(2)]
    KTp = [v3(carve(512, BF16), 4) for _ in range(2)]
    Vp = [v3(carve(512, BF16), 2) for _ in range(2)]
    ext0 = off[0]
    pext = [v3(carve(4 * 527), 4) for _ in range(2)]
    uext = [v3(carve(4 * 272, BF16), 4) for _ in range(2)]
    ext1 = off[0]
    off[0] = ext0
    psext = carve(4 * 16 * 23).rearrange("p (j b s) -> p j b s", j=4, b=16)
    usext = carve(4 * 16 * 19, BF16).rearrange("p (j b s) -> p j b s", j=4, b=16)
    off[0] = ext1
    Xg = v3(carve(4096), 4)
    hT = v3(carve(2048, BF16), 8)
    gateA = v3(carve(1024, BF16), 4)
    gateB = v3(carve(1024, BF16), 4)
    QT = v3(carve(1024, BF16), 4)
    brb = v3(carve(3072, BF16), 12)
    merged = v3(carve(2048, BF16), 8)
    tok_b = carve(512)
    tok_c = carve(512)
    tok_a = tok_b
    ssA = carve(8)
    rstdA = carve(8)
    ssD = carve(8)
    ssD2 = carve(8)
    rstdD = carve(8)
    NSLOT = 10
    slots = [carve(1024, BF16) for _ in range(NSLOT)]
    gpre = carve(2048)
    sttile = carve(512)
    ptmp = [carve(527) for _ in range(2)]
    mix = [carve(256, BF16) for _ in range(4)]
    tmpP = carve(16)
    T0 = off[0]
    tmax = [T0]

    def treset():
        off[0] = T0

    def tcarve(n, dt=F32):
        a = carve(n, dt)
        tmax[0] = max(tmax[0], off[0])
        return a

    def barrier():
        return [(e, S.cnt[e]) for e in ('pe', 'act', 'dve', 'pool') if S.cnt[e] > 0]

    treset()
    xnb = [tcarve(512, BF16) for _ in range(2)]
    junkA = tcarve(512, BF16)
    ytmp = [tcarve(1024) for _ in range(4)]
    junkD = tcarve(256, BF16)
    gpost = tcarve(1024)
    mhT = v3(tcarve(1024, BF16), 8)
    gmem = tcarve(1024)
    treset()
    sgj = [tcarve(512) for _ in range(2)]
    diag_flat = [tcarve(31 * 64, BF16) for _ in range(2)]
    diag = [v3(d_, 31) for d_ in diag_flat]
    cvg = v3(tcarve(2048), 4)
    sqg = v3(tcarve(2048), 4)
    mean_sb = tcarve(512)
    var_sb = tcarve(512)
    rstd_sb = tcarve(512)
    tmpA = tcarve(512)
    tmpB = tcarve(512)
    treset()
    Ebuf = [v3(tcarve(512, BF16), 2) for _ in range(2)]
    rz = tcarve(512)
    otmp = tcarve(512)
    KV2 = [tcarve(2048, BF16) for _ in range(4)]
    KTb2 = [v3(tcarve(512, BF16), 8) for _ in range(2)]
    Eb2 = [tcarve(32, BF16) for _ in range(2)]
    treset()
    sig3 = [tcarve(512) for _ in range(3)]
    macc = tcarve(512)
    ttmp = tcarve(512)
    off[0] = tmax[0]
    print("SBUF words used", off[0])

    bank_free = [None] * 8
    bank_next = [0]

    def get_bank():
        b = bank_next[0]
        bank_next[0] = (b + 1) % 6
        return b

    wlist = []
    PASSES = [(0, 512, False), (1, 512, False), (2, 512, False), (3, 512, False), (4, 128, True)]
    if DEBUG and DEBUG.startswith('P'):
        PASSES = PASSES[:int(DEBUG[1:])]

    def layer_blocks(l):
        bl = []
        g0 = P_GPRE + l * 8

        def wi(blk):
            return (win_d[l, blk], 8, g0, wc_in[l, blk])
        for jb in range(2):
            bl.append(wi(B_PGATE + jb))
        for jb in range(2):
            bl.append(wi(B_PIN + jb))
        for jb in range(2):
            bl.append(wi(B_CGLU + jb))
            bl.append(wi(B_CVAL + jb))
        for jb in range(2):
            bl.append(wi(B_CGATE + jb))
        for jb in range(2):
            bl.append(wi(B_Q + jb))
        for jb in range(2):
            bl.append(wi(B_XGATE + jb))
        for dq in range(4):
            for n in range(3):
                bl.append(wi(B_MERGE + n * 4 + dq))
            for n in range(3):
                bl.append((wbr_d[l, n, dq], 4, None, wc_br[l, n, dq]))
        for b4 in range(4):
            bl.append((wout_d[l, b4], 8, None, wc_out[l, b4]))
        return bl

    for l in range(2):
        for b4 in range(4):
            wlist.append((wkv_d[l, b4], 8, P_GMEM + l * 8, None, 'cast'))
    for _pi, _p in enumerate(PASSES):
        for l in range(2):
            for (a_, nk_, sc_, c_) in layer_blocks(l):
                wlist.append((a_, nk_, sc_, c_, 'cast+store' if (_pi == 0 and len(PASSES) > 1) else ('cached' if _pi > 0 else 'cast')))
    NB = len(wlist)
    w_issued = [0]
    w_ready = [None] * NB
    slot_last = [None] * NSLOT
    slot_owner = [None] * NSLOT
    t_par = [None]
    LA = 4

    slot_store = [None] * NSLOT
    cache_tok = {}

    def w_issue(i):
        ap, nk, sc, cap, mode = wlist[i]
        sl = i % NSLOT
        if slot_owner[sl] is not None:
            assert slot_last[sl] is not None, ("slot not released", i, slot_owner[sl])
        deps = list(slot_last[sl] or []) + [slot_store[sl]]
        dst = slots[sl][:, 0:nk * 256]
        if mode == 'cached':
            td = S.dma('pool', dst, cap, f"wsl{sl}", deps=deps + [cache_tok[str(cap)]])
        else:
            td = S.dma('pool', dst, ap, f"wsl{sl}", deps=deps)
            if mode == 'cast+store':
                ts = S.dma('sp', cap, dst, f"wcs{sl}", deps=[td])
                slot_store[sl] = ts
                cache_tok[str(cap)] = ts
        slot_owner[sl] = i
        slot_last[sl] = None
        w_ready[i] = td

    w_ptr = [0]

    def wget():
        i = w_ptr[0]
        w_ptr[0] += 1
        while w_issued[0] < min(NB, i + 1 + LA):
            w_issue(w_issued[0])
            w_issued[0] += 1
        sl = i % NSLOT
        nk = wlist[i][1]
        return i, v3(slots[sl][:, 0:nk * 256], nk), w_ready[i]

    def wrel(i, tok):
        slot_last[i % NSLOT] = [tok]

    def mm_group(out_ap, pairs, deps, bank):
        n = len(pairs)
        t = None
        for i, (lt, rh) in enumerate(pairs):
            d = (list(deps) + [bank_free[bank]]) if i == 0 else []
            t = S.op('pe', lambda E, lt=lt, rh=rh, i=i, n=n: E.matmul(out_ap, lhsT=lt, rhs=rh, start=(i == 0), stop=(i == n - 1)),
                     d, sig=(i == n - 1))
        return t

    store_toks = []

    t_par[0] = S.dma('sp', par, par_d, "ldp0")
    t_id = S.dma('sp', identf, ident_d, "ldp2")
    t_ib = S.op('dve', lambda E: E.tensor_copy(out=identb, in_=identf), [t_id])
    t_of = S.op('dve', lambda E: E.memset(onesf, 1.0 / 512.0))
    t_ob = S.op('dve', lambda E: E.memset(onesb, 1.0))
    t_z = None
    for l in range(2):
        S.op('dve', lambda E, l=l: E.memset(pext[l][:, :, 0:15], 0.0), sig=False)
        t_z = S.op('dve', lambda E, l=l: E.memset(uext[l][:, :, 0:30], 0.0))
    t_pw = []
    for l in range(2):
        td = S.dma('sp', poolwf, poolw_d[l], "ldpw", deps=[t_pw[-1]] if t_pw else [])
        t_pw.append(S.op('dve', lambda E, l=l: E.tensor_copy(out=poolwb[l], in_=poolwf), [td]))

    junk_last = [None, None]

    def a_front(t, src_tile, src_rdy, g_ap, g_ready, bank=None):
        tsq_ = S.op('act', lambda E, t=t: E.activation(out=junkA, in_=src_tile, func=AF.Square, accum_out=ssA[:, t:t + 1]),
                    [src_rdy, junk_last[0]])
        junk_last[0] = tsq_
        t_ln = S.op('act', lambda E, t=t: E.activation(out=rstdA[:, t:t + 1], in_=ssA[:, t:t + 1], func=AF.Ln, scale=1.0 / 1024, bias=EPS), [tsq_])
        t_ex = S.op('act', lambda E, t=t: E.activation(out=rstdA[:, t:t + 1], in_=rstdA[:, t:t + 1], func=AF.Exp, scale=-0.5), [t_ln])
        xb = xnb[t % 2]
        t_xn = S.op('dve', lambda E, t=t, xb=xb: E.scalar_tensor_tensor(out=xb, in0=src_tile, scalar=rstdA[:, t:t + 1], in1=g_ap,
                                                                         op0=ALU.mult, op1=ALU.mult),
                    [t_ex, rms_transpose.xb_free[t % 2], g_ready])
        b = get_bank() if bank is None else bank
        pb = banks[b].bitcast(BF16)
        tt = None
        for k in range(8):
            tt = S.op('pe', lambda E, k=k, xb=xb, pb=pb: E.transpose(out=pb[:, k * 128:(k + 1) * 128], in_=xb[:, k * 128:(k + 1) * 128], identity=identb),
                      [t_xn, t_ib, bank_free[b]] if k == 0 else [], sig=(k == 7))
        rms_transpose.xb_free[t % 2] = tt
        return b, pb, tt

    def a_back(t, front, dst, dst_free):
        b, pb, tt = front
        te = S.op('act', lambda E, t=t, pb=pb: E.activation(out=dst[:, :, t * 128:(t + 1) * 128], in_=v3(pb, 8), func=AF.Copy),
                  [tt] + list(dst_free))
        bank_free[b] = te
        return te

    def rms_transpose(src_tiles, nt, dst, src_ready, dst_free, g_ap, g_ready):
        outs = []
        fr = {0: a_front(0, src_tiles[0], src_ready[0], g_ap, g_ready)}
        for t in range(nt):
            if t + 1 < nt:
                fr[t + 1] = a_front(t + 1, src_tiles[t + 1], src_ready[t + 1], g_ap, g_ready)
            outs.append(a_back(t, fr.pop(t), dst, dst_free))
        return outs
    rms_transpose.xb_free = [None, None]
    rms_transpose.sq_free = [None, None]

    t_m = [S.dma('sp', ytmp[t], mem_d[t], f"ldm{t}") for t in range(2)]
    x_pref = {}
    for t in range(PASSES[0][1] // 128):
        x_pref[t] = S.dma('sp', Xg[:, t, :], x_d[PASSES[0][0] * 4 + t], f"ldx{t}")
    t_gpre = S.dma('sp', gpre, gpre_d, "ldp3")
    tokbufs = [tok_b, tok_c]
    tok_st = [None, None]
    tok_i = [0]
    t_mh = []
    mh_rd = []
    for l in range(2):
        t_gm = S.dma('sp', gmem, gmem_d[:, l * 1024:(l + 1) * 1024], "ldgm", deps=t_mh)
        t_mh = rms_transpose([ytmp[t] for t in range(2)], 2, mhT, t_m, mh_rd, gmem, t_gm)
        wk = [wget() for _ in range(2)]
        for h in range(4):
            i, W, tr = wk[h // 2]
            b = get_bank()
            tm = mm_group(banks[b][:, 0:256], [(W[:, k, (h % 2) * 128:(h % 2) * 128 + 128], mhT[:, k, :]) for k in range(8)],
                          [tr] + t_mh, b)
            te = S.op('act', lambda E, l=l, h=h, b=b: E.activation(out=KTp[l][:, h, :], in_=banks[b][:, 0:256], func=AF.Copy), [tm])
            bank_free[b] = te
        for mt in range(2):
            b = get_bank()
            tm = None
            for q in range(2):
                i, W, tr = wk[q]
                tm = mm_group(banks[b][:, q * 256:(q + 1) * 256], [(mhT[:, k, mt * 128:(mt + 1) * 128], W[:, k, :]) for k in range(8)],
                              [tr] + t_mh, b)
            tkb = tokbufs[tok_i[0] % 2]
            te = S.op('dve', lambda E, b=b, tkb=tkb: E.tensor_copy(out=tkb, in_=banks[b][:, :]), [tm, tok_st[tok_i[0] % 2]])
            bank_free[b] = te
            tok_st[tok_i[0] % 2] = S.dma('sp', kp_d[l, mt * 128:(mt + 1) * 128, :], tkb, f"st_a{tok_i[0] % 2}", [te])
            store_toks.append(tok_st[tok_i[0] % 2])
            tok_i[0] += 1
        for q in range(2):
            wrel(wk[q][0], tm)
        wv = [wget() for _ in range(2)]
        for mt in range(2):
            b = get_bank()
            tm = None
            for q in range(2):
                i, W, tr = wv[q]
                tm = mm_group(banks[b][:, q * 256:(q + 1) * 256], [(mhT[:, k, mt * 128:(mt + 1) * 128], W[:, k, :]) for k in range(8)],
                              [tr] + t_mh, b)
            tkb = tokbufs[tok_i[0] % 2]
            te = S.op('dve', lambda E, b=b, tkb=tkb: E.tensor_copy(out=tkb, in_=banks[b][:, :]), [tm, tok_st[tok_i[0] % 2]])
            te2 = S.op('act', lambda E, l=l, mt=mt, b=b: E.activation(out=Vp[l][:, mt, :], in_=banks[b][:, :], func=AF.Copy), [tm, te])
            bank_free[b] = te2
            tok_st[tok_i[0] % 2] = S.dma('sp', vp_d[l, mt * 128:(mt + 1) * 128, :], tkb, f"st_a{tok_i[0] % 2}", [te])
            store_toks.append(tok_st[tok_i[0] % 2])
            tok_i[0] += 1
        for q in range(2):
            wrel(wv[q][0], tm)
        mh_rd = [tm]
    xg_free = [[] for _ in range(4)]
    set_tokbuf_free('b', [tok_st[0]])
    set_tokbuf_free('c', [tok_st[1]])
    for l in range(2):
        src = spool_d[l].rearrange("t (b r) c -> (t b) r c", r=15)
        store_toks.append(S.dma('sp', pools_old_d[l], src[:, 8:15, :], "st_o"))
        src = sconv_d[l].rearrange("t (b r) c -> (t b) r c", r=30)
        store_toks.append(S.dma('sp', convs_old_d[l], src[:, 8:30, :], "st_o"))

    diag_ready = {}
    diag_all_st = [[], []]
    first_pass = PASSES[0][0]

    for e_ in ('pe', 'act', 'dve'):
        S.wait(e_, [t_par[0], t_id, t_gpre])

    class _Stop(Exception):
        pass

    def dbg_stop(tag, samp, l):
        if DEBUG == tag and samp and l == 1:
            bt = barrier()
            ds = [S.dma('sp', dbg_b, brb.rearrange("p a b -> p (a b)"), "dbg1", bt),
                  S.dma('sp', dbg_m, merged.rearrange("p a b -> p (a b)"), "dbg2", bt),
                  S.dma('sp', dbg_x, Xg.rearrange("p a b -> p (a b)"), "dbg3", bt),
                  S.dma('sp', dbg_q, QT.rearrange("p a b -> p (a b)"), "dbg4", bt),
                  S.dma('sp', dbg_g, gateB.rearrange("p a b -> p (a b)"), "dbg5", bt),
                  S.dma('sp', dbg_h, hT.rearrange("p a b -> p (a b)"), "dbg6", bt)]
            S.wait('sp', ds)
            S.emit(nc)
            es.close()
            raise _Stop()

    cur_bar = [[]]
    prev_pool_T = [False]

    def mark(label):
        MARKS.append((label, sum(1 for o in S.ops['pe'] if o[0] == 'op')))

    def phase_barrier(pool_too=False, tokens=None):
        engs = ['pe', 'act', 'dve'] + (['pool'] if (pool_too or prev_pool_T[0]) else [])
        bt = [(e, S.cnt[e]) for e in engs if S.cnt[e] > 0]
        if tokens is not None:
            bt = list(tokens)
        bt = bt + diag_all_st[0] + diag_all_st[1]
        S.wait('act', bt)
        S.wait('dve', bt)
        if pool_too:
            S.wait('pool', bt)
        prev_pool_T[0] = pool_too
        cur_bar[0] = bt
        return bt

    carry_tok = {('p', 0): [t_z], ('p', 1): [t_z], ('u', 0): [t_z], ('u', 1): [t_z], ('ps',): [], ('us',): []}
    x_ready = [None] * 4
    fused_th = [None]
    last_tokA = [None]

    for (p, N, samp) in PASSES:
      try:
        nt = N // 128
        for t in range(nt):
            if t in x_pref:
                x_ready[t] = x_pref.pop(t)
            else:
                x_ready[t] = S.dma('sp', Xg[:, t, :], x_d[p * 4 + t], f"ldx{t}", deps=xg_free[t])
        _pi = [q[0] for q in PASSES].index(p)
        nxt_pass = PASSES[_pi + 1] if _pi + 1 < len(PASSES) else None
        if samp:
            bt0 = barrier()
            carry_tok[('ps',)] = list(bt0)
            carry_tok[('us',)] = list(bt0)
        for l in range(2):
            g0 = P_GPRE + l * 8
            mark(f"p{p}l{l}:A")
            phase_barrier()
            if fused_th[0] is not None:
                t_h = fused_th[0]
                fused_th[0] = None
            else:
                t_h = rms_transpose([Xg[:, t, :] for t in range(nt)], nt, hT, x_ready, hT_free_list(), gpre[:, l * 1024:(l + 1) * 1024], t_gpre)
            hT_rd = []
            bt_afterA = [(e, S.cnt[e]) for e in ('pe', 'act', 'dve') if S.cnt[e] > 0]
            if DEBUG == 'A' or (DEBUG == 'SA' and samp and l == 1):
                d1 = S.dma('sp', dbg_h, hT.rearrange("p a b -> p (a b)"), "dbg1", t_h)
                d2 = S.dma('sp', dbg_r[:, 0:8], ssA, "dbg2", t_h)
                d3 = S.dma('sp', dbg_r[:, 8:16], rstdA, "dbg3", t_h)
                d4 = S.dma('sp', dbg_x, Xg.rearrange("p a b -> p (a b)"), "dbg4", t_h)
                S.wait('sp', [d1, d2, d3, d4])
                S.emit(nc)
                es.close()
                return nc

            def proj(Wv, jj, deps):
                b = get_bank()
                tm = mm_group(banks[b][:, 0:N], [(Wv[:, k, jj * 128:(jj + 1) * 128], hT[:, k, 0:N]) for k in range(8)],
                              list(deps) + t_h, b)
                hT_rd.append(tm)
                return b, tm

            tokmaj = samp or p == 3
            tl = nt - 1

            if samp:
                for tb in range(2):
                    td = S.dma('sp', sttile[0:120, :], spool_d[l, tb], "ldst", deps=state_free())
                    b = get_bank()
                    tt = None
                    for j in range(4):
                        tt = S.op('pe', lambda E, j=j, b=b: E.transpose(out=banks[b][:, j * 120:(j + 1) * 120], in_=sttile[0:120, j * 128:(j + 1) * 128],
                                                                       identity=identf[0:120, 0:120]),
                                  [td, t_id, bank_free[b]] if j == 0 else [], sig=(j == 3))
                    set_state_free([tt])
                    te = None
                    for j in range(4):
                        te = S.op('dve', lambda E, j=j, b=b, tb=tb: E.tensor_copy(
                            out=psext[:, j, tb * 8:(tb + 1) * 8, 0:15],
                            in_=banks[b][:, j * 120:(j + 1) * 120].rearrange("p (b r) -> p b r", r=15)),
                            [tt] + carry_tok[('ps',)] if j == 0 else [])
                    bank_free[b] = te
                    sample_hist_p = te
                for tb in range(4):
                    td = S.dma('sp', sttile[0:120, :], sconv_d[l, tb], "ldst", deps=state_free())
                    b = get_bank()
                    tt = None
                    for j in range(4):
                        tt = S.op('pe', lambda E, j=j, b=b: E.transpose(out=banks[b][:, j * 120:(j + 1) * 120], in_=sttile[0:120, j * 128:(j + 1) * 128],
                                                                       identity=identf[0:120, 0:120]),
                                  [td, t_id, bank_free[b]] if j == 0 else [], sig=(j == 3))
                    set_state_free([tt])
                    te = None
                    for j in range(4):
                        te = S.op('dve', lambda E, j=j, b=b, tb=tb: E.tensor_copy(
                            out=usext[:, j, tb * 4:(tb + 1) * 4, 0:30],
                            in_=banks[b][:, j * 120:(j + 1) * 120].rearrange("p (b r) -> p b r", r=30)),
                            [tt] + carry_tok[('us',)] if j == 0 else [])
                    bank_free[b] = te
                    sample_hist_u = te

            dbg_stop('SB1', samp, l)
            mark(f"p{p}l{l}:pool")
            gate_w = []
            for jb in range(2):
                ig, Wg, trg = wget()
                tm = None
                for jj in range(2):
                    j = 2 * jb + jj
                    b, tm = proj(Wg, jj, [trg])
                    tg = S.op('act', lambda E, N=N, b=b, j=j: E.activation(out=gateB[:, j, 0:N], in_=banks[b][:, 0:N], func=AF.Silu),
                              [tm] + gateB_free[j])
                    bank_free[b] = tg
                    gate_w.append(tg)
                wrel(ig, tm)
            tokb_p = 6 if tokmaj else None
            tm_tp = None
            t_a_last = None
            pool_rd = []
            pool_pending = []
            pend = [None]
            pend2 = [None]

            def pool_mm():
                if pend[0] is None:
                    return
                j_, mx_, tmx_ = pend[0]
                pend[0] = None
                b2 = get_bank()
                tm2 = mm_group(banks[b2][:, 0:N], [(poolwb[l][:, j_ * 128:(j_ + 1) * 128], mx_[:, 0:N])], [tmx_, t_pw[l]], b2)
                mix_free4[j_] = tm2
                pend2[0] = (j_, b2, tm2)

            def pool_ep():
                if pend2[0] is None:
                    return None
                j_, b2, tm2 = pend2[0]
                pend2[0] = None
                ta = S.op('dve', lambda E, N=N, b2=b2, j=j_, l=l: E.scalar_tensor_tensor(
                    out=brb[:, j, 0:N], in0=banks[b2][:, 0:N], scalar=par[:, P_PSC + l * 4 + j:P_PSC + l * 4 + j + 1],
                    in1=gateB[:, j, 0:N], op0=ALU.mult, op1=ALU.mult), [tm2, gate_w[j_]] + br_free)
                bank_free[b2] = ta
                gateB_free[j_] = [ta]
                return ta

            for jb in range(2):
                ip, Wp, trp = wget()
                tlast = None
                for jj in range(2):
                    j = 2 * jb + jj
                    win = WINS[j]
                    b, tm = proj(Wp, jj, [trp])
                    tlast = tm
                    if samp:
                        xe = psext[:, j]
                        o_ap = xe[:, :, 15:23]
                        i0 = banks[b][:, 0:N].rearrange("p (b s) -> p b s", s=8)
                        wdeps = [sample_hist_p]
                        L = 23
                        sl_ = lambda ap, a, b_: ap[:, :, a:b_]
                        tv = [ptmp[q][:, 0:16 * 23].rearrange("p (b s) -> p b s", s=23) for q in range(2)]
                    else:
                        xe = pext[l][:, j]
                        o_ap = xe[:, 15:15 + N]
                        i0 = banks[b][:, 0:N]
                        wdeps = carry_tok[('p', l)]
                        L = 15 + N
                        sl_ = lambda ap, a, b_: ap[:, a:b_]
                        tv = [ptmp[q][:, 0:L] for q in range(2)]
                    tw = S.op('act', lambda E, o_ap=o_ap, i0=i0: E.activation(out=o_ap, in_=i0, func=AF.Copy), [tm] + list(wdeps) + pool_rd[-1:])
                    bank_free[b] = tw
                    s_ap = xe
                    v = 0
                    tprev = tw
                    d = 1
                    qi = 0
                    while d < win:
                        dst = tv[qi]
                        tprev = S.op('dve', lambda E, dst=dst, s_ap=s_ap, v=v, d=d, L=L, sl_=sl_: E.tensor_tensor(
                            out=sl_(dst, v + d, L), in0=sl_(s_ap, v + d, L), in1=sl_(s_ap, v, L - d), op=ALU.add), [tprev, t_a_last])
                        s_ap = dst
                        v += d
                        d *= 2
                        qi ^= 1
                    mx = mix[j]
                    if samp:
                        mo = mx[:, 0:N].rearrange("p (b s) -> p b s", s=8)
                    else:
                        mo = mx[:, 0:N]
                    tmx = S.op('dve', lambda E, mo=mo, s_ap=s_ap, xe=xe, L=L, sl_=sl_, win=win: E.scalar_tensor_tensor(
                        out=mo, in0=sl_(s_ap, 15, L), scalar=1.0 / win, in1=sl_(xe, 15, L), op0=ALU.mult, op1=ALU.subtract),
                        [tprev, mix_free4[j]])
                    if p == 0:
                        S.op('dve', lambda E, s_ap=s_ap, j=j: E.tensor_tensor(out=tmpP[:, 0:15], in0=s_ap[:, 15:30],
                                                                              in1=par[:, P_ICNT + j * 16:P_ICNT + j * 16 + 15], op=ALU.mult), [tmx], sig=True)
                        tmx = S.op('dve', lambda E, mx=mx, xe=xe: E.tensor_tensor(out=mx[:, 0:15], in0=tmpP[:, 0:15], in1=xe[:, 15:30], op=ALU.subtract),
                                   [('dve', S.cnt['dve'])])
                    pool_rd.append(tmx)
                    pool_pending.append((j, mx, tmx))
                if tokmaj:
                    tm_tp = mm_group(banks[tokb_p][:, jb * 256:(jb + 1) * 256],
                                     [(hT[:, k, tl * 128:(tl + 1) * 128], Wp[:, k, :]) for k in range(8)], [trp] + t_h, tokb_p)
                    hT_rd.append(tm_tp)
                    tlast = tm_tp
                wrel(ip, tlast)
            if tokmaj:
                tcp = S.op('dve', lambda E: E.tensor_copy(out=tok_c, in_=banks[tokb_p][:, :]), [tm_tp] + tokbuf_free('c'))
                bank_free[tokb_p] = tcp
                if samp:
                    st = S.dma('sp', pools_new_d[l], tok_c, "st_c", [tcp])
                else:
                    st = S.dma('sp', poolp_d[l], tok_c[113:128, :], "st_c", [tcp])
                store_toks.append(st)
                set_tokbuf_free('c', [st])
            mark(f"p{p}l{l}:conv")
            phase_barrier(tokens=bt_afterA)
            diag_ld = {}
            if p != first_pass:
                for j_ in range(2):
                    diag_ld[j_] = S.dma('sp', diag_flat[j_], diag_c[l, j_], f"ldd{j_}", deps=[diag_ready[(l, j_)]] + diag_free[j_] + diag_all_st[j_] + cur_bar[0])
            dbg_stop('SB0', samp, l)
            u_wr = []
            tokb_u = 7 if tokmaj else None
            tokb_g = 6 if tokmaj else None
            tm_tu = tm_tg = None
            for jb in range(2):
                ig, Wg, trg = wget()
                iv, Wv, trv = wget()
                tlast = None
                for jj in range(2):
                    j = 2 * jb + jj
                    b, tm = proj(Wg, jj, [trg])
                    sg = sgj[j % 2]
                    ts = S.op('act', lambda E, N=N, b=b, sg=sg: E.activation(out=sg[:, 0:N], in_=banks[b][:, 0:N], func=AF.Sigmoid),
                              [tm, sg_free[j % 2]])
                    bank_free[b] = ts
                    b2, tm2 = proj(Wv, jj, [trv])
                    if samp:
                        o_ap = usext[:, j, :, 30:38]
                        i0 = banks[b2][:, 0:N].rearrange("p (b s) -> p b s", s=8)
                        i1 = sg[:, 0:N].rearrange("p (b s) -> p b s", s=8)
                        wdeps = [sample_hist_u]
                    else:
                        o_ap = uext[l][:, j, 30:30 + N]
                        i0 = banks[b2][:, 0:N]
                        i1 = sg[:, 0:N]
                        wdeps = carry_tok[('u', l)]
                    tu = S.op('dve', lambda E, o_ap=o_ap, i0=i0, i1=i1: E.tensor_tensor(out=o_ap, in0=i0, in1=i1, op=ALU.mult),
                              [tm2, ts] + list(wdeps))
                    bank_free[b2] = tu
                    sg_free[j % 2] = tu
                    u_wr.append(tu)
                    tlast = tm2
                if tokmaj:
                    tm_tg = mm_group(banks[tokb_g][:, jb * 256:(jb + 1) * 256],
                                     [(hT[:, k, tl * 128:(tl + 1) * 128], Wg[:, k, :]) for k in range(8)], [trg] + t_h, tokb_g)
                    tm_tu = mm_group(banks[tokb_u][:, jb * 256:(jb + 1) * 256],
                                     [(hT[:, k, tl * 128:(tl + 1) * 128], Wv[:, k, :]) for k in range(8)], [trv] + t_h, tokb_u)
                    hT_rd.append(tm_tu)
                    tlast = tm_tu
                wrel(ig, tlast)
                wrel(iv, tlast)
            if tokmaj:
                ts = S.op('act', lambda E: E.activation(out=tok_b, in_=banks[tokb_g][:, :], func=AF.Sigmoid), [tm_tg, tm_tu] + tokbuf_free('b'))
                bank_free[tokb_g] = ts
                tu = S.op('dve', lambda E: E.tensor_tensor(out=tok_b, in0=banks[tokb_u][:, :], in1=tok_b, op=ALU.mult), [ts, tm_tu])
                bank_free[tokb_u] = tu
                if samp:
                    st = S.dma('sp', convs_new_d[l], tok_b, "st_b", [tu])
                else:
                    st = S.dma('sp', convp_d[l], tok_b[98:128, :], "st_b", [tu])
                store_toks.append(st)
                set_tokbuf_free('b', [st])
            dbg_stop('SC1', samp, l)
            conv_mm_last = None
            cv_wr = []
            for j in range(4):
                dg = diag[j % 2]
                st_d = None
                if p == first_pass:
                    td = None
                    for k in range(31):
                        td = S.op('dve', lambda E, dg=dg, k=k, j=j, l=l: E.tensor_scalar(
                            out=dg[:, k, :], in0=identb, scalar1=par[:, P_CW + (l * 4 + j) * 31 + k:P_CW + (l * 4 + j) * 31 + k + 1],
                            scalar2=None, op0=ALU.mult), ([t_ib, t_par[0]] + diag_free[j % 2] + diag_all_st[j % 2]) if k == 0 else [], sig=(k == 30))
                    st_d = S.dma('sp', diag_c[l, j], diag_flat[j % 2], f"std{l}{j}", [td])
                    diag_ready[(l, j)] = st_d
                    diag_all_st[j % 2].append(st_d)
                else:
                    td = diag_ld[j]
                b = get_bank()
                if samp:
                    pairs = [(dg[:, k, :], usext[:, j, :, k:k + 8]) for k in range(31)]
                    o_ap = banks[b][:, 0:N].rearrange("p (b s) -> p b s", s=8)
                else:
                    pairs = [(dg[:, k, :], uext[l][:, j, k:k + N]) for k in range(31)]
                    o_ap = banks[b][:, 0:N]
                tm = mm_group(o_ap, pairs, [td] + u_wr, b)
                diag_free[j % 2] = [tm, st_d]
                conv_mm_last = tm
                if j + 2 < 4 and p != first_pass:
                    diag_ld[j + 2] = S.dma('sp', diag_flat[j % 2], diag_c[l, j + 2], f"ldd{j % 2}", deps=[diag_ready[(l, j + 2)], tm] + diag_all_st[j % 2] + cur_bar[0])
                t1 = S.op('act', lambda E, N=N, b=b, j=j, l=l: E.activation(out=cvg[:, j, 0:N], in_=banks[b][:, 0:N], func=AF.Identity,
                                                                      bias=par[:, P_CB + l * 4 + j:P_CB + l * 4 + j + 1]), [tm] + cvg_free)
                t2 = S.op('act', lambda E, N=N, b=b, j=j, l=l: E.activation(out=sqg[:, j, 0:N], in_=banks[b][:, 0:N], func=AF.Square,
                                                                      bias=par[:, P_CB + l * 4 + j:P_CB + l * 4 + j + 1]), [tm])
                bank_free[b] = t2
                cv_wr.append(t2)
            if not samp:
                tc = S.op('dve', lambda E, N=N, l=l: E.tensor_copy(out=uext[l][:, :, 0:30], in_=uext[l][:, :, N:N + 30]), [conv_mm_last])
                carry_tok[('u', l)] = [tc]
            else:
                carry_tok[('us',)] = [conv_mm_last]
            for (j_, mx_, tmx_) in pool_pending:
                pend[0] = (j_, mx_, tmx_)
                pool_mm()
                t_a_last = pool_ep()
            if not samp:
                tc = S.op('dve', lambda E, N=N, l=l: E.tensor_copy(out=pext[l][:, :, 0:15], in_=pext[l][:, :, N:N + 15]), [t_a_last])
                carry_tok[('p', l)] = [tc]
            else:
                carry_tok[('ps',)] = [t_a_last]

            dbg_stop('SC2', samp, l)
            dbg_stop('SC0', samp, l)
            gate_w = []
            for jb in range(2):
                ig, Wg, trg = wget()
                tm = None
                for jj in range(2):
                    j = 2 * jb + jj
                    b, tm = proj(Wg, jj, [trg])
                    tg = S.op('act', lambda E, N=N, b=b, j=j: E.activation(out=gateA[:, j, 0:N], in_=banks[b][:, 0:N], func=AF.Silu),
                              [tm] + gateA_free[j])
                    bank_free[b] = tg
                    gate_w.append(tg)
                wrel(ig, tm)
            s1 = tmpA[:, 0:N]
            s2 = tmpB[:, 0:N]
            tr1 = S.op('dve', lambda E, N=N, s1=s1: E.tensor_reduce(out=s1, in_=cvg[:, :, 0:N].rearrange("p j n -> p n j"), axis=mybir.AxisListType.X, op=ALU.add),
                       cv_wr + stat_free)
            tr2 = S.op('dve', lambda E, N=N, s2=s2: E.tensor_reduce(out=s2, in_=sqg[:, :, 0:N].rearrange("p j n -> p n j"), axis=mybir.AxisListType.X, op=ALU.add),
                       cv_wr + stat_free)
            bm = get_bank()
            tmm = mm_group(banks[bm][:, 0:N], [(onesf, s1)], [tr1, t_of], bm)
            bq = get_bank()
            tmq = mm_group(banks[bq][:, 0:N], [(onesf, s2)], [tr2], bq)
            t_mean = S.op('act', lambda E, bm=bm, N=N: E.activation(out=mean_sb[:, 0:N], in_=banks[bm][:, 0:N], func=AF.Copy), [tmm] + stat_free)
            bank_free[bm] = t_mean
            t_m2 = S.op('dve', lambda E, N=N: E.tensor_tensor(out=var_sb[:, 0:N], in0=mean_sb[:, 0:N], in1=mean_sb[:, 0:N], op=ALU.mult), [t_mean])
            t_var = S.op('dve', lambda E, bq=bq, N=N: E.tensor_tensor(out=var_sb[:, 0:N], in0=banks[bq][:, 0:N], in1=var_sb[:, 0:N], op=ALU.subtract), [t_m2, tmq])
            bank_free[bq] = t_var
            t_l = S.op('act', lambda E, N=N: E.activation(out=rstd_sb[:, 0:N], in_=var_sb[:, 0:N], func=AF.Ln, bias=EPS), [t_var])
            t_r = S.op('act', lambda E, N=N: E.activation(out=rstd_sb[:, 0:N], in_=rstd_sb[:, 0:N], func=AF.Exp, scale=-0.5), [t_l])
            mean_b = mean_sb[:, 0:N].unsqueeze(1).broadcast_to([128, 4, N])
            rstd_b = rstd_sb[:, 0:N].unsqueeze(1).broadcast_to([128, 4, N])
            ta = S.op('dve', lambda E, N=N, mean_b=mean_b: E.tensor_tensor(out=cvg[:, :, 0:N], in0=cvg[:, :, 0:N], in1=mean_b, op=ALU.subtract),
                      [t_mean, tmm, tr1])
            tb = S.op('dve', lambda E, N=N, rstd_b=rstd_b: E.tensor_tensor(out=cvg[:, :, 0:N], in0=cvg[:, :, 0:N], in1=rstd_b, op=ALU.mult), [ta, t_r])
            tcs_all = []
            for j in range(4):
                tcs_all.append(S.op('act', lambda E, N=N, j=j, l=l: E.activation(out=sqg[:, j, 0:N], in_=cvg[:, j, 0:N], func=AF.Silu,
                                                                                 scale=par[:, P_LNG + l * 4 + j:P_LNG + l * 4 + j + 1],
                                                                                 bias=par[:, P_LNB + l * 4 + j:P_LNB + l * 4 + j + 1]), [tb, tmq, tr2]))
            tb_last = S.op('dve', lambda E, N=N: E.tensor_tensor(out=brb[:, 4:8, 0:N], in0=sqg[:, :, 0:N], in1=gateA[:, :, 0:N], op=ALU.mult),
                           tcs_all + gate_w + br_free)
            for j in range(4):
                gateA_free[j] = [tb_last]
            cvg_free[:] = [tb_last]
            stat_free[:] = [tb_last]
            t_bconv = tb_last

            dbg_stop('SB2', samp, l)
            mark(f"p{p}l{l}:attn")
            phase_barrier()
            q_w = []
            for jb in range(2):
                iq, Wq, trq = wget()
                tm = None
                for jj in range(2):
                    j = 2 * jb + jj
                    b, tm = proj(Wq, jj, [trq])
                    tq = S.op('act', lambda E, N=N, b=b, j=j: E.activation(out=QT[:, j, 0:N], in_=banks[b][:, 0:N], func=AF.Copy,
                                                                      scale=1.0 / math.sqrt(128.0)), [tm] + qt_free)
                    bank_free[b] = tq
                    q_w.append(tq)
                wrel(iq, tm)
            gate_w = [None] * 4
            xg_blocks = [wget(), wget()]

            def xg_chunk(j):
                ig, Wg, trg = xg_blocks[j // 2]
                b, tm = proj(Wg, j % 2, [trg])
                tg = S.op('act', lambda E, N=N, b=b, j=j: E.activation(out=gateB[:, j, 0:N], in_=banks[b][:, 0:N], func=AF.Silu),
                          [tm] + gateB_free[j])
                bank_free[b] = tg
                gate_w[j] = tg
                if j % 2 == 1:
                    wrel(ig, tm)

            if samp:
                for j in range(4):
                    xg_chunk(j)
            att_last = None
            if not samp:
                def s_stage(h):
                    Eh = Ebuf[h % 2]
                    tes = []
                    for mt in range(2):
                        b = get_bank()
                        tm = mm_group(banks[b][:, 0:N], [(KTp[l][:, h, mt * 128:(mt + 1) * 128], QT[:, h, 0:N])], [q_w[h]], b)
                        te = S.op('act', lambda E, N=N, b=b, Eh=Eh, mt=mt: E.activation(out=Eh[:, mt, 0:N], in_=banks[b][:, 0:N], func=AF.Exp),
                                  [tm, e_free[h % 2]])
                        bank_free[b] = te
                        tes.append(te)
                    return tes

                tes_next = s_stage(0)
                for h in range(4):
                    xg_chunk(h)
                    Eh = Ebuf[h % 2]
                    tes = tes_next
                    if h + 1 < 4:
                        tes_next = s_stage(h + 1)
                    bo = get_bank()
                    tmo = mm_group(banks[bo][:, 0:N], [(Vp[l][:, mt, h * 128:(h + 1) * 128], Eh[:, mt, 0:N]) for mt in range(2)], tes, bo)
                    bz = get_bank()
                    tmz = mm_group(banks[bz][:, 0:N], [(onesb, Eh[:, mt, 0:N]) for mt in range(2)], tes + [t_ob], bz)
                    e_free[h % 2] = tmz
                    t1 = S.op('dve', lambda E, N=N, bz=bz: E.reciprocal(out=rz[:, 0:N], in_=banks[bz][:, 0:N]), [tmz, att_last])
                    bank_free[bz] = t1
                    t2 = S.op('dve', lambda E, N=N, bo=bo: E.tensor_tensor(out=otmp[:, 0:N], in0=banks[bo][:, 0:N], in1=rz[:, 0:N], op=ALU.mult), [t1, tmo])
                    bank_free[bo] = t2
                    t3 = S.op('dve', lambda E, N=N, h=h: E.tensor_tensor(out=brb[:, 8 + h, 0:N], in0=otmp[:, 0:N], in1=gateB[:, h, 0:N], op=ALU.mult),
                              [t2, gate_w[h]] + br_free)
                    gateB_free[h] = [t3]
                    att_last = t3
            else:
                BO, BZ = 6, 7
                kb_free = [[], []]
                vb_free = [[], []]
                kt_free = [[], []]
                eb_free = [[], []]

                kv_tok = {}
                kvbuf_free = [[], [], [], []]

                def sa_load(bp):
                    q = bp % 4
                    kv_tok[bp] = S.dma('pool', KV2[q], ckv_d[l, bp], f"ldkv{q}", deps=kvbuf_free[q] + cur_bar[0])
                    kvbuf_free[q] = []

                def sa_stage1(bi):
                    bp, b2 = bi // 2, bi % 2
                    q2 = bi % 2
                    KTb = KTb2[q2]
                    Kb = v3(KV2[bp % 4][:, b2 * 2048:b2 * 2048 + 1024], 2)
                    Vb = v3(KV2[bp % 4][:, b2 * 2048 + 1024:b2 * 2048 + 2048], 2)
                    tkc = kv_tok[bp]
                    b = get_bank()
                    pb = banks[b].bitcast(BF16)
                    tt = None
                    for mt in range(2):
                        for h in range(4):
                            c = mt * 4 + h
                            tt = S.op('pe', lambda E, mt=mt, h=h, c=c, pb=pb, Kb=Kb: E.transpose(out=pb[:, c * 128:(c + 1) * 128],
                                                                                                in_=Kb[:, mt, h * 128:(h + 1) * 128], identity=identb),
                                      [tkc, t_ib, bank_free[b]] if c == 0 else [], sig=(c == 7))
                    kvbuf_free[bp % 4].append(tt)
                    tkt = S.op('dve', lambda E, pb=pb, KTb=KTb: E.tensor_copy(out=KTb, in_=v3(pb, 8)), [tt] + kt_free[q2])
                    bank_free[b] = tkt
                    return tkt, tkc, Vb

                def sa_stage2(bi, tkt, tvc, Vb):
                    q2 = bi % 2
                    KTb, Eb = KTb2[q2], Eb2[q2]
                    bs = get_bank()
                    tm = None
                    for h in range(4):
                        for mt in range(2):
                            c = h * 2 + mt
                            tm = S.op('pe', lambda E, h=h, mt=mt, c=c, bs=bs, bi=bi, KTb=KTb: E.matmul(
                                banks[bs][:, c * 8:(c + 1) * 8], lhsT=KTb[:, mt * 4 + h, :], rhs=QT[:, h, bi * 8:(bi + 1) * 8], start=True, stop=True),
                                [tkt, bank_free[bs]] + q_w if c == 0 else [], sig=(c == 7))
                    kt_free[q2] = [tm]
                    te = S.op('act', lambda E, bs=bs, Eb=Eb: E.activation(out=Eb, in_=banks[bs][:, 0:64], func=AF.Exp), [tm] + eb_free[q2])
                    bank_free[bs] = te
                    tmo = tmz = None
                    for h in range(4):
                        for mt in range(2):
                            c = h * 2 + mt
                            col = bi * 32 + h * 8
                            tmo = S.op('pe', lambda E, h=h, mt=mt, c=c, col=col, Vb=Vb, Eb=Eb: E.matmul(
                                banks[BO][:, col:col + 8], lhsT=Vb[:, mt, h * 128:(h + 1) * 128], rhs=Eb[:, c * 8:(c + 1) * 8],
                                start=(mt == 0), stop=(mt == 1)), [te, tvc, bank_free[BO]] if c == 0 else [], sig=(c == 7))
                    for h in range(4):
                        for mt in range(2):
                            c = h * 2 + mt
                            col = bi * 32 + h * 8
                            tmz = S.op('pe', lambda E, h=h, mt=mt, c=c, col=col, Eb=Eb: E.matmul(
                                banks[BZ][:, col:col + 8], lhsT=onesb, rhs=Eb[:, c * 8:(c + 1) * 8],
                                start=(mt == 0), stop=(mt == 1)), [te, t_ob, bank_free[BZ]] if c == 0 else [], sig=(c == 7))
                    kvbuf_free[(bi // 2) % 4].append(tmo)
                    eb_free[q2] = [tmz]
                    return tmo, tmz

                for bp_ in range(4):
                    sa_load(bp_)
                nxt = sa_stage1(0)
                tmo = tmz = None
                for bi in range(16):
                    cur = nxt
                    if bi + 1 < 16:
                        nxt = sa_stage1(bi + 1)
                    tmo, tmz = sa_stage2(bi, cur[0], cur[1], cur[2])
                    if bi % 2 == 1 and bi // 2 + 4 < 8:
                        sa_load(bi // 2 + 4)
                t1 = S.op('dve', lambda E: E.reciprocal(out=rz[:, 0:512], in_=banks[BZ][:, 0:512]), [tmz])
                bank_free[BZ] = t1
                t2 = S.op('dve', lambda E: E.tensor_tensor(out=otmp[:, 0:512], in0=banks[BO][:, 0:512], in1=rz[:, 0:512], op=ALU.mult), [t1, tmo])
                bank_free[BO] = t2
                t3 = S.op('dve', lambda E: E.tensor_tensor(out=brb[:, 8:12, 0:128].rearrange("p h (b s) -> p b h s", s=8),
                                                           in0=otmp[:, 0:512].rearrange("p (b h s) -> p b h s", h=4, s=8),
                                                           in1=gateB[:, :, 0:128].rearrange("p h (b s) -> p b h s", s=8), op=ALU.mult),
                          [t2] + gate_w + br_free)
                att_last = t3
                for h in range(4):
                    gateB_free[h] = [att_last]
            qt_free[:] = [att_last]
            dbg_stop('SB3', samp, l)
            mark(f"p{p}l{l}:merge")
            btm = phase_barrier()
            t_gp = S.dma('sp', gpost, gpost_d[:, l * 1024:(l + 1) * 1024], "ldg", deps=btm)
            br_rd = []
            mg_wr = []
            for dq in range(4):
                Wm = [wget() for _ in range(3)]
                Wb = [wget() for _ in range(3)]
                tlast = None
                for jj in range(2):
                    j = 2 * dq + jj
                    tsg = []
                    for n in range(3):
                        b, tm = proj(Wm[n][1], jj, [Wm[n][2]])
                        ts = S.op('act', lambda E, N=N, b=b, n=n: E.activation(out=sig3[n][:, 0:N], in_=banks[b][:, 0:N], func=AF.Sigmoid),
                                  [tm, sig_free[n]])
                        bank_free[b] = ts
                        tsg.append(ts)
                    bb = []
                    for n in range(3):
                        b = get_bank()
                        tm = mm_group(banks[b][:, 0:N], [(Wb[n][1][:, wk, jj * 128:(jj + 1) * 128], brb[:, n * 4 + wk, 0:N]) for wk in range(4)],
                                      [Wb[n][2], t_bconv, t_a_last, att_last], b)
                        bb.append((b, tm))
                        br_rd.append(tm)
                        tlast = tm
                    t1 = S.op('dve', lambda E, N=N, b=bb[0][0]: E.tensor_tensor(out=macc[:, 0:N], in0=banks[b][:, 0:N], in1=sig3[0][:, 0:N], op=ALU.mult),
                              [bb[0][1], tsg[0], mg_wr[-1] if mg_wr else None])
                    bank_free[bb[0][0]] = t1
                    sig_free[0] = t1
                    t2 = S.op('dve', lambda E, N=N, b=bb[1][0]: E.tensor_tensor(out=ttmp[:, 0:N], in0=banks[b][:, 0:N], in1=sig3[1][:, 0:N], op=ALU.mult),
                              [bb[1][1], tsg[1]])
                    bank_free[bb[1][0]] = t2
                    sig_free[1] = t2
                    t3 = S.op('dve', lambda E, N=N: E.tensor_tensor(out=macc[:, 0:N], in0=macc[:, 0:N], in1=ttmp[:, 0:N], op=ALU.add), [t1, t2])
                    t4 = S.op('dve', lambda E, N=N, b=bb[2][0]: E.tensor_tensor(out=ttmp[:, 0:N], in0=banks[b][:, 0:N], in1=sig3[2][:, 0:N], op=ALU.mult),
                              [bb[2][1], tsg[2], t3])
                    bank_free[bb[2][0]] = t4
                    sig_free[2] = t4
                    t5 = S.op('dve', lambda E, N=N, j=j: E.tensor_tensor(out=merged[:, j, 0:N], in0=macc[:, 0:N], in1=ttmp[:, 0:N], op=ALU.add),
                              [t3, t4] + mg_free)
                    mg_wr.append(t5)
                for n in range(3):
                    wrel(Wm[n][0], tlast)
                    wrel(Wb[n][0], tlast)
            br_free[:] = [br_rd[-1]]
            set_hT_free([hT_rd[-1], br_rd[-1]])
            mark(f"p{p}l{l}:D")
            bt = phase_barrier()
            Wo = [wget() for _ in range(4)]
            tlast = None
            d_free = [None, None]
            d_state = {}

            def d_stage1(t):
                bh = []
                tl_ = None
                yt = ytmp[t]
                for half in range(2):
                    b = 2 * t + half
                    tm = None
                    for q in range(2):
                        blk = half * 2 + q
                        tm = mm_group(banks[b][:, q * 256:(q + 1) * 256],
                                      [(merged[:, k, t * 128:(t + 1) * 128], Wo[blk][1][:, k, :]) for k in range(8)],
                                      [Wo[blk][2]] + mg_wr, b)
                    bh.append((b, tm))
                    tl_ = tm
                tsq = []
                tys = []
                for half in range(2):
                    tq = S.op('act', lambda E, b=bh[half][0], half=half, t=t: E.activation(out=junkD, in_=banks[b][:, :], func=AF.Square,
                                                                                         accum_out=ssD[:, 2 * t + half:2 * t + half + 1]),
                              [bh[half][1], junk_last[1]])
                    junk_last[1] = tq
                    ty = S.op('dve', lambda E, b=bh[half][0], half=half, yt=yt: E.tensor_tensor(
                        out=yt[:, half * 512:(half + 1) * 512], in0=banks[b][:, :], in1=gpost[:, half * 512:half * 512 + 512], op=ALU.mult),
                        [bh[half][1], tq, t_gp])
                    bank_free[bh[half][0]] = ty
                    tsq.append(tq)
                    tys.append(ty)
                d_state[t] = (tsq, tys)
                return tl_

            def d_stage2(t):
                tsq, tys = d_state.pop(t)
                yt = ytmp[t]
                tsum = S.op('dve', lambda E, t=t: E.tensor_tensor(out=ssD2[:, t:t + 1], in0=ssD[:, 2 * t:2 * t + 1], in1=ssD[:, 2 * t + 1:2 * t + 2], op=ALU.add), tsq)
                tl1 = S.op('act', lambda E, t=t: E.activation(out=rstdD[:, t:t + 1], in_=ssD2[:, t:t + 1], func=AF.Ln, scale=1.0 / 1024, bias=EPS), [tsum])
                tl2 = S.op('act', lambda E, t=t: E.activation(out=rstdD[:, t:t + 1], in_=rstdD[:, t:t + 1], func=AF.Exp, scale=-0.5), [tl1])
                tx = S.op('dve', lambda E, t=t, yt=yt: E.scalar_tensor_tensor(out=Xg[:, t, :], in0=yt, scalar=rstdD[:, t:t + 1], in1=Xg[:, t, :],
                                                                             op0=ALU.mult, op1=ALU.add), [tl2] + tys)
                d_free[t % 2] = tx
                x_ready[t] = tx
                if l == 1:
                    st = S.dma('sp', y_d[p * 4 + t], Xg[:, t, :], f"st_y{t}", [tx])
                    store_toks.append(st)
                    xg_free[t] = [st]
                    if nxt_pass is not None:
                        for tp_ in ([t - 1] if t >= 1 else []) + ([t] if t == nt - 1 else []):
                            if tp_ < nxt_pass[1] // 128:
                                x_pref[tp_] = S.dma('sp', Xg[:, tp_, :], x_d[nxt_pass[0] * 4 + tp_], f"ldx{tp_}", deps=xg_free[tp_])

            tlast = None
            for t in range(nt):
                tlast = d_stage1(t)
            fronts = {}
            th_next = []
            d_stage2(0)
            for t in range(nt):
                if t + 1 < nt:
                    d_stage2(t + 1)
                if l == 0:
                    fronts[t] = a_front(t, Xg[:, t, :], x_ready[t], gpre[:, (l + 1) * 1024:(l + 2) * 1024], t_gpre, bank=2 * t)
                    if t >= 1:
                        th_next.append(a_back(t - 1, fronts.pop(t - 1), hT, hT_free_list()))
            if l == 0:
                th_next.append(a_back(nt - 1, fronts.pop(nt - 1), hT, hT_free_list()))
                fused_th[0] = th_next
            for q in range(4):
                wrel(Wo[q][0], tlast)
            mg_free[:] = [tlast]
            if (DEBUG == 'L0' and l == 0 and p == 0) or (DEBUG == 'L1' and l == 1 and p == 0) or (DEBUG == 'S0' and l == 0 and samp) or (DEBUG == 'S1' and l == 1 and samp):
                bt = barrier()
                ds = [S.dma('sp', dbg_b, brb.rearrange("p a b -> p (a b)"), "dbg1", bt),
                      S.dma('sp', dbg_m, merged.rearrange("p a b -> p (a b)"), "dbg2", bt),
                      S.dma('sp', dbg_x, Xg.rearrange("p a b -> p (a b)"), "dbg3", bt),
                      S.dma('sp', dbg_q, QT.rearrange("p a b -> p (a b)"), "dbg4", bt),
                      S.dma('sp', dbg_g, gateB.rearrange("p a b -> p (a b)"), "dbg5", bt),
                      S.dma('sp', dbg_h, hT.rearrange("p a b -> p (a b)"), "dbg6", bt)]
                S.wait('sp', ds)
                S.emit(nc)
                es.close()
                return nc

      except _Stop:
        return nc
    mark("end")
    _mx = {}
    for (k_, v_) in store_toks:
        _mx[k_] = max(_mx.get(k_, 0), v_)
    S.wait('sp', list(_mx.items()))
    S.emit(nc)
    es.close()
    return nc


sg_free = [None, None]
gateA_free = [[], [], [], []]
gateB_free = [[], [], [], []]
mix_free4 = [None, None, None, None]
diag_free = [[], []]
cvg_free = []
stat_free = []
br_free = []
mix_free = [None, None]
qt_free = []
e_free = [None, None]
kv_free = {'k': [], 'v': [], 'kb': [], 'vb': [], 'kt': [], 'eb': []}
sig_free = [None, None, None]
mg_free = []
dphase_last = [None]
_hT_free = [[]]
_state_free = [[]]
_tokbuf = {'b': [], 'c': []}


def hT_free_list():
    return _hT_free[0]


def set_hT_free(v):
    _hT_free[0] = list(v)


def state_free():
    return _state_free[0]


def set_state_free(v):
    _state_free[0] = list(v)


def tokbuf_free(k):
    return _tokbuf[k]


def set_tokbuf_free(k, v):
    _tokbuf[k] = list(v)


def _reset_state():
    global sg_free, diag_free, cvg_free, stat_free, br_free, mix_free, qt_free, e_free, kv_free, sig_free, mg_free
    sg_free[:] = [None, None]
    for i in range(4):
        gateA_free[i] = []
        gateB_free[i] = []
        mix_free4[i] = None
    diag_free[0] = []
    diag_free[1] = []
    cvg_free[:] = []
    stat_free[:] = []
    br_free[:] = []
    mix_free[:] = [None, None]
    qt_free[:] = []
    e_free[:] = [None, None]
    for k in kv_free:
        kv_free[k] = []
    sig_free[:] = [None, None, None]
    mg_free[:] = []
    dphase_last[0] = None
    _hT_free[0] = []
    _state_free[0] = []
    _tokbuf['b'] = []
    _tokbuf['c'] = []


_NC_CACHE = {}


def _prep_shared(inp):
    f = np.float32
    w_in = np.asarray(inp["w_in"], f)
    win_l = np.ascontiguousarray(w_in.reshape(2, 8, 128, 26, 256).transpose(0, 3, 2, 1, 4)).reshape(2, 26, 128, 2048)
    w_br = np.asarray(inp["w_branch"], f)
    wbr_l = np.ascontiguousarray(w_br.reshape(2, 3, 4, 128, 4, 256).transpose(0, 1, 4, 3, 2, 5)).reshape(2, 3, 4, 128, 1024)
    w_out = np.asarray(inp["w_out"], f)
    wout_l = np.ascontiguousarray(w_out.reshape(2, 8, 128, 4, 256).transpose(0, 3, 2, 1, 4)).reshape(2, 4, 128, 2048)
    w_kv = np.asarray(inp["w_mem_kv"], f)
    wkv_l = np.ascontiguousarray(w_kv.reshape(2, 8, 128, 4, 256).transpose(0, 3, 2, 1, 4)).reshape(2, 4, 128, 2048)
    pool_w = np.asarray(inp["pool_w"], f)
    poolw_l = np.ascontiguousarray(pool_w.transpose(0, 2, 1, 3)).reshape(2, 128, 512)
    par = np.zeros((128, NPAR), f)
    par[:, P_GPRE:P_GPRE + 16] = np.asarray(inp["norm_pre"], f).reshape(2, 8, 128).transpose(2, 0, 1).reshape(128, 16)
    par[:, P_GMEM:P_GMEM + 16] = np.asarray(inp["mem_norm"], f).reshape(2, 8, 128).transpose(2, 0, 1).reshape(128, 16)
    par[:, P_PSC:P_PSC + 8] = np.asarray(inp["pool_scale"], f).reshape(2, 4, 128).transpose(2, 0, 1).reshape(128, 8)
    par[:, P_CB:P_CB + 8] = np.asarray(inp["conv_b"], f).reshape(2, 4, 128).transpose(2, 0, 1).reshape(128, 8)
    par[:, P_LNG:P_LNG + 8] = np.asarray(inp["conv_ln_g"], f).reshape(2, 4, 128).transpose(2, 0, 1).reshape(128, 8)
    par[:, P_LNB:P_LNB + 8] = np.asarray(inp["conv_ln_b"], f).reshape(2, 4, 128).transpose(2, 0, 1).reshape(128, 8)
    par[:, P_CW:P_CW + 248] = np.asarray(inp["conv_w"], f).reshape(2, 31, 4, 128).transpose(3, 0, 2, 1).reshape(128, 248)
    icnt = np.zeros((4, 16), f)
    for j, w in enumerate(WINS):
        for t in range(16):
            icnt[j, t] = 1.0 / min(t + 1, w)
    par[:, P_ICNT:P_ICNT + 64] = icnt.reshape(1, 64)
    gpost = np.ascontiguousarray(np.broadcast_to(np.asarray(inp["norm_post"], f).reshape(1, 2048), (128, 2048)))
    gpre_b = np.ascontiguousarray(np.broadcast_to(np.asarray(inp["norm_pre"], f).reshape(1, 2048), (128, 2048)))
    gmem_b = np.ascontiguousarray(np.broadcast_to(np.asarray(inp["mem_norm"], f).reshape(1, 2048), (128, 2048)))
    return {"w_in": win_l, "w_br": wbr_l, "w_out": wout_l, "w_kv": wkv_l, "pool_w": poolw_l, "params": par,
            "gpost": gpost, "gpre": gpre_b, "gmem": gmem_b, "ident": np.eye(128, dtype=f)}


def kernel(**inp):
    f = np.float32
    _reset_state()
    nc = build_program()
    shared = _prep_shared(inp)
    xp = np.asarray(inp["x_prompt"], f)
    xs = np.asarray(inp["x_sample"], f)
    memp = np.asarray(inp["mem_prompt"], f)
    sp = np.asarray(inp["state_pool"], f)
    sc = np.asarray(inp["state_conv"], f)
    ck = np.asarray(inp["cache_mem_k"], f)
    cv = np.asarray(inp["cache_mem_v"], f)
    in_maps = []
    for c in range(NCORES):
        b0 = 16 * c
        x = np.concatenate([xp[c].reshape(16, 128, 1024), xs[b0:b0 + 16].reshape(1, 128, 1024)], axis=0)
        m = dict(shared)
        m["x"] = np.ascontiguousarray(x)
        m["mem"] = np.ascontiguousarray(memp[c].reshape(2, 128, 1024))
        m["spool"] = np.ascontiguousarray(sp[:, b0:b0 + 16].reshape(2, 2, 120, 512))
        m["sconv"] = np.ascontiguousarray(sc[:, b0:b0 + 16].reshape(2, 4, 120, 512))
        kk = ck[:, b0:b0 + 16].reshape(2, 8, 2, 2, 128, 512)
        vv = cv[:, b0:b0 + 16].reshape(2, 8, 2, 2, 128, 512)
        kvv = np.stack([kk, vv], axis=3)
        m["ckv"] = np.ascontiguousarray(kvv.transpose(0, 1, 5, 2, 3, 4, 6)).reshape(2, 8, 128, 4096)
        in_maps.append(m)
    res = run_bass_kernel_spmd(nc, in_maps, core_ids=list(range(NCORES)))
    R = res.results
    y_p = np.stack([R[c]["y"][:16].reshape(2048, 1024) for c in range(NCORES)], 0)
    y_s = np.concatenate([R[c]["y"][16].reshape(16, 8, 1024) for c in range(NCORES)], 0)
    pool_p = np.stack([R[c]["pool_p"] for c in range(NCORES)], 1)
    conv_p = np.stack([R[c]["conv_p"] for c in range(NCORES)], 1)
    k_p = np.stack([R[c]["k_p"].reshape(2, 256, 4, 128) for c in range(NCORES)], 1)
    v_p = np.stack([R[c]["v_p"].reshape(2, 256, 4, 128) for c in range(NCORES)], 1)
    pool_s = np.concatenate([np.concatenate([R[c]["pool_s_old"], R[c]["pool_s_new"].reshape(2, 16, 8, 512)], axis=2) for c in range(NCORES)], 1)
    conv_s = np.concatenate([np.concatenate([R[c]["conv_s_old"], R[c]["conv_s_new"].reshape(2, 16, 8, 512)], axis=2) for c in range(NCORES)], 1)
    return (y_p.astype(f), y_s.astype(f), pool_p.astype(f), conv_p.astype(f), k_p.astype(f), v_p.astype(f),
            pool_s.astype(f), conv_s.astype(f))
```
